# Optimizing a Trainium2 kernel written in Bass

```python
import jax
import jax.numpy as jnp
from jax import lax
import numpy as np

D_MODEL = 1024
BATCH = 8
SEQ = 4096
DEPTH = 2

GRID_W = 64
CTX_LEN = 256
EPS = 1e-6

LRU_WIDTH = 512
LRU_BLOCKS = 8
LRU_BLOCK = LRU_WIDTH // LRU_BLOCKS
CONV_W = 4
CONV_PAD_L = 2
LRU_C = 8.0

MLA_HEADS = 8
MLA_NOPE = 64
MLA_ROPE = 32
MLA_V = 64
Q_LORA = 384
KV_LORA = 256
ROPE_BASE = 10000.0
Q_BLOCK = 128

RET_HEADS = 4
RET_DK = 64
RET_DV = 128
RET_CHUNK = 128

D_FF = -(-8 * D_MODEL // (3 * 256)) * 256

KV_SIZES = (LRU_WIDTH, KV_LORA, MLA_ROPE, RET_HEADS * RET_DK, RET_HEADS * RET_DV)
Q_SIZES = (Q_LORA, RET_HEADS * RET_DK, LRU_WIDTH, RET_HEADS * RET_DV, 3 * D_MODEL)
N_KV = sum(KV_SIZES)
N_IN = N_KV + sum(Q_SIZES)

kernel_name = 'hybrid_rglru_mla_retention_dit'


def rmsnorm(x, g):
    xf = x.astype(jnp.float32)
    y = xf * lax.rsqrt(jnp.mean(xf * xf, axis=-1, keepdims=True) + EPS)
    return (y * g).astype(x.dtype)


def head_norm(o):
    of = o.astype(jnp.float32)
    mu = jnp.mean(of, axis=-1, keepdims=True)
    var = jnp.mean(jnp.square(of - mu), axis=-1, keepdims=True)
    return ((of - mu) * lax.rsqrt(var + EPS)).astype(o.dtype)


def modulate(h, shift, scale):
    return h * (1.0 + scale) + shift


def split_cols(z, sizes):
    parts, off = [], 0
    for s in sizes:
        parts.append(z[..., off:off + s])
        off += s
    return parts


def rotate(x, cos, sin):
    half = x.shape[-1] // 2
    x1, x2 = x[..., :half], x[..., half:]
    return jnp.concatenate([x1 * cos - x2 * sin, x1 * sin + x2 * cos], axis=-1)


def axial_rope_tables(n_tokens, dtype):
    rows = n_tokens // GRID_W
    row = jnp.repeat(jnp.arange(rows, dtype=jnp.float32), GRID_W)
    col = jnp.tile(jnp.arange(GRID_W, dtype=jnp.float32), rows)
    n_freq = MLA_ROPE // 4
    inv = jnp.power(ROPE_BASE, -jnp.arange(n_freq, dtype=jnp.float32) / n_freq)
    ang = jnp.concatenate([row[:, None] * inv, col[:, None] * inv], axis=-1)
    return jnp.cos(ang).astype(dtype), jnp.sin(ang).astype(dtype)


def retention_rope_tables(start, n, dtype):
    theta = 1.0 / jnp.power(10000.0, jnp.linspace(0.0, 1.0, RET_DK // 2, dtype=jnp.float32))
    pos = start + jnp.arange(n, dtype=jnp.float32)
    ang = pos[:, None] * theta
    return jnp.cos(ang)[:, None, :].astype(dtype), jnp.sin(ang)[:, None, :].astype(dtype)


def retention_log_decays():
    h = jnp.arange(RET_HEADS, dtype=jnp.float32)
    return (jnp.log1p(-jnp.exp2(-5.0 - h)), jnp.log1p(-jnp.exp2(-5.5 - h)))


def depthwise_conv(u, w, b):
    out = lax.conv_general_dilated(
        u, w[:, None, :], window_strides=(1,), padding=[(CONV_PAD_L, CONV_W - 1 - CONV_PAD_L)],
        dimension_numbers=('NWC', 'WIO', 'NWC'), feature_group_count=u.shape[-1])
    return out + b


def rglru_coefficients(u, wa, ba, wx, bx, lam):
    uf = u.astype(jnp.float32)
    ub = uf.reshape(*u.shape[:-1], LRU_BLOCKS, LRU_BLOCK)
    r = jax.nn.sigmoid(jnp.einsum('blnd,nde->blne', ub, wa).reshape(u.shape) + ba)
    i = jax.nn.sigmoid(jnp.einsum('blnd,nde->blne', ub, wx).reshape(u.shape) + bx)
    log_a = -LRU_C * r * jax.nn.softplus(-lam)
    a = jnp.exp(log_a)
    b = jnp.sqrt(-jnp.expm1(2.0 * log_a)) * (i * uf)
    return a, b


def _lin_combine(l, r):
    return (l[0] * r[0], r[0] * l[1] + r[1])


def linear_scan(a, b, h0, reverse):
    if reverse:
        a, b = a[:, ::-1], b[:, ::-1]
    b = b.at[:, 0].add(a[:, 0] * h0)
    _, h = lax.associative_scan(_lin_combine, (a, b), axis=1)
    return h[:, ::-1] if reverse else h


def attention(q, k, v):
    s = jnp.einsum('bqhd,bkhd->bhqk', q, k).astype(jnp.float32) * (q.shape[-1] ** -0.5)
    p = jax.nn.softmax(s, axis=-1).astype(v.dtype)
    return jnp.einsum('bhqk,bkhd->bqhd', p, v)


def blocked_attention(q, k, v):
    B, L, H, D = q.shape
    nb = L // Q_BLOCK
    qb = jnp.moveaxis(q.reshape(B, nb, Q_BLOCK, H, D), 1, 0)
    o = lax.map(lambda qi: attention(qi, k, v), qb)
    return jnp.moveaxis(o, 0, 1).reshape(B, L, H, v.shape[-1])


def retention_chunkwise(q, k, v, log_gamma, r0):
    B, L, H, DK = q.shape
    DV = v.shape[-1]
    n = L // RET_CHUNK
    dt = q.dtype
    pos = jnp.arange(RET_CHUNK, dtype=jnp.float32)
    diff = pos[:, None] - pos[None, :]
    inner = jnp.where(diff >= 0, jnp.exp(log_gamma[:, None, None] * jnp.maximum(diff, 0.0)), 0.0).astype(dt)
    q_decay = jnp.exp(log_gamma[:, None] * (pos + 1.0))[..., None].astype(dt)
    k_decay = jnp.exp(log_gamma[:, None] * (RET_CHUNK - 1.0 - pos))[..., None].astype(dt)
    chunk_decay = jnp.exp(log_gamma * RET_CHUNK)[:, None, None].astype(dt)

    def to_chunks(t):
        return t.reshape(B, n, RET_CHUNK, H, t.shape[-1]).transpose(1, 0, 3, 2, 4)

    def step(r, xs):
        qi, ki, vi = xs
        s = jnp.einsum('bhqd,bhkd->bhqk', qi, ki) * inner
        o = jnp.einsum('bhqk,bhkv->bhqv', s, vi) + jnp.einsum('bhqd,bhdv->bhqv', qi * q_decay, r)
        r = chunk_decay * r + jnp.einsum('bhkd,bhkv->bhdv', ki * k_decay, vi)
        return r, o

    r, o = lax.scan(step, r0.astype(dt), (to_chunks(q), to_chunks(k), to_chunks(v)))
    return o.transpose(1, 0, 3, 2, 4).reshape(B, L, H, DV), r


def retention_state(k, v, log_gamma):
    L = k.shape[1]
    w = jnp.exp((L - 1.0 - jnp.arange(L, dtype=jnp.float32))[:, None] * log_gamma[None, :]).astype(k.dtype)
    return jnp.einsum('blhd,blhv->bhdv', k * w[None, :, :, None], v)


def swiglu(h, w1, w3, w2):
    return (jax.nn.silu(h @ w1) * (h @ w3)) @ w2


def token_mixer(h, hc, w_in, conv_w, conv_b, lru_wa, lru_ba, lru_wx, lru_bx, lru_lam,
                g_q, w_uq, g_kv, w_ukv, w_oa, w_ob, w_oc, w_out,
                rope, ret_lat, ret_ctx, log_decays, need_ctx_out):
    B, L, _ = h.shape
    Lc = hc.shape[1]
    dt = h.dtype
    z = h @ w_in
    zc = hc @ (w_in if need_ctx_out else w_in[:, :N_KV])
    src = split_cols(z[..., :N_KV], KV_SIZES)
    qry = split_cols(z[..., N_KV:], Q_SIZES)
    csrc = split_cols(zc[..., :N_KV], KV_SIZES)
    cqry = split_cols(zc[..., N_KV:], Q_SIZES) if need_ctx_out else None
    ident = lambda t: t
    flip = lambda t: t[:, ::-1]

    u = depthwise_conv(src[0], conv_w, conv_b)
    uc = depthwise_conv(csrc[0], conv_w, conv_b)
    rec_lat, rec_ctx = [], []
    for d in range(2):
        rev = d == 1
        p = (lru_wa[d], lru_ba[d], lru_wx[d], lru_bx[d], lru_lam[d])
        ac, bc = rglru_coefficients(uc, *p)
        hcd = linear_scan(ac, bc, jnp.zeros_like(bc[:, 0]), rev)
        a, b = rglru_coefficients(u, *p)
        rec_lat.append(linear_scan(a, b, hcd[:, 0] if rev else hcd[:, -1], rev))
        rec_ctx.append(hcd)
    y_a = (jax.nn.gelu(qry[2]) * (rec_lat[0] + rec_lat[1])).astype(dt)

    cos, sin = rope
    rot_k = lambda t: rotate(t, cos, sin)
    rot_q = lambda t: rotate(t, cos[:, None, :], sin[:, None, :])

    def mla_keys(kvd, kr, rot):
        n = kvd.shape[1]
        kv = (rmsnorm(kvd, g_kv) @ w_ukv).reshape(B, n, MLA_HEADS, MLA_NOPE + MLA_V)
        kr = jnp.broadcast_to(rot(kr)[:, :, None, :], (B, n, MLA_HEADS, MLA_ROPE))
        return jnp.concatenate([kv[..., :MLA_NOPE], kr], axis=-1), kv[..., MLA_NOPE:]

    def mla_queries(qd, rot):
        n = qd.shape[1]
        q = (rmsnorm(qd, g_q) @ w_uq).reshape(B, n, MLA_HEADS, MLA_NOPE + MLA_ROPE)
        return jnp.concatenate([q[..., :MLA_NOPE], rot(q[..., MLA_NOPE:])], axis=-1)

    k_lat, v_lat = mla_keys(src[1], src[2], rot_k)
    k_ctx, v_ctx = mla_keys(csrc[1], csrc[2], ident)
    y_b = blocked_attention(mla_queries(qry[0], rot_q),
                            jnp.concatenate([k_ctx, k_lat], axis=1),
                            jnp.concatenate([v_ctx, v_lat], axis=1)).reshape(B, L, MLA_HEADS * MLA_V)

    rl_cos, rl_sin = ret_lat
    rc_cos, rc_sin = ret_ctx
    kscale = RET_DK ** -0.5
    heads = lambda t, dim: t.reshape(*t.shape[:2], RET_HEADS, dim)
    rq = rotate(heads(qry[1], RET_DK), rl_cos, rl_sin)
    rk = rotate(heads(src[3], RET_DK), rl_cos, rl_sin) * kscale
    rv = heads(src[4], RET_DV)
    crk = rotate(heads(csrc[3], RET_DK), rc_cos, rc_sin) * kscale
    crv = heads(csrc[4], RET_DV)
    crq = rotate(heads(cqry[1], RET_DK), rc_cos, rc_sin) if need_ctx_out else None
    ret_out, ret_out_c = [], []
    for d in range(2):
        f = flip if d == 1 else ident
        lg = log_decays[d]
        if need_ctx_out:
            r0 = jnp.zeros((B, RET_HEADS, RET_DK, RET_DV), crk.dtype)
            occ, r_ctx = retention_chunkwise(f(crq), f(crk), f(crv), lg, r0)
            ret_out_c.append(f(occ))
        else:
            r_ctx = retention_state(f(crk), f(crv), lg)
        ol, _ = retention_chunkwise(f(rq), f(rk), f(rv), lg, r_ctx)
        ret_out.append(f(ol))
    y_c = jax.nn.silu(qry[3]) * head_norm(ret_out[0] + ret_out[1]).reshape(B, L, RET_HEADS * RET_DV)

    def merge(ya, yb, yc, gm):
        ga, gb, gc = jnp.split(gm, 3, axis=-1)
        m = (jax.nn.sigmoid(ga) * (ya @ w_oa) + jax.nn.sigmoid(gb) * (yb @ w_ob)
             + jax.nn.sigmoid(gc) * (yc @ w_oc))
        return m @ w_out

    y = merge(y_a, y_b, y_c, qry[4])
    if not need_ctx_out:
        return y, None
    y_ac = (jax.nn.gelu(cqry[2]) * (rec_ctx[0] + rec_ctx[1])).astype(dt)
    y_bc = attention(mla_queries(cqry[0], ident), k_ctx, v_ctx).reshape(B, Lc, MLA_HEADS * MLA_V)
    y_cc = jax.nn.silu(cqry[3]) * head_norm(ret_out_c[0] + ret_out_c[1]).reshape(B, Lc, RET_HEADS * RET_DV)
    return y, merge(y_ac, y_bc, y_cc, cqry[4])


def setup_inputs(seed: int = 0) -> dict:
    key = jax.random.key(seed)
    ks = iter(jax.random.split(key, 32))
    f32 = jnp.float32
    D = D_MODEL

    def nrm(shape, scale):
        return jax.random.normal(next(ks), shape, f32) * scale

    u = jax.random.uniform(next(ks), (DEPTH, 2, LRU_WIDTH), f32, 0.9, 0.999)
    a = u ** (1.0 / LRU_C)
    lam = jnp.log(a) - jnp.log1p(-a)
    return {
        'x': nrm((BATCH, SEQ, D), 1.0),
        'c': nrm((BATCH, D), 1.0),
        'ctx': nrm((BATCH, CTX_LEN, D), 1.0),
        'c_ctx': nrm((D,), 1.0),
        'w_mod': nrm((DEPTH, D, 6 * D), 0.5 * D ** -0.5),
        'b_mod': nrm((DEPTH, 6 * D), 0.02),
        'g_mix': 1.0 + nrm((DEPTH, D), 0.02),
        'g_ffn': 1.0 + nrm((DEPTH, D), 0.02),
        'w_in': nrm((DEPTH, D, N_IN), D ** -0.5),
        'conv_w': nrm((DEPTH, CONV_W, LRU_WIDTH), CONV_W ** -0.5),
        'conv_b': nrm((DEPTH, LRU_WIDTH), 0.02),
        'lru_wa': nrm((DEPTH, 2, LRU_BLOCKS, LRU_BLOCK, LRU_BLOCK), LRU_BLOCK ** -0.5),
        'lru_ba': nrm((DEPTH, 2, LRU_WIDTH), 0.02),
        'lru_wx': nrm((DEPTH, 2, LRU_BLOCKS, LRU_BLOCK, LRU_BLOCK), LRU_BLOCK ** -0.5),
        'lru_bx': nrm((DEPTH, 2, LRU_WIDTH), 0.02),
        'lru_lam': lam,
        'g_q': 1.0 + nrm((DEPTH, Q_LORA), 0.02),
        'w_uq': nrm((DEPTH, Q_LORA, MLA_HEADS * (MLA_NOPE + MLA_ROPE)), Q_LORA ** -0.5),
        'g_kv': 1.0 + nrm((DEPTH, KV_LORA), 0.02),
        'w_ukv': nrm((DEPTH, KV_LORA, MLA_HEADS * (MLA_NOPE + MLA_V)), KV_LORA ** -0.5),
        'w_oa': nrm((DEPTH, LRU_WIDTH, D), LRU_WIDTH ** -0.5),
        'w_ob': nrm((DEPTH, MLA_HEADS * MLA_V, D), (MLA_HEADS * MLA_V) ** -0.5),
        'w_oc': nrm((DEPTH, RET_HEADS * RET_DV, D), (RET_HEADS * RET_DV) ** -0.5),
        'w_out': nrm((DEPTH, D, D), D ** -0.5),
        'w_ff1': nrm((DEPTH, D, D_FF), D ** -0.5),
        'w_ff3': nrm((DEPTH, D, D_FF), D ** -0.5),
        'w_ff2': nrm((DEPTH, D_FF, D), D_FF ** -0.5),
        'g_final': 1.0 + nrm((D,), 0.02),
    }


def reference(x, c, ctx, c_ctx, w_mod, b_mod, g_mix, g_ffn, w_in, conv_w, conv_b,
              lru_wa, lru_ba, lru_wx, lru_bx, lru_lam, g_q, w_uq, g_kv, w_ukv,
              w_oa, w_ob, w_oc, w_out, w_ff1, w_ff3, w_ff2, g_final):
    B, L, D = x.shape
    Lc = ctx.shape[1]
    dt = x.dtype
    rope = axial_rope_tables(L, dt)
    ret_lat = retention_rope_tables(Lc, L, dt)
    ret_ctx = retention_rope_tables(0, Lc, dt)
    log_decays = retention_log_decays()
    c_act = jax.nn.silu(c)
    cc_act = jax.nn.silu(c_ctx)
    xc = ctx
    for l in range(DEPTH):
        last = l == DEPTH - 1
        mod = c_act @ w_mod[l] + b_mod[l]
        sh_a, sc_a, ga_a, sh_f, sc_f, ga_f = [m[:, None, :] for m in jnp.split(mod, 6, axis=-1)]
        n_cm = 2 * D if last else 6 * D
        mc = jnp.split(cc_act @ w_mod[l][:, :n_cm] + b_mod[l][:n_cm], n_cm // D)
        h = modulate(rmsnorm(x, g_mix[l]), sh_a, sc_a)
        hc = modulate(rmsnorm(xc, g_mix[l]), mc[0], mc[1])
        y, yc = token_mixer(h, hc, w_in[l], conv_w[l], conv_b[l], lru_wa[l], lru_ba[l], lru_wx[l],
                            lru_bx[l], lru_lam[l], g_q[l], w_uq[l], g_kv[l], w_ukv[l],
                            w_oa[l], w_ob[l], w_oc[l], w_out[l],
                            rope, ret_lat, ret_ctx, log_decays, not last)
        x = x + ga_a * y
        x = x + ga_f * swiglu(modulate(rmsnorm(x, g_ffn[l]), sh_f, sc_f), w_ff1[l], w_ff3[l], w_ff2[l])
        if not last:
            xc = xc + mc[2] * yc
            xc = xc + mc[5] * swiglu(modulate(rmsnorm(xc, g_ffn[l]), mc[3], mc[4]),
                                     w_ff1[l], w_ff3[l], w_ff2[l])
    return rmsnorm(x, g_final)
```

```python
import numpy as np
from contextlib import ExitStack
import concourse.bass as bass
import concourse.mybir as mybir
from concourse.bass_utils import run_bass_kernel_spmd

F32 = mybir.dt.float32
BF16 = mybir.dt.bfloat16
AF = mybir.ActivationFunctionType
ALU = mybir.AluOpType

D = 1024
LC = 256
LL = 4096
T = LC + LL
DEPTH = 2
NIN = 6304
DFF = 2816
EPS = 1e-6
TILES = [(0, 256)] + [(256 + 512 * i, 512) for i in range(8)]
NCH = T // 128

VC = {}
_off = 0
def _reg(name, n):
    global _off
    VC[name] = _off
    _off += n
_reg("c", 16)
for _l in range(DEPTH):
    _reg(f"gmix{_l}", 8); _reg(f"gffn{_l}", 8); _reg(f"bmod{_l}", 48)
    _reg(f"convw{_l}", 16)
    _reg(f"convb{_l}", 4)
    _reg(f"ba{_l}", 8); _reg(f"bx{_l}", 8); _reg(f"lam{_l}", 8)
    _reg(f"gq{_l}", 3); _reg(f"gkv{_l}", 2)
_reg("gfin", 8)
NV = _off


class Buf:
    __slots__ = ("w", "rs", "excl")

    def __init__(self, excl=False):
        self.w = None
        self.rs = {}
        self.excl = excl


class Sched:
    def __init__(self, nc, es):
        self.nc = nc
        self.engs = {"pe": nc.tensor, "act": nc.scalar, "dve": nc.vector, "pool": nc.gpsimd, "sp": nc.sync}
        self.semobj = {}
        self.cnt = {}
        for e in ("pe", "act", "dve", "pool"):
            self.semobj[e] = es.enter_context(nc.semaphore("s_" + e))
            self.cnt[e] = 0
        self.R = 8
        self.rpos = {"sp": 0, "pool": 0}
        self.rcnt = {}
        for q in ("sp", "pool"):
            for j in range(self.R):
                self.semobj[(q, j)] = es.enter_context(nc.semaphore(f"d_{q}{j}"))
                self.rcnt[(q, j)] = 0
        self.known = {e: {} for e in self.engs}

    def _wait(self, eng, tok):
        if tok is None:
            return
        k, v = tok
        if self.known[eng].get(k, 0) >= v:
            return
        self.engs[eng].wait_ge(self.semobj[k], v)
        self.known[eng][k] = v

    def _deps(self, eng, reads, writes):
        for b in reads:
            if b.excl:
                self._wait(eng, b.w)
                for k, v in b.rs.items():
                    self._wait(eng, (k, v))
            else:
                self._wait(eng, b.w)
        for b in writes:
            self._wait(eng, b.w)
            for k, v in b.rs.items():
                self._wait(eng, (k, v))

    def _register(self, tok, reads, writes):
        k, v = tok
        for b in reads:
            if b.excl:
                b.w = tok
                b.rs = {}
            else:
                if b.rs.get(k, 0) < v:
                    b.rs[k] = v
        for b in writes:
            b.w = tok
            b.rs = {}

    def op(self, eng, fn, reads=(), writes=()):
        self._deps(eng, reads, writes)
        ins = fn(self.engs[eng])
        self.cnt[eng] += 1
        ins.then_inc(self.semobj[eng], 1)
        tok = (eng, self.cnt[eng])
        self._register(tok, reads, writes)
        return tok

    def mm(self, out_ap, lhsT, rhs, start, stop, reads=(), out_buf=None, inc=None):
        if inc is None:
            inc = stop
        self._deps("pe", reads, [out_buf] if start else [])
        ins = self.nc.tensor.matmul(out_ap, lhsT, rhs, start=start, stop=stop)
        tok = ("pe", self.cnt["pe"] + 1)
        if inc:
            self.cnt["pe"] += 1
            ins.then_inc(self.semobj["pe"], 1)
        self._register(tok, reads, [out_buf] if stop else [])
        return tok

    def group(self, out_buf, out_ap, pairs, reads=()):
        n = len(pairs)
        for i, (l, r) in enumerate(pairs):
            self.mm(out_ap, l, r, i == 0, i == n - 1, reads=reads, out_buf=out_buf)

    def transpose(self, out_buf, out_ap, in_ap, ident, reads=()):
        self._deps("pe", reads, [out_buf])
        ins = self.nc.tensor.transpose(out_ap, in_ap, ident)
        self.cnt["pe"] += 1
        ins.then_inc(self.semobj["pe"], 1)
        tok = ("pe", self.cnt["pe"])
        self._register(tok, reads, [out_buf])

    def dma(self, q, out, in_, reads=(), writes=()):
        j = self.rpos[q]
        self.rpos[q] = (j + 1) % self.R
        key = (q, j)
        prev = self.rcnt[key]
        if prev > 0:
            self._wait(q, (key, prev))
        self._deps(q, reads, writes)
        ins = self.engs[q].dma_start(out=out, in_=in_)
        ins.then_inc(self.semobj[key], 16)
        self.rcnt[key] = prev + 16
        tok = (key, prev + 16)
        self._register(tok, reads, writes)
        return tok

    def barrier(self):
        toks = [(e, self.cnt[e]) for e in ("pe", "act", "dve", "pool") if self.cnt[e] > 0]
        toks += [(k, v) for k, v in self.rcnt.items() if v > 0]
        for e in self.engs:
            for t in toks:
                self._wait(e, t)


class PS:
    def __init__(self, nc, es):
        self.t = [es.enter_context(nc.psum_tensor(f"ps{i}", [128, 512], F32)) for i in range(7)]
        self.b = [Buf(excl=True) for _ in range(7)]
        self.tb = es.enter_context(nc.psum_tensor("psb", [128, 1024], BF16))
        self.bb = Buf(excl=True)
        self.pos = 0

    def next(self):
        i = self.pos
        self.pos = (i + 1) % 7
        return self.t[i], self.b[i]


def build_program(debug=False, nlayers=DEPTH, stop_after=None):
    nc = bass.Bass("TRN2", target_bir_lowering=False)
    dt_in = lambda name, shape, dt=F32: nc.dram_tensor(name, shape, dt, kind="ExternalInput").ap()
    skind = "ExternalOutput" if debug else "Internal"
    dt_sc = lambda name, shape, dt: nc.dram_tensor(name, shape, dt, kind=skind).ap()

    xin = dt_in("xin", [D, T])
    vecs_d = dt_in("vecs", [128, NV])
    w_mod = dt_in("w_mod", [DEPTH, D, 6 * D])
    w_in = dt_in("w_in", [DEPTH, D, NIN])
    lru_wa = dt_in("lru_wa", [DEPTH, 2, 8, 64, 64])
    lru_wx = dt_in("lru_wx", [DEPTH, 2, 8, 64, 64])
    w_uq = dt_in("w_uq", [DEPTH, 384, 768])
    w_ukv = dt_in("w_ukv", [DEPTH, 256, 1024])
    w_oa = dt_in("w_oa", [DEPTH, 512, D])
    w_ob = dt_in("w_ob", [DEPTH, 512, D])
    w_oc = dt_in("w_oc", [DEPTH, 512, D])
    w_out = dt_in("w_out", [DEPTH, D, D])
    w_ff1 = dt_in("w_ff1", [DEPTH, D, DFF])
    w_ff3 = dt_in("w_ff3", [DEPTH, D, DFF])
    w_ff2 = dt_in("w_ff2", [DEPTH, DFF, D])
    ropeA = dt_in("ropeA", [2, 128, T])
    retA = dt_in("retA", [2, 128, T])
    retAk = dt_in("retAk", [2, 128, T])
    rdec = dt_in("rdec", [128, 4 * 8])
    qdec = dt_in("qdec", [4, 2, 64, 128])
    maskT = dt_in("maskT", [4, 128, 512])
    cmat = dt_in("cmat", [128, 256])
    outT = nc.dram_tensor("outT", [D, LL], F32, kind="ExternalOutput").ap()

    Hs = dt_sc("Hs", [D, T], BF16)
    Us = dt_sc("Us", [512, T], F32)
    Kns = dt_sc("Kns", [512, T], BF16)
    Krs = dt_sc("Krs", [32, T], BF16)
    Vs = dt_sc("Vs", [T, 512], BF16)
    RKf = dt_sc("RKf", [256, T], BF16)
    RKt = dt_sc("RKt", [T, 256], BF16)
    RVs = dt_sc("RVs", [T, 512], BF16)
    Qns = dt_sc("Qns", [512, T], BF16)
    Qrs = dt_sc("Qrs", [256, T], BF16)
    RQs = dt_sc("RQs", [256, T], BF16)
    RECs = dt_sc("RECs", [512, T], BF16)
    YBs = dt_sc("YBs", [512, T], BF16)
    RETs = dt_sc("RETs", [512, T], BF16)
    X1s = dt_sc("X1s", [D, T], F32)
    X2s = dt_sc("X2s", [D, T], F32)

    with ExitStack() as es:
        S = Sched(nc, es)
        P = PS(nc, es)
        _uid = [0]

        def sb(st, name, shape, dt):
            _uid[0] += 1
            return st.enter_context(nc.sbuf_tensor(f"sb{_uid[0]}_{name}", shape, dt))

        vecs = sb(es, "vecs", [128, NV], F32)
        vecsB = Buf()
        cm = sb(es, "cm", [128, 256], F32)
        cmB = Buf()
        onesb = sb(es, "onesb", [128, 128], BF16)
        identb = sb(es, "identb", [128, 128], BF16)
        constB = Buf()
        modv = sb(es, "modv", [128, 96], F32)
        modvB = Buf()
        amix = sb(es, "amix", [128, 16], F32)
        affn = sb(es, "affn", [128, 16], F32)
        cact = sb(es, "cact", [128, 16], F32)
        epsc = sb(es, "epsc", [128, 1], F32)
        S.dma("sp", vecs[:], vecs_d, writes=[vecsB])
        S.dma("sp", cm[:], cmat, writes=[cmB])
        S.op("dve", lambda e: e.tensor_copy(out=onesb[:], in_=cm[:, 0:128]), reads=[cmB], writes=[constB])
        S.op("dve", lambda e: e.tensor_copy(out=identb[:], in_=cm[:, 128:256]), reads=[cmB], writes=[constB])
        S.op("dve", lambda e: e.memset(epsc[:], EPS), writes=[constB])
        S.op("act", lambda e: e.activation(out=cact[:], in_=vecs[:, VC["c"]:VC["c"] + 16], func=AF.Silu),
             reads=[vecsB], writes=[constB])
        onesf = cm[:, 0:128]
        identf = cm[:, 128:256]
        S.barrier()

        def vcol(name, i, n=1):
            return vecs[:, VC[name] + i:VC[name] + i + n]

        def rms_rstd(st, src_sq_list, nparts_scale, N, tag):
            pt, pb = P.next()
            S.group(pb, pt[:, :N], [(onesb[:], a) for a in src_sq_list], reads=[constB] + tag["reads"])
            sq = tag["sq"]
            S.op("act", lambda e: e.activation(out=sq[:, :N], in_=pt[:, :N], func=AF.Sqrt,
                                               bias=epsc[:], scale=nparts_scale), reads=[pb], writes=[tag["sqB"]])
            S.op("dve", lambda e: e.reciprocal(out=sq[:, :N], in_=sq[:, :N]), reads=[], writes=[tag["sqB"]])
            return sq

        for l in range(nlayers):
            last = l == DEPTH - 1
            xsrc = xin if l == 0 else X2s

            with ExitStack() as st:
                wst = [sb(st, f"wm{i}", [128, 8, 768], F32) for i in range(2)]
                wstB = [Buf(), Buf()]
                pt, pb = P.next()
                for blk in range(8):
                    buf, bB = wst[blk % 2], wstB[blk % 2]
                    S.dma("sp", buf[:], w_mod[l][:, blk * 768:(blk + 1) * 768].rearrange("(k p) c -> p k c", p=128),
                          writes=[bB])
                    for fc in range(6):
                        f = blk * 6 + fc
                        S.group(pb, pt[:, 2 * f:2 * f + 2],
                                [(buf[:, k, fc * 128:(fc + 1) * 128], cact[:, 2 * k:2 * k + 2]) for k in range(8)],
                                reads=[bB, constB])
                bm = VC[f"bmod{l}"]
                for j in range(2):
                    S.op("dve", lambda e, j=j: e.tensor_tensor(out=modv[:, j:96:2], in0=pt[:, j:96:2],
                                                               in1=vecs[:, bm:bm + 48], op=ALU.add),
                         reads=[pb, vecsB], writes=[modvB])
                for j in range(2):
                    S.op("dve", lambda e, j=j: e.scalar_tensor_tensor(
                        out=amix[:, j:16:2], in0=modv[:, 16 + j:32:2], scalar=1.0, in1=vcol(f"gmix{l}", 0, 8),
                        op0=ALU.add, op1=ALU.mult), reads=[modvB], writes=[modvB])
                    S.op("dve", lambda e, j=j: e.scalar_tensor_tensor(
                        out=affn[:, j:16:2], in0=modv[:, 64 + j:80:2], scalar=1.0, in1=vcol(f"gffn{l}", 0, 8),
                        op0=ALU.add, op1=ALU.mult), reads=[modvB], writes=[modvB])
                S.barrier()
            if stop_after == (l, 0):
                break

            def norm_mod(st, xt, xB, N, jj, A, shift_base, ht, hB, wk):
                S.op("act", lambda e: e.activation(out=wk["xsq"][:, :, :N], in_=xt[:, :, :N], func=AF.Square),
                     reads=[xB], writes=[wk["xsqB"]])
                rs = rms_rstd(st, [wk["xsq"][:, k, :N] for k in range(8)], 1.0 / D, N,
                              {"reads": [wk["xsqB"]], "sq": wk["rstd"], "sqB": wk["rstdB"]})
                for k in range(8):
                    tmp, tB = wk["tmp"][k % 2], wk["tmpB"][k % 2]
                    S.op("dve", lambda e, k=k, tmp=tmp: e.scalar_tensor_tensor(
                        out=tmp[:, :N], in0=xt[:, k, :N], scalar=A[:, 2 * k + jj:2 * k + jj + 1], in1=rs[:, :N],
                        op0=ALU.mult, op1=ALU.mult), reads=[xB, wk["rstdB"], modvB], writes=[tB])
                    S.op("act", lambda e, k=k, tmp=tmp: e.activation(
                        out=ht[:, k, :N], in_=tmp[:, :N], func=AF.Identity,
                        bias=modv[:, shift_base + 2 * k + jj:shift_base + 2 * k + jj + 1], scale=1.0),
                         reads=[tB, modvB], writes=[hB])

            def mk_norm_wk(st, pfx):
                return {"xsq": sb(st, pfx + "xsq", [128, 8, 512], BF16), "xsqB": Buf(),
                        "rstd": sb(st, pfx + "rstd", [128, 512], F32), "rstdB": Buf(),
                        "tmp": [sb(st, pfx + f"tmp{i}", [128, 512], F32) for i in range(2)], "tmpB": [Buf(), Buf()]}

            with ExitStack() as st:
                wA = sb(st, "wA", [128, 8, 2208], BF16)
                wS = sb(st, "wS", [128, 8, 544], BF16)
                wkvN = sb(st, "wkvN", [128, 2, 512], BF16)
                wkvV = sb(st, "wkvV", [128, 2, 512], BF16)
                wqN = sb(st, "wqN", [128, 3, 512], BF16)
                wqR = sb(st, "wqR", [128, 3, 256], BF16)
                wqRs = sb(st, "wqRs", [128, 3, 256], BF16)
                wB = Buf()
                wl = w_in[l]
                for k in range(8):
                    S.dma("pool", wA[:, k, :], wl[k * 128:(k + 1) * 128, 0:2208], writes=[wB])
                    rows = wl[k * 128:(k + 1) * 128, :]
                    S.dma("pool", wS[:, k, 0:16], rows[:, 784:800], writes=[wB])
                    S.dma("pool", wS[:, k, 16:32], rows[:, 768:784], writes=[wB])
                    for (dst0, src0) in ((32, 800), (288, 1952)):
                        src = rows[:, src0:src0 + 256].rearrange("p (h two d) -> p h two d", two=2, d=32)
                        dst = wS[:, k, dst0:dst0 + 256].rearrange("p (h two d) -> p h two d", two=2, d=32)
                        S.dma("pool", dst[:, :, 0, :], src[:, :, 1, :], writes=[wB])
                        S.dma("pool", dst[:, :, 1, :], src[:, :, 0, :], writes=[wB])
                for k in range(2):
                    src = w_ukv[l][k * 128:(k + 1) * 128, :].rearrange("p (h x) -> p h x", x=128)
                    S.dma("pool", wkvN[:, k, :].rearrange("p (h x) -> p h x", x=64), src[:, :, 0:64], writes=[wB])
                    S.dma("pool", wkvV[:, k, :].rearrange("p (h x) -> p h x", x=64), src[:, :, 64:128], writes=[wB])
                for k in range(3):
                    src = w_uq[l][k * 128:(k + 1) * 128, :].rearrange("p (h x) -> p h x", x=96)
                    S.dma("pool", wqN[:, k, :].rearrange("p (h x) -> p h x", x=64), src[:, :, 0:64], writes=[wB])
                    S.dma("pool", wqR[:, k, :].rearrange("p (h x) -> p h x", x=32), src[:, :, 64:96], writes=[wB])
                    S.dma("pool", wqRs[:, k, :].rearrange("p (h x) -> p h x", x=32)[:, :, 0:16], src[:, :, 80:96],
                          writes=[wB])
                    S.dma("pool", wqRs[:, k, :].rearrange("p (h x) -> p h x", x=32)[:, :, 16:32], src[:, :, 64:80],
                          writes=[wB])

                xt = sb(st, "p1xt", [128, 8, 512], F32); xB = Buf()
                ht = sb(st, "p1ht", [128, 8, 512], BF16); hB = Buf()
                wk = mk_norm_wk(st, "p1")
                tabA = sb(st, "tabA", [128, 2, 512], F32)
                tabR = sb(st, "tabR", [128, 2, 512], F32)
                tabRk = sb(st, "tabRk", [128, 2, 512], F32)
                tabB = Buf()
                stf = [sb(st, f"stf{i}", [128, 4, 512], F32) for i in range(2)]; stfB = [Buf(), Buf()]
                stb = [sb(st, f"stb{i}", [128, 4, 512], BF16) for i in range(3)]; stbB = [Buf() for _ in range(3)]
                lat = sb(st, "p1lat", [128, 3, 512], F32); latB = Buf()
                latsq = sb(st, "p1latsq", [128, 3, 512], BF16); latsqB = Buf()
                latn = sb(st, "p1latn", [128, 3, 512], BF16); latnB = Buf()
                lrs = sb(st, "p1lrs", [128, 512], F32); lrsB = Buf()
                r1 = sb(st, "p1r1", [128, 512], F32); r1B = Buf()
                r2 = sb(st, "p1r2", [128, 512], F32); r2B = Buf()
                cnt = {"f": 0, "b": 0}

                def nstf():
                    i = cnt["f"] % 2; cnt["f"] += 1
                    return stf[i], stfB[i]

                def nstb():
                    i = cnt["b"] % 3; cnt["b"] += 1
                    return stb[i], stbB[i]

                def rope_fm(pairs_lhs, pairs_lhs_sw, rhs_list, M, N, tab, dst_ap, dstB, reads):
                    p1, b1 = P.next()
                    S.group(b1, p1[:M, :N], list(zip(pairs_lhs, rhs_list)), reads=reads)
                    p2, b2 = P.next()
                    S.group(b2, p2[:M, :N], list(zip(pairs_lhs_sw, rhs_list)), reads=reads)
                    S.op("dve", lambda e: e.tensor_tensor(out=r1[:M, :N], in0=p1[:M, :N], in1=tab[:M, 0, :N], op=ALU.mult),
                         reads=[b1, tabB], writes=[r1B])
                    S.op("dve", lambda e: e.tensor_tensor(out=r2[:M, :N], in0=p2[:M, :N], in1=tab[:M, 1, :N], op=ALU.mult),
                         reads=[b2, tabB], writes=[r2B])
                    S.op("dve", lambda e: e.tensor_tensor(out=dst_ap, in0=r1[:M, :N], in1=r2[:M, :N], op=ALU.add),
                         reads=[r1B, r2B], writes=[dstB])

                for ti, (t0, N) in enumerate(TILES):
                    jj = 1 if ti == 0 else 0
                    ns = N // 128
                    S.dma("sp", xt[:, :, :N], xsrc[:, t0:t0 + N].rearrange("(k p) t -> p k t", p=128), writes=[xB])
                    S.dma("sp", tabA[:, :, :N], ropeA[:, :, t0:t0 + N].rearrange("c p t -> p c t"), writes=[tabB])
                    S.dma("sp", tabR[:, :, :N], retA[:, :, t0:t0 + N].rearrange("c p t -> p c t"), writes=[tabB])
                    S.dma("sp", tabRk[:, :, :N], retAk[:, :, t0:t0 + N].rearrange("c p t -> p c t"), writes=[tabB])
                    norm_mod(st, xt, xB, N, jj, amix, 0, ht, hB, wk)
                    S.dma("pool", Hs[:, t0:t0 + N].rearrange("(k p) t -> p k t", p=128), ht[:, :, :N], reads=[hB])
                    hr = [ht[:, k, :N] for k in range(8)]
                    R = [hB, wB]
                    sf, sfB = nstf()
                    for m in range(4):
                        pt, pb = P.next()
                        S.group(pb, pt[:, :N], [(wA[:, k, m * 128:(m + 1) * 128], hr[k]) for k in range(8)], reads=R)
                        S.op("act", lambda e, m=m, pt=pt: e.activation(out=sf[:, m, :N], in_=pt[:, :N], func=AF.Copy),
                             reads=[pb], writes=[sfB])
                    S.dma("pool", Us[:, t0:t0 + N].rearrange("(k p) t -> p k t", p=128), sf[:, :, :N], reads=[sfB])

                    def latent_norm(c0, nchunk, gname):
                        for m in range(nchunk):
                            pt, pb = P.next()
                            S.group(pb, pt[:, :N], [(wA[:, k, c0 + m * 128:c0 + (m + 1) * 128], hr[k]) for k in range(8)],
                                    reads=R)
                            S.op("act", lambda e, m=m, pt=pt: e.activation(out=lat[:, m, :N], in_=pt[:, :N], func=AF.Copy),
                                 reads=[pb], writes=[latB])
                        S.op("act", lambda e: e.activation(out=latsq[:, :nchunk, :N], in_=lat[:, :nchunk, :N],
                                                           func=AF.Square), reads=[latB], writes=[latsqB])
                        rs = rms_rstd(st, [latsq[:, m, :N] for m in range(nchunk)], 1.0 / (128 * nchunk), N,
                                      {"reads": [latsqB], "sq": lrs, "sqB": lrsB})
                        for m in range(nchunk):
                            S.op("dve", lambda e, m=m: e.scalar_tensor_tensor(
                                out=latn[:, m, :N], in0=lat[:, m, :N], scalar=vcol(gname, m), in1=rs[:, :N],
                                op0=ALU.mult, op1=ALU.mult), reads=[latB, lrsB, vecsB], writes=[latnB])

                    latent_norm(512, 2, f"gkv{l}")
                    sbf, sbB = nstb()
                    for m in range(4):
                        pt, pb = P.next()
                        S.group(pb, pt[:, :N], [(wkvN[:, k, m * 128:(m + 1) * 128], latn[:, k, :N]) for k in range(2)],
                                reads=[latnB, wB])
                        S.op("act", lambda e, m=m, pt=pt: e.activation(out=sbf[:, m, :N], in_=pt[:, :N], func=AF.Copy),
                             reads=[pb], writes=[sbB])
                    S.dma("pool", Kns[:, t0:t0 + N].rearrange("(k p) t -> p k t", p=128), sbf[:, :, :N], reads=[sbB])
                    sbf, sbB = nstb()
                    for s in range(ns):
                        pt, pb = P.next()
                        S.group(pb, pt[:, :], [(latn[:, k, s * 128:(s + 1) * 128], wkvV[:, k, :]) for k in range(2)],
                                reads=[latnB, wB])
                        S.op("act", lambda e, s=s, pt=pt: e.activation(out=sbf[:, s, :], in_=pt[:, :], func=AF.Copy),
                             reads=[pb], writes=[sbB])
                    S.dma("pool", Vs[t0:t0 + N, :].rearrange("(s p) c -> p s c", p=128), sbf[:, :ns, :], reads=[sbB])

                    sbf, sbB = nstb()
                    rope_fm([wA[:, k, 768:800] for k in range(8)], [wS[:, k, 0:32] for k in range(8)], hr, 32, N,
                            tabA, sbf[:32, 0, :N], sbB, R)
                    S.dma("pool", Krs[:, t0:t0 + N], sbf[:32, 0, :N], reads=[sbB])

                    sbf, sbB = nstb()
                    for m in range(2):
                        rope_fm([wA[:, k, 800 + m * 128:800 + (m + 1) * 128] for k in range(8)],
                                [wS[:, k, 32 + m * 128:32 + (m + 1) * 128] for k in range(8)], hr, 128, N,
                                tabRk, sbf[:, m, :N], sbB, R)
                    S.dma("pool", RKf[:, t0:t0 + N].rearrange("(k p) t -> p k t", p=128), sbf[:, 0:2, :N], reads=[sbB])
                    sb2, sb2B = nstb()
                    for s in range(ns):
                        for m in range(2):
                            c0 = (s * 2 + m) * 128
                            S.transpose(P.bb, P.tb[:, c0:c0 + 128], sbf[:, m, s * 128:(s + 1) * 128], identb[:],
                                        reads=[sbB, constB])
                    S.op("act", lambda e: e.activation(out=sb2[:, 0:2, :].rearrange("p a b -> p (a b)")[:, :ns * 256],
                                                       in_=P.tb[:, :ns * 256], func=AF.Copy),
                         reads=[P.bb], writes=[sb2B])
                    S.dma("pool", RKt[t0:t0 + N, :].rearrange("(s p) c -> p s c", p=128),
                          sb2[:, 0:2, :].rearrange("p a b -> p (a b)")[:, :ns * 256].rearrange("p (s c) -> p s c", c=256),
                          reads=[sb2B])

                    sbf, sbB = nstb()
                    for s in range(ns):
                        pt, pb = P.next()
                        S.group(pb, pt[:, :], [(ht[:, k, s * 128:(s + 1) * 128], wA[:, k, 1056:1568]) for k in range(8)],
                                reads=R)
                        S.op("act", lambda e, s=s, pt=pt: e.activation(out=sbf[:, s, :], in_=pt[:, :], func=AF.Copy),
                             reads=[pb], writes=[sbB])
                    S.dma("pool", RVs[t0:t0 + N, :].rearrange("(s p) c -> p s c", p=128), sbf[:, :ns, :], reads=[sbB])

                    latent_norm(1568, 3, f"gq{l}")
                    qr = [latn[:, k, :N] for k in range(3)]
                    sbf, sbB = nstb()
                    for m in range(4):
                        pt, pb = P.next()
                        S.group(pb, pt[:, :N], [(wqN[:, k, m * 128:(m + 1) * 128], qr[k]) for k in range(3)],
                                reads=[latnB, wB])
                        S.op("act", lambda e, m=m, pt=pt: e.activation(out=sbf[:, m, :N], in_=pt[:, :N], func=AF.Copy),
                             reads=[pb], writes=[sbB])
                    S.dma("pool", Qns[:, t0:t0 + N].rearrange("(k p) t -> p k t", p=128), sbf[:, :, :N], reads=[sbB])
                    sbf, sbB = nstb()
                    for m in range(2):
                        rope_fm([wqR[:, k, m * 128:(m + 1) * 128] for k in range(3)],
                                [wqRs[:, k, m * 128:(m + 1) * 128] for k in range(3)], qr, 128, N,
                                tabA, sbf[:, m, :N], sbB, [latnB, wB])
                    S.dma("pool", Qrs[:, t0:t0 + N].rearrange("(k p) t -> p k t", p=128), sbf[:, 0:2, :N], reads=[sbB])

                    sbf, sbB = nstb()
                    for m in range(2):
                        rope_fm([wA[:, k, 1952 + m * 128:1952 + (m + 1) * 128] for k in range(8)],
                                [wS[:, k, 288 + m * 128:288 + (m + 1) * 128] for k in range(8)], hr, 128, N,
                                tabR, sbf[:, m, :N], sbB, R)
                    S.dma("pool", RQs[:, t0:t0 + N].rearrange("(k p) t -> p k t", p=128), sbf[:, 0:2, :N], reads=[sbB])
                S.barrier()
            if stop_after == (l, 1):
                break

            with ExitStack() as st:
                wblk = sb(st, "wblk", [128, 16, 128], BF16)
                wbB = Buf()
                S.op("dve", lambda e: e.memset(wblk[:], 0.0), writes=[wbB])
                for d in range(2):
                    for g, wsrc in enumerate((lru_wa, lru_wx)):
                        for c in range(4):
                            idx = (d * 2 + g) * 4 + c
                            S.dma("pool", wblk[0:64, idx, 0:64], wsrc[l, d, 2 * c], writes=[wbB])
                            S.dma("pool", wblk[64:128, idx, 64:128], wsrc[l, d, 2 * c + 1], writes=[wbB])
                coef = sb(st, "coef", [128, 16], F32)
                coefB = Buf()
                lamc = VC[f"lam{l}"]
                S.op("act", lambda e: e.activation(out=coef[:, 0:8], in_=vecs[:, lamc:lamc + 8], func=AF.Exp, scale=-1.0),
                     reads=[vecsB], writes=[coefB])
                S.op("act", lambda e: e.activation(out=coef[:, 0:8], in_=coef[:, 0:8], func=AF.Ln, bias=1.0, scale=1.0),
                     writes=[coefB])
                S.op("dve", lambda e: e.tensor_scalar(out=coef[:, 8:16], in0=coef[:, 0:8], scalar1=-16.0, scalar2=None,
                                                      op0=ALU.mult), writes=[coefB])
                S.op("dve", lambda e: e.tensor_scalar(out=coef[:, 0:8], in0=coef[:, 0:8], scalar1=-8.0, scalar2=None,
                                                      op0=ALU.mult), writes=[coefB])
                Uc = sb(st, "Uc", [128, T], F32); UcB = Buf()
                u = sb(st, "u", [128, T], F32); uB = Buf()
                ub = sb(st, "ub", [128, T], BF16); ubB = Buf()
                rr = sb(st, "rr", [128, T], F32); rrB = Buf()
                ii = sb(st, "ii", [128, T], F32); iiB = Buf()
                e2 = sb(st, "e2", [128, T], F32); e2B = Buf()
                hs = [sb(st, f"hs{d}", [128, T], F32) for d in range(2)]; hsB = [Buf(), Buf()]
                recb = sb(st, "recb", [128, T], BF16); recB = Buf()
                cw = VC[f"convw{l}"]
                for c in range(4):
                    S.dma("sp", Uc[:], Us[c * 128:(c + 1) * 128, :], writes=[UcB])
                    S.op("dve", lambda e, c=c: e.tensor_scalar(
                        out=u[:], in0=Uc[:], scalar1=vecs[:, cw + 2 * 4 + c:cw + 2 * 4 + c + 1],
                        scalar2=vcol(f"convb{l}", c), op0=ALU.mult, op1=ALU.add), reads=[UcB, vecsB], writes=[uB])
                    for (s0, s1) in ((0, LC), (LC, T)):
                        for tap, off in ((0, -2), (1, -1), (3, 1)):
                            lo = s0 + max(0, -off)
                            hi = s1 - max(0, off)
                            S.op("dve", lambda e, c=c, tap=tap, lo=lo, hi=hi, off=off: e.scalar_tensor_tensor(
                                out=u[:, lo:hi], in0=Uc[:, lo + off:hi + off],
                                scalar=vecs[:, cw + tap * 4 + c:cw + tap * 4 + c + 1], in1=u[:, lo:hi],
                                op0=ALU.mult, op1=ALU.add), reads=[UcB, vecsB], writes=[uB])
                    S.op("act", lambda e: e.activation(out=ub[:], in_=u[:], func=AF.Copy), reads=[uB], writes=[ubB])
                    for d in range(2):
                        for g, (dst, dB, bname) in enumerate(((rr, rrB, f"ba{l}"), (ii, iiB, f"bx{l}"))):
                            idx = (d * 2 + g) * 4 + c
                            for (t0, N) in TILES:
                                pt, pb = P.next()
                                S.group(pb, pt[:, :N], [(wblk[:, idx, :], ub[:, t0:t0 + N])], reads=[wbB, ubB])
                                S.op("act", lambda e, pt=pt, dst=dst, t0=t0, N=N, bname=bname, d=d, c=c: e.activation(
                                    out=dst[:, t0:t0 + N], in_=pt[:, :N], func=AF.Sigmoid,
                                    bias=vcol(bname, d * 4 + c), scale=1.0), reads=[pb, vecsB], writes=[dB])
                        S.op("act", lambda e, d=d, c=c: e.activation(out=e2[:], in_=rr[:], func=AF.Exp,
                                                                     scale=coef[:, 8 + d * 4 + c:8 + d * 4 + c + 1]),
                             reads=[rrB, coefB], writes=[e2B])
                        S.op("act", lambda e, d=d, c=c: e.activation(out=rr[:], in_=rr[:], func=AF.Exp,
                                                                     scale=coef[:, d * 4 + c:d * 4 + c + 1]),
                             reads=[coefB], writes=[rrB])
                        S.op("act", lambda e: e.activation(out=e2[:], in_=e2[:], func=AF.Sqrt, bias=1.0, scale=-1.0),
                             writes=[e2B])
                        S.op("dve", lambda e: e.tensor_tensor(out=ii[:], in0=ii[:], in1=u[:], op=ALU.mult),
                             reads=[uB], writes=[iiB])
                        S.op("dve", lambda e: e.tensor_tensor(out=ii[:], in0=ii[:], in1=e2[:], op=ALU.mult),
                             reads=[e2B], writes=[iiB])
                        h = hs[d]
                        if d == 0:
                            S.op("dve", lambda e, h=h: e.tensor_tensor_scan(out=h[:], data0=rr[:], data1=ii[:], initial=0.0,
                                                                           op0=ALU.mult, op1=ALU.add),
                                 reads=[rrB, iiB], writes=[hsB[d]])
                        else:
                            S.op("dve", lambda e, h=h: e.tensor_tensor_scan(
                                out=h[:, 0:LC][:, ::-1], data0=rr[:, 0:LC][:, ::-1],
                                data1=ii[:, 0:LC][:, ::-1], initial=0.0, op0=ALU.mult, op1=ALU.add),
                                 reads=[rrB, iiB], writes=[hsB[d]])
                            S.op("dve", lambda e, h=h: e.tensor_tensor_scan(
                                out=h[:, LC:T][:, ::-1], data0=rr[:, LC:T][:, ::-1], data1=ii[:, LC:T][:, ::-1],
                                initial=h[:, 0:1], op0=ALU.mult, op1=ALU.add),
                                 reads=[rrB, iiB], writes=[hsB[d]])
                    S.op("dve", lambda e: e.tensor_tensor(out=recb[:], in0=hs[0][:], in1=hs[1][:], op=ALU.add),
                         reads=[hsB[0], hsB[1]], writes=[recB])
                    S.dma("pool", RECs[c * 128:(c + 1) * 128, :], recb[:], reads=[recB])
                S.barrier()
            if stop_after == (l, 2):
                break

            with ExitStack() as st:
                Kt = [sb(st, f"Kt{i}", [96, T], BF16) for i in range(2)]
                Qt = [sb(st, f"Qt{i}", [96, T], BF16) for i in range(2)]
                Vt = [sb(st, f"Vt{i}", [128, NCH, 128], BF16) for i in range(2)]
                hdB = [Buf(), Buf()]
                Pt = [sb(st, f"Pt{i}", [128, 512], BF16) for i in range(4)]; PtB = [Buf() for _ in range(4)]
                rd = sb(st, "rd", [128, 512], F32); rdB = Buf()
                osb = sb(st, "osb", [64, 512], F32); osbB = Buf()
                ybs = [sb(st, f"ybs{i}", [64, 512], BF16) for i in range(2)]; ybsB = [Buf(), Buf()]
                for i in range(2):
                    S.op("dve", lambda e, i=i: e.memset(Vt[i][:, :, 64:128], 1.0), writes=[hdB[i]])
                scale = 96.0 ** -0.5
                pc = 0
                oc = 0
                for h in range(8):
                    b = h % 2
                    S.dma("sp", Kt[b][0:64, :], Kns[h * 64:(h + 1) * 64, :], writes=[hdB[b]])
                    S.dma("sp", Kt[b][64:96, :], Krs[:, :], writes=[hdB[b]])
                    S.dma("sp", Qt[b][0:64, :], Qns[h * 64:(h + 1) * 64, :], writes=[hdB[b]])
                    S.dma("sp", Qt[b][64:96, :], Qrs[h * 32:(h + 1) * 32, :], writes=[hdB[b]])
                    S.dma("sp", Vt[b][:, :, 0:64], Vs[:, h * 64:(h + 1) * 64].rearrange("(j p) d -> p j d", p=128),
                          writes=[hdB[b]])
                    qtiles = TILES[1:] + ([TILES[0]] if not last else [])
                    for (q0, N) in qtiles:
                        keys = range(NCH) if q0 >= LC else range(LC // 128)
                        po, pob = P.next()
                        nk = len(keys)
                        for ji, j in enumerate(keys):
                            pS, pSb = P.next()
                            if pS is po:
                                pS, pSb = P.next()
                            S.group(pSb, pS[:, :N], [(Kt[b][:, j * 128:(j + 1) * 128], Qt[b][:, q0:q0 + N])], reads=[hdB[b]])
                            pt_, ptB_ = Pt[pc % 4], PtB[pc % 4]
                            pc += 1
                            S.op("act", lambda e, pS=pS, pt_=pt_: e.activation(out=pt_[:, :N], in_=pS[:, :N], func=AF.Exp,
                                                                               scale=scale), reads=[pSb], writes=[ptB_])
                            S.mm(po[:, :N], Vt[b][:, j, :], pt_[:, :N], ji == 0, ji == nk - 1, reads=[ptB_, hdB[b]],
                                 out_buf=pob)
                        S.op("dve", lambda e: e.reciprocal(out=rd[64:128, :N], in_=po[64:128, :N]), reads=[pob], writes=[rdB])
                        S.op("act", lambda e: e.activation(out=osb[:, :N], in_=po[0:64, :N], func=AF.Copy),
                             reads=[pob], writes=[osbB])
                        pr, prb = P.next()
                        S.group(prb, pr[0:64, :N], [(cm[64:128, 192:256], rd[64:128, :N])], reads=[rdB, cmB])
                        yb_, ybB_ = ybs[oc % 2], ybsB[oc % 2]
                        oc += 1
                        S.op("dve", lambda e, yb_=yb_, pr=pr: e.tensor_tensor(out=yb_[:, :N], in0=osb[:, :N],
                                                                              in1=pr[0:64, :N], op=ALU.mult),
                             reads=[osbB, prb], writes=[ybB_])
                        S.dma("pool", YBs[h * 64:(h + 1) * 64, q0:q0 + N], yb_[:, :N], reads=[ybB_])
                S.barrier()
            if stop_after == (l, 3):
                break

            with ExitStack() as st:
                rq = sb(st, "rq", [64, T], BF16)
                rkf = sb(st, "rkf", [64, T], BF16)
                ktm = sb(st, "ktm", [128, NCH, 64], BF16)
                vtm = sb(st, "vtm", [128, NCH, 128], BF16)
                ldB = Buf()
                qd = sb(st, "qd", [64, 2, 128], F32)
                mk = sb(st, "mk", [128, 512], F32)
                rdc = sb(st, "rdc", [128, 32], F32)
                cB = Buf()
                S.dma("sp", rdc[:], rdec, writes=[cB])
                qfr = [sb(st, f"qfr{d}", [64, T], BF16) for d in range(2)]; qfrB = [Buf(), Buf()]
                kfr = [sb(st, f"kfr{d}", [128, NCH, 64], BF16) for d in range(2)]; kfrB = [Buf(), Buf()]
                Rst = [sb(st, f"Rst{d}", [64, NCH, 128], BF16) for d in range(2)]; RstB = [Buf(), Buf()]
                rcur = [sb(st, f"rcur{d}", [64, 128], F32) for d in range(2)]; rcurB = [Buf(), Buf()]
                sm = [sb(st, f"sm{i}", [128, 512], BF16) for i in range(2)]; smB = [Buf(), Buf()]
                Oh = sb(st, "Oh", [128, T], F32); OhB = Buf()
                osq = sb(st, "osq", [128, 512], F32); osqB = Buf()
                msq = sb(st, "msq", [128, 512], F32); msqB = Buf()
                var = sb(st, "var", [128, 512], F32); varB = Buf()
                cen = sb(st, "cen", [128, 512], F32); cenB = Buf()
                rn = [sb(st, f"rn{i}", [128, 512], BF16) for i in range(2)]; rnB = [Buf(), Buf()]
                orders = [list(range(NCH)), [1, 0] + list(range(NCH - 1, 1, -1))]
                smc = 0
                rnc = 0
                for h in range(4):
                    S.dma("sp", rq[:], RQs[h * 64:(h + 1) * 64, :], writes=[ldB])
                    S.dma("sp", rkf[:], RKf[h * 64:(h + 1) * 64, :], writes=[ldB])
                    S.dma("sp", ktm[:], RKt[:, h * 64:(h + 1) * 64].rearrange("(c p) d -> p c d", p=128), writes=[ldB])
                    S.dma("sp", vtm[:], RVs[:, h * 128:(h + 1) * 128].rearrange("(c p) d -> p c d", p=128), writes=[ldB])
                    S.dma("sp", qd[:], qdec[h].rearrange("d p j -> p d j"), writes=[cB])
                    S.dma("sp", mk[:], maskT[h], writes=[cB])
                    for d in range(2):
                        S.op("dve", lambda e, d=d: e.tensor_tensor(
                            out=qfr[d][:].rearrange("p (c j) -> p c j", j=128),
                            in0=rq[:].rearrange("p (c j) -> p c j", j=128),
                            in1=qd[:, d, :].unsqueeze(1).to_broadcast([64, NCH, 128]), op=ALU.mult),
                             reads=[ldB, cB], writes=[qfrB[d]])
                        S.op("dve", lambda e, d=d, h=h: e.tensor_scalar(
                            out=kfr[d][:], in0=ktm[:], scalar1=rdc[:, h * 8 + d:h * 8 + d + 1], scalar2=None,
                            op0=ALU.mult), reads=[ldB, cB], writes=[kfrB[d]])
                        S.op("dve", lambda e, d=d: e.memset(rcur[d][:], 0.0), writes=[rcurB[d]])
                        for c in orders[d]:
                            S.op("act", lambda e, d=d, c=c: e.activation(out=Rst[d][:, c, :], in_=rcur[d][:], func=AF.Copy),
                                 reads=[rcurB[d]], writes=[RstB[d]])
                            pt, pb = P.next()
                            S.group(pb, pt[0:64, 0:128], [(kfr[d][:, c, :], vtm[:, c, :])], reads=[kfrB[d], ldB])
                            S.op("dve", lambda e, d=d, pt=pt, h=h: e.scalar_tensor_tensor(
                                out=rcur[d][:], in0=rcur[d][:], scalar=rdc[0:64, h * 8 + 2 + d:h * 8 + 3 + d],
                                in1=pt[0:64, 0:128], op0=ALU.mult, op1=ALU.add), reads=[pb, cB], writes=[rcurB[d]])
                    for g in range((NCH + 3) // 4):
                        cs = list(range(g * 4, min(NCH, g * 4 + 4)))
                        W = len(cs) * 128
                        pS, pSb = P.next()
                        for ci, c in enumerate(cs):
                            S.group(pSb, pS[:, ci * 128:(ci + 1) * 128],
                                    [(rkf[:, c * 128:(c + 1) * 128], rq[:, c * 128:(c + 1) * 128])], reads=[ldB])
                        sm_, smB_ = sm[smc % 2], smB[smc % 2]
                        smc += 1
                        S.op("dve", lambda e, pS=pS, sm_=sm_, W=W: e.tensor_tensor(out=sm_[:, :W], in0=pS[:, :W],
                                                                                  in1=mk[:, :W], op=ALU.mult),
                             reads=[pSb, cB], writes=[smB_])
                        po, pob = P.next()
                        for ci, c in enumerate(cs):
                            oap = po[:, ci * 128:(ci + 1) * 128]
                            S.mm(oap, vtm[:, c, :], sm_[:, ci * 128:(ci + 1) * 128], True, False,
                                 reads=[ldB, smB_], out_buf=pob)
                            S.mm(oap, Rst[0][:, c, :], qfr[0][:, c * 128:(c + 1) * 128], False, False,
                                 reads=[RstB[0], qfrB[0]], out_buf=pob)
                            S.mm(oap, Rst[1][:, c, :], qfr[1][:, c * 128:(c + 1) * 128], False, True,
                                 reads=[RstB[1], qfrB[1]], out_buf=pob)
                        S.op("act", lambda e, po=po, g=g, W=W: e.activation(out=Oh[:, g * 512:g * 512 + W], in_=po[:, :W],
                                                                            func=AF.Copy), reads=[pob], writes=[OhB])
                    for (t0, N) in TILES:
                        pm, pmb = P.next()
                        S.group(pmb, pm[:, :N], [(onesf, Oh[:, t0:t0 + N])], reads=[OhB, cmB])
                        S.op("act", lambda e, t0=t0, N=N: e.activation(out=osq[:, :N], in_=Oh[:, t0:t0 + N], func=AF.Square),
                             reads=[OhB], writes=[osqB])
                        pe2, pe2b = P.next()
                        S.group(pe2b, pe2[:, :N], [(onesf, osq[:, :N])], reads=[osqB, cmB])
                        S.op("act", lambda e, pm=pm, N=N: e.activation(out=msq[:, :N], in_=pm[:, :N], func=AF.Square,
                                                                       scale=1.0 / 128), reads=[pmb], writes=[msqB])
                        S.op("dve", lambda e, pe2=pe2, N=N: e.scalar_tensor_tensor(
                            out=var[:, :N], in0=pe2[:, :N], scalar=1.0 / 128, in1=msq[:, :N], op0=ALU.mult,
                            op1=ALU.subtract), reads=[pe2b, msqB], writes=[varB])
                        S.op("act", lambda e, N=N: e.activation(out=var[:, :N], in_=var[:, :N], func=AF.Sqrt, bias=epsc[:],
                                                                scale=1.0), reads=[constB], writes=[varB])
                        S.op("dve", lambda e, N=N: e.reciprocal(out=var[:, :N], in_=var[:, :N]), writes=[varB])
                        S.op("dve", lambda e, pm=pm, t0=t0, N=N: e.scalar_tensor_tensor(
                            out=cen[:, :N], in0=pm[:, :N], scalar=-1.0 / 128, in1=Oh[:, t0:t0 + N], op0=ALU.mult,
                            op1=ALU.add), reads=[pmb, OhB], writes=[cenB])
                        rn_, rnB_ = rn[rnc % 2], rnB[rnc % 2]
                        rnc += 1
                        S.op("dve", lambda e, rn_=rn_, N=N: e.tensor_tensor(out=rn_[:, :N], in0=cen[:, :N], in1=var[:, :N],
                                                                           op=ALU.mult), reads=[cenB, varB], writes=[rnB_])
                        S.dma("pool", RETs[h * 128:(h + 1) * 128, t0:t0 + N], rn_[:, :N], reads=[rnB_])
                S.barrier()
            if stop_after == (l, 4):
                break

            with ExitStack() as st:
                wG = sb(st, "wG", [128, 8, 4096], BF16)
                wo = [sb(st, f"wo{i}", [128, 4, 1024], BF16) for i in range(3)]
                wout = sb(st, "wout", [128, 8, 1024], BF16)
                wB = Buf()
                for k in range(8):
                    S.dma("pool", wG[:, k, :], w_in[l][k * 128:(k + 1) * 128, 2208:NIN], writes=[wB])
                    S.dma("pool", wout[:, k, :], w_out[l][k * 128:(k + 1) * 128, :], writes=[wB])
                for i, wsrc in enumerate((w_oa, w_ob, w_oc)):
                    S.dma("pool", wo[i][:], wsrc[l].rearrange("(k p) c -> p k c", p=128), writes=[wB])
                ht = sb(st, "aht", [128, 8, 512], BF16)
                xt = sb(st, "axt", [128, 8, 512], F32)
                yin = [sb(st, f"yin{i}", [128, 4, 512], BF16) for i in range(3)]
                inB = Buf()
                gt = sb(st, "agt", [128, 512], F32); gtB = Buf()
                sg = [sb(st, f"sg{i}", [128, 512], F32) for i in range(3)]; sgB = [Buf() for _ in range(3)]
                macc = sb(st, "macc", [128, 512], F32); maccB = Buf()
                mt2 = sb(st, "mt2", [128, 512], F32); mt2B = Buf()
                mb = sb(st, "amb", [128, 8, 512], BF16); mbB = Buf()
                x1t = sb(st, "x1t", [128, 8, 512], F32); x1B = Buf()
                for ti, (t0, N) in enumerate(TILES):
                    if last and ti == 0:
                        continue
                    jj = 1 if ti == 0 else 0
                    S.dma("sp", ht[:, :, :N], Hs[:, t0:t0 + N].rearrange("(k p) t -> p k t", p=128), writes=[inB])
                    S.dma("sp", xt[:, :, :N], xsrc[:, t0:t0 + N].rearrange("(k p) t -> p k t", p=128), writes=[inB])
                    for i, src in enumerate((RECs, YBs, RETs)):
                        S.dma("sp", yin[i][:, :, :N], src[:, t0:t0 + N].rearrange("(k p) t -> p k t", p=128), writes=[inB])
                    hr = [ht[:, k, :N] for k in range(8)]
                    for gi, (func, yi) in enumerate(((AF.Gelu, 0), (AF.Silu, 2))):
                        for m in range(4):
                            c0 = gi * 512 + m * 128
                            pt, pb = P.next()
                            S.group(pb, pt[:, :N], [(wG[:, k, c0:c0 + 128], hr[k]) for k in range(8)], reads=[inB, wB])
                            S.op("act", lambda e, pt=pt, func=func: e.activation(out=gt[:, :N], in_=pt[:, :N], func=func),
                                 reads=[pb], writes=[gtB])
                            S.op("dve", lambda e, yi=yi, m=m: e.tensor_tensor(out=yin[yi][:, m, :N], in0=gt[:, :N],
                                                                             in1=yin[yi][:, m, :N], op=ALU.mult),
                                 reads=[gtB], writes=[inB])
                    for mo in range(8):
                        for br in range(3):
                            c0 = 1024 + br * 1024 + mo * 128
                            pt, pb = P.next()
                            S.group(pb, pt[:, :N], [(wG[:, k, c0:c0 + 128], hr[k]) for k in range(8)], reads=[inB, wB])
                            S.op("act", lambda e, pt=pt, br=br: e.activation(out=sg[br][:, :N], in_=pt[:, :N],
                                                                             func=AF.Sigmoid), reads=[pb], writes=[sgB[br]])
                        for br in range(3):
                            pt, pb = P.next()
                            S.group(pb, pt[:, :N], [(wo[br][:, k, mo * 128:(mo + 1) * 128], yin[br][:, k, :N])
                                                    for k in range(4)], reads=[inB, wB])
                            if br == 0:
                                S.op("dve", lambda e, pt=pt: e.tensor_tensor(out=macc[:, :N], in0=pt[:, :N], in1=sg[0][:, :N],
                                                                             op=ALU.mult), reads=[pb, sgB[0]], writes=[maccB])
                            else:
                                S.op("dve", lambda e, pt=pt, br=br: e.tensor_tensor(out=mt2[:, :N], in0=pt[:, :N],
                                                                                    in1=sg[br][:, :N], op=ALU.mult),
                                     reads=[pb, sgB[br]], writes=[mt2B])
                                if br == 1:
                                    S.op("dve", lambda e: e.tensor_tensor(out=macc[:, :N], in0=macc[:, :N], in1=mt2[:, :N],
                                                                          op=ALU.add), reads=[mt2B], writes=[maccB])
                                else:
                                    S.op("dve", lambda e, mo=mo: e.tensor_tensor(out=mb[:, mo, :N], in0=macc[:, :N],
                                                                                 in1=mt2[:, :N], op=ALU.add),
                                         reads=[mt2B, maccB], writes=[mbB])
                    for mo in range(8):
                        pt, pb = P.next()
                        S.group(pb, pt[:, :N], [(wout[:, k, mo * 128:(mo + 1) * 128], mb[:, k, :N]) for k in range(8)],
                                reads=[mbB, wB])
                        S.op("dve", lambda e, pt=pt, mo=mo: e.scalar_tensor_tensor(
                            out=x1t[:, mo, :N], in0=pt[:, :N], scalar=modv[:, 32 + 2 * mo + jj:32 + 2 * mo + jj + 1],
                            in1=xt[:, mo, :N], op0=ALU.mult, op1=ALU.add), reads=[pb, inB, modvB], writes=[x1B])
                    S.dma("pool", X1s[:, t0:t0 + N].rearrange("(k p) t -> p k t", p=128), x1t[:, :, :N], reads=[x1B])
                S.barrier()
            if stop_after == (l, 5):
                break

            with ExitStack() as st:
                w1 = sb(st, "w1", [128, 8, DFF], BF16)
                w3 = sb(st, "w3", [128, 8, DFF], BF16)
                w2 = sb(st, "w2", [128, 22, D], BF16)
                wB = Buf()
                for k in range(8):
                    S.dma("pool", w1[:, k, :], w_ff1[l][k * 128:(k + 1) * 128, :], writes=[wB])
                    S.dma("pool", w3[:, k, :], w_ff3[l][k * 128:(k + 1) * 128, :], writes=[wB])
                for k in range(22):
                    S.dma("pool", w2[:, k, :], w_ff2[l][k * 128:(k + 1) * 128, :], writes=[wB])
                xt = sb(st, "bxt", [128, 8, 512], F32); xB = Buf()
                hf = sb(st, "bhf", [128, 8, 512], BF16); hfB = Buf()
                wk = mk_norm_wk(st, "b")
                s1 = sb(st, "bs1", [128, 512], F32); s1B = Buf()
                gg = sb(st, "bgg", [128, 22, 512], BF16); ggB = Buf()
                for ti, (t0, N) in enumerate(TILES):
                    if last and ti == 0:
                        continue
                    jj = 1 if ti == 0 else 0
                    S.dma("sp", xt[:, :, :N], X1s[:, t0:t0 + N].rearrange("(k p) t -> p k t", p=128), writes=[xB])
                    norm_mod(st, xt, xB, N, jj, affn, 48, hf, hfB, wk)
                    hr = [hf[:, k, :N] for k in range(8)]
                    for m in range(22):
                        pt, pb = P.next()
                        S.group(pb, pt[:, :N], [(w1[:, k, m * 128:(m + 1) * 128], hr[k]) for k in range(8)], reads=[hfB, wB])
                        S.op("act", lambda e, pt=pt: e.activation(out=s1[:, :N], in_=pt[:, :N], func=AF.Silu),
                             reads=[pb], writes=[s1B])
                        pt3, pb3 = P.next()
                        S.group(pb3, pt3[:, :N], [(w3[:, k, m * 128:(m + 1) * 128], hr[k]) for k in range(8)], reads=[hfB, wB])
                        S.op("dve", lambda e, pt3=pt3, m=m: e.tensor_tensor(out=gg[:, m, :N], in0=pt3[:, :N], in1=s1[:, :N],
                                                                           op=ALU.mult), reads=[pb3, s1B], writes=[ggB])
                    for mo in range(8):
                        pt, pb = P.next()
                        S.group(pb, pt[:, :N], [(w2[:, k, mo * 128:(mo + 1) * 128], gg[:, k, :N]) for k in range(22)],
                                reads=[ggB, wB])
                        S.op("dve", lambda e, pt=pt, mo=mo: e.scalar_tensor_tensor(
                            out=xt[:, mo, :N], in0=pt[:, :N], scalar=modv[:, 80 + 2 * mo + jj:80 + 2 * mo + jj + 1],
                            in1=xt[:, mo, :N], op0=ALU.mult, op1=ALU.add), reads=[pb, modvB], writes=[xB])
                    if not last:
                        S.dma("pool", X2s[:, t0:t0 + N].rearrange("(k p) t -> p k t", p=128), xt[:, :, :N], reads=[xB])
                    else:
                        S.op("act", lambda e: e.activation(out=wk["xsq"][:, :, :N], in_=xt[:, :, :N], func=AF.Square),
                             reads=[xB], writes=[wk["xsqB"]])
                        rs = rms_rstd(st, [wk["xsq"][:, k, :N] for k in range(8)], 1.0 / D, N,
                                      {"reads": [wk["xsqB"]], "sq": wk["rstd"], "sqB": wk["rstdB"]})
                        for k in range(8):
                            S.op("dve", lambda e, k=k: e.scalar_tensor_tensor(
                                out=xt[:, k, :N], in0=xt[:, k, :N], scalar=vcol("gfin", k), in1=rs[:, :N],
                                op0=ALU.mult, op1=ALU.mult), reads=[wk["rstdB"], vecsB], writes=[xB])
                        S.dma("pool", outT[:, t0 - LC:t0 - LC + N].rearrange("(k p) t -> p k t", p=128), xt[:, :, :N],
                              reads=[xB])
                S.barrier()
            if stop_after == (l, 6):
                break
        S.barrier()
    return nc


def _fm(v):
    return np.ascontiguousarray(np.asarray(v, np.float32).reshape(-1, 128).T)


def _const_tables():
    f32 = np.float32
    rows = LL // 64
    row = np.repeat(np.arange(rows, dtype=f32), 64)
    col = np.tile(np.arange(64, dtype=f32), rows)
    inv = np.power(f32(10000.0), -np.arange(8, dtype=f32) / f32(8)).astype(f32)
    ang = np.concatenate([row[:, None] * inv, col[:, None] * inv], axis=-1).astype(f32)
    cos = np.cos(ang).astype(f32); sin = np.sin(ang).astype(f32)
    C = np.ones((32, T), f32); Sg = np.zeros((32, T), f32)
    C[0:16, LC:] = cos.T; C[16:32, LC:] = cos.T
    Sg[0:16, LC:] = -sin.T; Sg[16:32, LC:] = sin.T
    ropeA = np.stack([np.tile(C, (4, 1)), np.tile(Sg, (4, 1))]).astype(f32)
    theta = (1.0 / np.power(f32(10000.0), np.linspace(0.0, 1.0, 32, dtype=f32))).astype(f32)
    pos = np.arange(T, dtype=f32)
    ang = (pos[:, None] * theta).astype(f32)
    cos = np.cos(ang).astype(f32); sin = np.sin(ang).astype(f32)
    C = np.concatenate([cos.T, cos.T], 0); Sg = np.concatenate([-sin.T, sin.T], 0)
    retA = np.stack([np.tile(C, (2, 1)), np.tile(Sg, (2, 1))]).astype(f32)
    retAk = (retA * f32(0.125)).astype(f32)
    hh = np.arange(4, dtype=f32)
    lg = [np.log1p(-np.exp2(-5.0 - hh)).astype(f32), np.log1p(-np.exp2(-5.5 - hh)).astype(f32)]
    p = np.arange(128, dtype=f32)
    rdec = np.zeros((128, 32), f32)
    qdec = np.zeros((4, 2, 64, 128), f32)
    maskT = np.zeros((4, 128, 512), f32)
    for h in range(4):
        gf, gr = lg[0][h], lg[1][h]
        rdec[:, h * 8 + 0] = np.exp(gf * (127.0 - p))
        rdec[:, h * 8 + 1] = np.exp(gr * p)
        rdec[:, h * 8 + 2] = np.exp(gf * 128.0)
        rdec[:, h * 8 + 3] = np.exp(gr * 128.0)
        qdec[h, 0] = np.exp(gf * (p + 1.0))[None, :]
        qdec[h, 1] = np.exp(gr * (128.0 - p))[None, :]
        jj, ii = np.meshgrid(p, p, indexing="ij")
        m = np.where(ii > jj, np.exp(gf * np.maximum(ii - jj, 0)), np.where(ii < jj, np.exp(gr * np.maximum(jj - ii, 0)), 2.0))
        maskT[h] = np.tile(m.astype(f32), (1, 4))
    cmat = np.concatenate([np.ones((128, 128), f32), np.eye(128, dtype=f32)], 1)
    return dict(ropeA=ropeA, retA=retA, retAk=retAk, rdec=rdec.astype(f32), qdec=qdec.astype(f32),
                maskT=maskT.astype(f32), cmat=cmat)


def _pack_vecs(inp, b):
    v = np.zeros((128, NV), np.float32)
    cc = np.stack([_fm(inp["c"][b]), _fm(inp["c_ctx"])], -1).reshape(128, 16)
    v[:, VC["c"]:VC["c"] + 16] = cc
    for l in range(DEPTH):
        v[:, VC[f"gmix{l}"]:VC[f"gmix{l}"] + 8] = _fm(inp["g_mix"][l])
        v[:, VC[f"gffn{l}"]:VC[f"gffn{l}"] + 8] = _fm(inp["g_ffn"][l])
        v[:, VC[f"bmod{l}"]:VC[f"bmod{l}"] + 48] = _fm(inp["b_mod"][l])
        for tap in range(4):
            v[:, VC[f"convw{l}"] + tap * 4:VC[f"convw{l}"] + tap * 4 + 4] = _fm(inp["conv_w"][l, tap])
        v[:, VC[f"convb{l}"]:VC[f"convb{l}"] + 4] = _fm(inp["conv_b"][l])
        for d in range(2):
            v[:, VC[f"ba{l}"] + d * 4:VC[f"ba{l}"] + d * 4 + 4] = _fm(inp["lru_ba"][l, d])
            v[:, VC[f"bx{l}"] + d * 4:VC[f"bx{l}"] + d * 4 + 4] = _fm(inp["lru_bx"][l, d])
            v[:, VC[f"lam{l}"] + d * 4:VC[f"lam{l}"] + d * 4 + 4] = _fm(inp["lru_lam"][l, d])
        v[:, VC[f"gq{l}"]:VC[f"gq{l}"] + 3] = _fm(inp["g_q"][l])
        v[:, VC[f"gkv{l}"]:VC[f"gkv{l}"] + 2] = _fm(inp["g_kv"][l])
    v[:, VC["gfin"]:VC["gfin"] + 8] = _fm(inp["g_final"])
    return v


WKEYS = ["w_mod", "w_in", "lru_wa", "lru_wx", "w_uq", "w_ukv", "w_oa", "w_ob", "w_oc", "w_out", "w_ff1", "w_ff3", "w_ff2"]


def make_in_maps(inp, cores=range(8)):
    inp = {k: np.asarray(v) for k, v in inp.items()}
    consts = _const_tables()
    shared = {k: np.ascontiguousarray(inp[k], dtype=np.float32) for k in WKEYS}
    shared.update(consts)
    maps = []
    for b in cores:
        m = dict(shared)
        m["xin"] = np.ascontiguousarray(np.concatenate([inp["ctx"][b].T, inp["x"][b].T], axis=1), dtype=np.float32)
        m["vecs"] = _pack_vecs(inp, b)
        maps.append(m)
    return maps


_NC_CACHE = {}


def kernel(**inputs):
    if "nc" not in _NC_CACHE:
        _NC_CACHE["nc"] = build_program()
    nc = _NC_CACHE["nc"]
    in_maps = make_in_maps(inputs)
    res = run_bass_kernel_spmd(nc, in_maps, core_ids=list(range(8)))
    out = np.stack([np.ascontiguousarray(r["outT"].T) for r in res.results], axis=0)
    return out.astype(np.float32)
```

```python
import numpy as np
from contextlib import ExitStack
import concourse.bass as bass
import concourse.mybir as mybir
from concourse.bass_utils import run_bass_kernel_spmd

F32 = mybir.dt.float32
BF16 = mybir.dt.bfloat16
AF = mybir.ActivationFunctionType
ALU = mybir.AluOpType

D = 1024
LC = 256
LL = 4096
T = LC + LL
DEPTH = 2
NIN = 6304
DFF = 2816
EPS = 1e-6
TILES = [(0, 256)] + [(256 + 512 * i, 512) for i in range(8)]
NCH = T // 128

VC = {}
_off = 0
def _reg(name, n):
    global _off
    VC[name] = _off
    _off += n
_reg("c", 16)
for _l in range(DEPTH):
    _reg(f"gmix{_l}", 8); _reg(f"gffn{_l}", 8); _reg(f"bmod{_l}", 48)
    _reg(f"convw{_l}", 16)
    _reg(f"convb{_l}", 4)
    _reg(f"ba{_l}", 8); _reg(f"bx{_l}", 8); _reg(f"lam{_l}", 8)
    _reg(f"gq{_l}", 3); _reg(f"gkv{_l}", 2)
_reg("gfin", 8)
NV = _off


class Buf:
    __slots__ = ("w", "rs", "excl")

    def __init__(self, excl=False):
        self.w = None
        self.rs = {}
        self.excl = excl


class Sched:
    def __init__(self, nc, es):
        self.nc = nc
        self.engs = {"pe": nc.tensor, "act": nc.scalar, "dve": nc.vector, "pool": nc.gpsimd, "sp": nc.sync}
        self.semobj = {}
        self.cnt = {}
        for e in ("pe", "act", "dve", "pool"):
            self.semobj[e] = es.enter_context(nc.semaphore("s_" + e))
            self.cnt[e] = 0
        self.R = 8
        self.rpos = {"sp": 0, "pool": 0}
        self.rcnt = {}
        for q in ("sp", "pool"):
            for j in range(self.R):
                self.semobj[(q, j)] = es.enter_context(nc.semaphore(f"d_{q}{j}"))
                self.rcnt[(q, j)] = 0
        self.known = {e: {} for e in self.engs}

    def _wait(self, eng, tok):
        if tok is None:
            return
        k, v = tok
        if self.known[eng].get(k, 0) >= v:
            return
        self.engs[eng].wait_ge(self.semobj[k], v)
        self.known[eng][k] = v

    def _deps(self, eng, reads, writes):
        for b in reads:
            if b.excl:
                self._wait(eng, b.w)
                for k, v in b.rs.items():
                    self._wait(eng, (k, v))
            else:
                self._wait(eng, b.w)
        for b in writes:
            self._wait(eng, b.w)
            for k, v in b.rs.items():
                self._wait(eng, (k, v))

    def _register(self, tok, reads, writes):
        k, v = tok
        for b in reads:
            if b.excl:
                b.w = tok
                b.rs = {}
            else:
                if b.rs.get(k, 0) < v:
                    b.rs[k] = v
        for b in writes:
            b.w = tok
            b.rs = {}

    def op(self, eng, fn, reads=(), writes=()):
        self._deps(eng, reads, writes)
        ins = fn(self.engs[eng])
        self.cnt[eng] += 1
        ins.then_inc(self.semobj[eng], 1)
        tok = (eng, self.cnt[eng])
        self._register(tok, reads, writes)
        return tok

    def mm(self, out_ap, lhsT, rhs, start, stop, reads=(), out_buf=None, inc=None):
        if inc is None:
            inc = stop
        self._deps("pe", reads, [out_buf] if start else [])
        ins = self.nc.tensor.matmul(out_ap, lhsT, rhs, start=start, stop=stop)
        tok = ("pe", self.cnt["pe"] + 1)
        if inc:
            self.cnt["pe"] += 1
            ins.then_inc(self.semobj["pe"], 1)
        self._register(tok, reads, [out_buf] if stop else [])
        return tok

    def group(self, out_buf, out_ap, pairs, reads=()):
        n = len(pairs)
        for i, (l, r) in enumerate(pairs):
            self.mm(out_ap, l, r, i == 0, i == n - 1, reads=reads, out_buf=out_buf)

    def transpose(self, out_buf, out_ap, in_ap, ident, reads=()):
        self._deps("pe", reads, [out_buf])
        ins = self.nc.tensor.transpose(out_ap, in_ap, ident)
        self.cnt["pe"] += 1
        ins.then_inc(self.semobj["pe"], 1)
        tok = ("pe", self.cnt["pe"])
        self._register(tok, reads, [out_buf])

    def dma(self, q, out, in_, reads=(), writes=()):
        j = self.rpos[q]
        self.rpos[q] = (j + 1) % self.R
        key = (q, j)
        prev = self.rcnt[key]
        if prev > 0:
            self._wait(q, (key, prev))
        self._deps(q, reads, writes)
        ins = self.engs[q].dma_start(out=out, in_=in_)
        ins.then_inc(self.semobj[key], 16)
        self.rcnt[key] = prev + 16
        tok = (key, prev + 16)
        self._register(tok, reads, writes)
        return tok

    def barrier(self):
        toks = [(e, self.cnt[e]) for e in ("pe", "act", "dve", "pool") if self.cnt[e] > 0]
        toks += [(k, v) for k, v in self.rcnt.items() if v > 0]
        for e in self.engs:
            for t in toks:
                self._wait(e, t)


class PS:
    def __init__(self, nc, es):
        self.t = [es.enter_context(nc.psum_tensor(f"ps{i}", [128, 512], F32)) for i in range(7)]
        self.b = [Buf(excl=True) for _ in range(7)]
        self.tb = es.enter_context(nc.psum_tensor("psb", [128, 1024], BF16))
        self.bb = Buf(excl=True)
        self.pos = 0

    def next(self):
        i = self.pos
        self.pos = (i + 1) % 7
        return self.t[i], self.b[i]


def build_program(debug=False, nlayers=DEPTH, stop_after=None):
    nc = bass.Bass("TRN2", target_bir_lowering=False)
    dt_in = lambda name, shape, dt=F32: nc.dram_tensor(name, shape, dt, kind="ExternalInput").ap()
    skind = "ExternalOutput" if debug else "Internal"
    dt_sc = lambda name, shape, dt: nc.dram_tensor(name, shape, dt, kind=skind).ap()

    xin = dt_in("xin", [D, T])
    vecs_d = dt_in("vecs", [128, NV])
    w_mod = dt_in("w_mod", [DEPTH, D, 6 * D])
    w_in = dt_in("w_in", [DEPTH, D, NIN])
    lru_wa = dt_in("lru_wa", [DEPTH, 2, 8, 64, 64])
    lru_wx = dt_in("lru_wx", [DEPTH, 2, 8, 64, 64])
    w_uq = dt_in("w_uq", [DEPTH, 384, 768])
    w_ukv = dt_in("w_ukv", [DEPTH, 256, 1024])
    w_oa = dt_in("w_oa", [DEPTH, 512, D])
    w_ob = dt_in("w_ob", [DEPTH, 512, D])
    w_oc = dt_in("w_oc", [DEPTH, 512, D])
    w_out = dt_in("w_out", [DEPTH, D, D])
    w_ff1 = dt_in("w_ff1", [DEPTH, D, DFF])
    w_ff3 = dt_in("w_ff3", [DEPTH, D, DFF])
    w_ff2 = dt_in("w_ff2", [DEPTH, DFF, D])
    ropeA = dt_in("ropeA", [2, 128, T])
    retA = dt_in("retA", [2, 128, T])
    retAk = dt_in("retAk", [2, 128, T])
    rdec = dt_in("rdec", [128, 4 * 8])
    qdec = dt_in("qdec", [4, 2, 64, 128])
    maskT = dt_in("maskT", [4, 128, 512])
    cmat = dt_in("cmat", [128, 256])
    outT = nc.dram_tensor("outT", [D, LL], F32, kind="ExternalOutput").ap()

    Hs = dt_sc("Hs", [D, T], BF16)
    Us = dt_sc("Us", [512, T], F32)
    Kns = dt_sc("Kns", [512, T], BF16)
    Krs = dt_sc("Krs", [32, T], BF16)
    Vs = dt_sc("Vs", [T, 512], BF16)
    RKf = dt_sc("RKf", [256, T], BF16)
    RKt = dt_sc("RKt", [T, 256], BF16)
    RVs = dt_sc("RVs", [T, 512], BF16)
    Qns = dt_sc("Qns", [512, T], BF16)
    Qrs = dt_sc("Qrs", [256, T], BF16)
    RQs = dt_sc("RQs", [256, T], BF16)
    RECs = dt_sc("RECs", [512, T], BF16)
    YBs = dt_sc("YBs", [512, T], BF16)
    RETs = dt_sc("RETs", [512, T], BF16)
    X1s = dt_sc("X1s", [D, T], F32)
    X2s = dt_sc("X2s", [D, T], F32)

    with ExitStack() as es:
        S = Sched(nc, es)
        P = PS(nc, es)
        _uid = [0]

        def sb(st, name, shape, dt):
            _uid[0] += 1
            return st.enter_context(nc.sbuf_tensor(f"sb{_uid[0]}_{name}", shape, dt))

        vecs = sb(es, "vecs", [128, NV], F32)
        vecsB = Buf()
        cm = sb(es, "cm", [128, 256], F32)
        cmB = Buf()
        onesb = sb(es, "onesb", [128, 128], BF16)
        identb = sb(es, "identb", [128, 128], BF16)
        constB = Buf()
        modv = sb(es, "modv", [128, 96], F32)
        modvB = Buf()
        amix = sb(es, "amix", [128, 16], F32)
        affn = sb(es, "affn", [128, 16], F32)
        cact = sb(es, "cact", [128, 16], F32)
        epsc = sb(es, "epsc", [128, 1], F32)
        S.dma("sp", vecs[:], vecs_d, writes=[vecsB])
        S.dma("sp", cm[:], cmat, writes=[cmB])
        S.op("dve", lambda e: e.tensor_copy(out=onesb[:], in_=cm[:, 0:128]), reads=[cmB], writes=[constB])
        S.op("dve", lambda e: e.tensor_copy(out=identb[:], in_=cm[:, 128:256]), reads=[cmB], writes=[constB])
        S.op("dve", lambda e: e.memset(epsc[:], EPS), writes=[constB])
        S.op("act", lambda e: e.activation(out=cact[:], in_=vecs[:, VC["c"]:VC["c"] + 16], func=AF.Silu),
             reads=[vecsB], writes=[constB])
        onesf = cm[:, 0:128]
        identf = cm[:, 128:256]
        S.barrier()

        def vcol(name, i, n=1):
            return vecs[:, VC[name] + i:VC[name] + i + n]

        def rms_rstd(st, src_sq_list, nparts_scale, N, tag):
            pt, pb = P.next()
            S.group(pb, pt[:, :N], [(onesb[:], a) for a in src_sq_list], reads=[constB] + tag["reads"])
            sq = tag["sq"]
            S.op("act", lambda e: e.activation(out=sq[:, :N], in_=pt[:, :N], func=AF.Sqrt,
                                               bias=epsc[:], scale=nparts_scale), reads=[pb], writes=[tag["sqB"]])
            S.op("dve", lambda e: e.reciprocal(out=sq[:, :N], in_=sq[:, :N]), reads=[], writes=[tag["sqB"]])
            return sq

        for l in range(nlayers):
            last = l == DEPTH - 1
            xsrc = xin if l == 0 else X2s

            w1st = ExitStack()
            wA = sb(w1st, "wA", [128, 8, 2208], BF16)
            wS = sb(w1st, "wS", [128, 8, 544], BF16)
            wkvN = sb(w1st, "wkvN", [128, 2, 512], BF16)
            wkvV = sb(w1st, "wkvV", [128, 2, 512], BF16)
            wqN = sb(w1st, "wqN", [128, 3, 512], BF16)
            wqR = sb(w1st, "wqR", [128, 3, 256], BF16)
            wqRs = sb(w1st, "wqRs", [128, 3, 256], BF16)
            wB = Buf()
            wl = w_in[l]
            S.dma("pool", wA[:], wl[:, 0:2208].rearrange("(k p) c -> p k c", p=128), writes=[wB])
            for k in range(8):
                rows = wl[k * 128:(k + 1) * 128, :]
                S.dma("pool", wS[:, k, 0:16], rows[:, 784:800], writes=[wB])
                S.dma("pool", wS[:, k, 16:32], rows[:, 768:784], writes=[wB])
                for (dst0, src0) in ((32, 800), (288, 1952)):
                    src = rows[:, src0:src0 + 256].rearrange("p (h two d) -> p h two d", two=2, d=32)
                    dst = wS[:, k, dst0:dst0 + 256].rearrange("p (h two d) -> p h two d", two=2, d=32)
                    S.dma("pool", dst[:, :, 0, :], src[:, :, 1, :], writes=[wB])
                    S.dma("pool", dst[:, :, 1, :], src[:, :, 0, :], writes=[wB])
            for k in range(2):
                src = w_ukv[l][k * 128:(k + 1) * 128, :].rearrange("p (h x) -> p h x", x=128)
                S.dma("pool", wkvN[:, k, :].rearrange("p (h x) -> p h x", x=64), src[:, :, 0:64], writes=[wB])
                S.dma("pool", wkvV[:, k, :].rearrange("p (h x) -> p h x", x=64), src[:, :, 64:128], writes=[wB])
            for k in range(3):
                src = w_uq[l][k * 128:(k + 1) * 128, :].rearrange("p (h x) -> p h x", x=96)
                S.dma("pool", wqN[:, k, :].rearrange("p (h x) -> p h x", x=64), src[:, :, 0:64], writes=[wB])
                S.dma("pool", wqR[:, k, :].rearrange("p (h x) -> p h x", x=32), src[:, :, 64:96], writes=[wB])
                S.dma("pool", wqRs[:, k, :].rearrange("p (h x) -> p h x", x=32)[:, :, 0:16], src[:, :, 80:96],
                      writes=[wB])
                S.dma("pool", wqRs[:, k, :].rearrange("p (h x) -> p h x", x=32)[:, :, 16:32], src[:, :, 64:80],
                      writes=[wB])

            with ExitStack() as st:
                wst = [sb(st, f"wm{i}", [128, 8, 768], F32) for i in range(2)]
                wstB = [Buf(), Buf()]
                pt, pb = P.next()
                for blk in range(8):
                    buf, bB = wst[blk % 2], wstB[blk % 2]
                    S.dma("sp", buf[:], w_mod[l][:, blk * 768:(blk + 1) * 768].rearrange("(k p) c -> p k c", p=128),
                          writes=[bB])
                    for fc in range(6):
                        f = blk * 6 + fc
                        S.group(pb, pt[:, 2 * f:2 * f + 2],
                                [(buf[:, k, fc * 128:(fc + 1) * 128], cact[:, 2 * k:2 * k + 2]) for k in range(8)],
                                reads=[bB, constB])
                bm = VC[f"bmod{l}"]
                for j in range(2):
                    S.op("dve", lambda e, j=j: e.tensor_tensor(out=modv[:, j:96:2], in0=pt[:, j:96:2],
                                                               in1=vecs[:, bm:bm + 48], op=ALU.add),
                         reads=[pb, vecsB], writes=[modvB])
                for j in range(2):
                    S.op("dve", lambda e, j=j: e.scalar_tensor_tensor(
                        out=amix[:, j:16:2], in0=modv[:, 16 + j:32:2], scalar=1.0, in1=vcol(f"gmix{l}", 0, 8),
                        op0=ALU.add, op1=ALU.mult), reads=[modvB], writes=[modvB])
                    S.op("dve", lambda e, j=j: e.scalar_tensor_tensor(
                        out=affn[:, j:16:2], in0=modv[:, 64 + j:80:2], scalar=1.0, in1=vcol(f"gffn{l}", 0, 8),
                        op0=ALU.add, op1=ALU.mult), reads=[modvB], writes=[modvB])
                S.barrier()
            if stop_after == (l, 0):
                break

            def norm_mod(st, xt, xB, N, jj, A, shift_base, ht, hB, wk):
                S.op("act", lambda e: e.activation(out=wk["xsq"][:, :, :N], in_=xt[:, :, :N], func=AF.Square),
                     reads=[xB], writes=[wk["xsqB"]])
                rs = rms_rstd(st, [wk["xsq"][:, k, :N] for k in range(8)], 1.0 / D, N,
                              {"reads": [wk["xsqB"]], "sq": wk["rstd"], "sqB": wk["rstdB"]})
                for k in range(8):
                    tmp, tB = wk["tmp"][k % 2], wk["tmpB"][k % 2]
                    S.op("dve", lambda e, k=k, tmp=tmp: e.scalar_tensor_tensor(
                        out=tmp[:, :N], in0=xt[:, k, :N], scalar=A[:, 2 * k + jj:2 * k + jj + 1], in1=rs[:, :N],
                        op0=ALU.mult, op1=ALU.mult), reads=[xB, wk["rstdB"], modvB], writes=[tB])
                    S.op("act", lambda e, k=k, tmp=tmp: e.activation(
                        out=ht[:, k, :N], in_=tmp[:, :N], func=AF.Identity,
                        bias=modv[:, shift_base + 2 * k + jj:shift_base + 2 * k + jj + 1], scale=1.0),
                         reads=[tB, modvB], writes=[hB])

            def mk_norm_wk(st, pfx):
                return {"xsq": sb(st, pfx + "xsq", [128, 8, 512], BF16), "xsqB": Buf(),
                        "rstd": sb(st, pfx + "rstd", [128, 512], F32), "rstdB": Buf(),
                        "tmp": [sb(st, pfx + f"tmp{i}", [128, 512], F32) for i in range(2)], "tmpB": [Buf(), Buf()]}

            with ExitStack() as st:
                xt = sb(st, "p1xt", [128, 8, 512], F32); xB = Buf()
                ht = sb(st, "p1ht", [128, 8, 512], BF16); hB = Buf()
                wk = mk_norm_wk(st, "p1")
                tabA = sb(st, "tabA", [128, 2, 512], F32)
                tabR = sb(st, "tabR", [128, 2, 512], F32)
                tabRk = sb(st, "tabRk", [128, 2, 512], F32)
                tabB = Buf()
                stf = [sb(st, f"stf{i}", [128, 4, 512], F32) for i in range(2)]; stfB = [Buf(), Buf()]
                stb = [sb(st, f"stb{i}", [128, 4, 512], BF16) for i in range(3)]; stbB = [Buf() for _ in range(3)]
                lat = sb(st, "p1lat", [128, 3, 512], F32); latB = Buf()
                latsq = sb(st, "p1latsq", [128, 3, 512], BF16); latsqB = Buf()
                latn = sb(st, "p1latn", [128, 3, 512], BF16); latnB = Buf()
                lrs = sb(st, "p1lrs", [128, 512], F32); lrsB = Buf()
                r1 = sb(st, "p1r1", [128, 512], F32); r1B = Buf()
                r2 = sb(st, "p1r2", [128, 512], F32); r2B = Buf()
                cnt = {"f": 0, "b": 0}

                def nstf():
                    i = cnt["f"] % 2; cnt["f"] += 1
                    return stf[i], stfB[i]

                def nstb():
                    i = cnt["b"] % 3; cnt["b"] += 1
                    return stb[i], stbB[i]

                def rope_fm(pairs_lhs, pairs_lhs_sw, rhs_list, M, N, tab, dst_ap, dstB, reads):
                    p1, b1 = P.next()
                    S.group(b1, p1[:M, :N], list(zip(pairs_lhs, rhs_list)), reads=reads)
                    p2, b2 = P.next()
                    S.group(b2, p2[:M, :N], list(zip(pairs_lhs_sw, rhs_list)), reads=reads)
                    S.op("dve", lambda e: e.tensor_tensor(out=r1[:M, :N], in0=p1[:M, :N], in1=tab[:M, 0, :N], op=ALU.mult),
                         reads=[b1, tabB], writes=[r1B])
                    S.op("dve", lambda e: e.tensor_tensor(out=r2[:M, :N], in0=p2[:M, :N], in1=tab[:M, 1, :N], op=ALU.mult),
                         reads=[b2, tabB], writes=[r2B])
                    S.op("dve", lambda e: e.tensor_tensor(out=dst_ap, in0=r1[:M, :N], in1=r2[:M, :N], op=ALU.add),
                         reads=[r1B, r2B], writes=[dstB])

                for ti, (t0, N) in enumerate(TILES):
                    jj = 1 if ti == 0 else 0
                    ns = N // 128
                    S.dma("sp", xt[:, :, :N], xsrc[:, t0:t0 + N].rearrange("(k p) t -> p k t", p=128), writes=[xB])
                    S.dma("sp", tabA[:, :, :N], ropeA[:, :, t0:t0 + N].rearrange("c p t -> p c t"), writes=[tabB])
                    S.dma("sp", tabR[:, :, :N], retA[:, :, t0:t0 + N].rearrange("c p t -> p c t"), writes=[tabB])
                    S.dma("sp", tabRk[:, :, :N], retAk[:, :, t0:t0 + N].rearrange("c p t -> p c t"), writes=[tabB])
                    norm_mod(st, xt, xB, N, jj, amix, 0, ht, hB, wk)
                    S.dma("pool", Hs[:, t0:t0 + N].rearrange("(k p) t -> p k t", p=128), ht[:, :, :N], reads=[hB])
                    hr = [ht[:, k, :N] for k in range(8)]
                    R = [hB, wB]
                    sf, sfB = nstf()
                    for m in range(4):
                        pt, pb = P.next()
                        S.group(pb, pt[:, :N], [(wA[:, k, m * 128:(m + 1) * 128], hr[k]) for k in range(8)], reads=R)
                        S.op("act", lambda e, m=m, pt=pt: e.activation(out=sf[:, m, :N], in_=pt[:, :N], func=AF.Copy),
                             reads=[pb], writes=[sfB])
                    S.dma("pool", Us[:, t0:t0 + N].rearrange("(k p) t -> p k t", p=128), sf[:, :, :N], reads=[sfB])

                    def latent_norm(c0, nchunk, gname):
                        for m in range(nchunk):
                            pt, pb = P.next()
                            S.group(pb, pt[:, :N], [(wA[:, k, c0 + m * 128:c0 + (m + 1) * 128], hr[k]) for k in range(8)],
                                    reads=R)
                            S.op("act", lambda e, m=m, pt=pt: e.activation(out=lat[:, m, :N], in_=pt[:, :N], func=AF.Copy),
                                 reads=[pb], writes=[latB])
                        S.op("act", lambda e: e.activation(out=latsq[:, :nchunk, :N], in_=lat[:, :nchunk, :N],
                                                           func=AF.Square), reads=[latB], writes=[latsqB])
                        rs = rms_rstd(st, [latsq[:, m, :N] for m in range(nchunk)], 1.0 / (128 * nchunk), N,
                                      {"reads": [latsqB], "sq": lrs, "sqB": lrsB})
                        for m in range(nchunk):
                            S.op("dve", lambda e, m=m: e.scalar_tensor_tensor(
                                out=latn[:, m, :N], in0=lat[:, m, :N], scalar=vcol(gname, m), in1=rs[:, :N],
                                op0=ALU.mult, op1=ALU.mult), reads=[latB, lrsB, vecsB], writes=[latnB])

                    latent_norm(512, 2, f"gkv{l}")
                    sbf, sbB = nstb()
                    for m in range(4):
                        pt, pb = P.next()
                        S.group(pb, pt[:, :N], [(wkvN[:, k, m * 128:(m + 1) * 128], latn[:, k, :N]) for k in range(2)],
                                reads=[latnB, wB])
                        S.op("act", lambda e, m=m, pt=pt: e.activation(out=sbf[:, m, :N], in_=pt[:, :N], func=AF.Copy),
                             reads=[pb], writes=[sbB])
                    S.dma("pool", Kns[:, t0:t0 + N].rearrange("(k p) t -> p k t", p=128), sbf[:, :, :N], reads=[sbB])
                    sbf, sbB = nstb()
                    for s in range(ns):
                        pt, pb = P.next()
                        S.group(pb, pt[:, :], [(latn[:, k, s * 128:(s + 1) * 128], wkvV[:, k, :]) for k in range(2)],
                                reads=[latnB, wB])
                        S.op("act", lambda e, s=s, pt=pt: e.activation(out=sbf[:, s, :], in_=pt[:, :], func=AF.Copy),
                             reads=[pb], writes=[sbB])
                    S.dma("pool", Vs[t0:t0 + N, :].rearrange("(s p) c -> p s c", p=128), sbf[:, :ns, :], reads=[sbB])

                    sbf, sbB = nstb()
                    rope_fm([wA[:, k, 768:800] for k in range(8)], [wS[:, k, 0:32] for k in range(8)], hr, 32, N,
                            tabA, sbf[:32, 0, :N], sbB, R)
                    S.dma("pool", Krs[:, t0:t0 + N], sbf[:32, 0, :N], reads=[sbB])

                    sbf, sbB = nstb()
                    for m in range(2):
                        rope_fm([wA[:, k, 800 + m * 128:800 + (m + 1) * 128] for k in range(8)],
                                [wS[:, k, 32 + m * 128:32 + (m + 1) * 128] for k in range(8)], hr, 128, N,
                                tabRk, sbf[:, m, :N], sbB, R)
                    S.dma("pool", RKf[:, t0:t0 + N].rearrange("(k p) t -> p k t", p=128), sbf[:, 0:2, :N], reads=[sbB])
                    sb2, sb2B = nstb()
                    for s in range(ns):
                        for m in range(2):
                            c0 = (s * 2 + m) * 128
                            S.transpose(P.bb, P.tb[:, c0:c0 + 128], sbf[:, m, s * 128:(s + 1) * 128], identb[:],
                                        reads=[sbB, constB])
                    S.op("act", lambda e: e.activation(out=sb2[:, 0:2, :].rearrange("p a b -> p (a b)")[:, :ns * 256],
                                                       in_=P.tb[:, :ns * 256], func=AF.Copy),
                         reads=[P.bb], writes=[sb2B])
                    S.dma("pool", RKt[t0:t0 + N, :].rearrange("(s p) c -> p s c", p=128),
                          sb2[:, 0:2, :].rearrange("p a b -> p (a b)")[:, :ns * 256].rearrange("p (s c) -> p s c", c=256),
                          reads=[sb2B])

                    sbf, sbB = nstb()
                    for s in range(ns):
                        pt, pb = P.next()
                        S.group(pb, pt[:, :], [(ht[:, k, s * 128:(s + 1) * 128], wA[:, k, 1056:1568]) for k in range(8)],
                                reads=R)
                        S.op("act", lambda e, s=s, pt=pt: e.activation(out=sbf[:, s, :], in_=pt[:, :], func=AF.Copy),
                             reads=[pb], writes=[sbB])
                    S.dma("pool", RVs[t0:t0 + N, :].rearrange("(s p) c -> p s c", p=128), sbf[:, :ns, :], reads=[sbB])

                    latent_norm(1568, 3, f"gq{l}")
                    qr = [latn[:, k, :N] for k in range(3)]
                    sbf, sbB = nstb()
                    for m in range(4):
                        pt, pb = P.next()
                        S.group(pb, pt[:, :N], [(wqN[:, k, m * 128:(m + 1) * 128], qr[k]) for k in range(3)],
                                reads=[latnB, wB])
                        S.op("act", lambda e, m=m, pt=pt: e.activation(out=sbf[:, m, :N], in_=pt[:, :N], func=AF.Copy),
                             reads=[pb], writes=[sbB])
                    S.dma("pool", Qns[:, t0:t0 + N].rearrange("(k p) t -> p k t", p=128), sbf[:, :, :N], reads=[sbB])
                    sbf, sbB = nstb()
                    for m in range(2):
                        rope_fm([wqR[:, k, m * 128:(m + 1) * 128] for k in range(3)],
                                [wqRs[:, k, m * 128:(m + 1) * 128] for k in range(3)], qr, 128, N,
                                tabA, sbf[:, m, :N], sbB, [latnB, wB])
                    S.dma("pool", Qrs[:, t0:t0 + N].rearrange("(k p) t -> p k t", p=128), sbf[:, 0:2, :N], reads=[sbB])

                    sbf, sbB = nstb()
                    for m in range(2):
                        rope_fm([wA[:, k, 1952 + m * 128:1952 + (m + 1) * 128] for k in range(8)],
                                [wS[:, k, 288 + m * 128:288 + (m + 1) * 128] for k in range(8)], hr, 128, N,
                                tabR, sbf[:, m, :N], sbB, R)
                    S.dma("pool", RQs[:, t0:t0 + N].rearrange("(k p) t -> p k t", p=128), sbf[:, 0:2, :N], reads=[sbB])
                S.barrier()
            w1st.close()
            if stop_after == (l, 1):
                break

            with ExitStack() as st:
                wblk = sb(st, "wblk", [128, 16, 128], BF16)
                wbB = Buf()
                S.op("dve", lambda e: e.memset(wblk[:], 0.0), writes=[wbB])
                for d in range(2):
                    for g, wsrc in enumerate((lru_wa, lru_wx)):
                        for c in range(4):
                            idx = (d * 2 + g) * 4 + c
                            S.dma("pool", wblk[0:64, idx, 0:64], wsrc[l, d, 2 * c], writes=[wbB])
                            S.dma("pool", wblk[64:128, idx, 64:128], wsrc[l, d, 2 * c + 1], writes=[wbB])
                coef = sb(st, "coef", [128, 16], F32)
                coefB = Buf()
                lamc = VC[f"lam{l}"]
                S.op("act", lambda e: e.activation(out=coef[:, 0:8], in_=vecs[:, lamc:lamc + 8], func=AF.Exp, scale=-1.0),
                     reads=[vecsB], writes=[coefB])
                S.op("act", lambda e: e.activation(out=coef[:, 0:8], in_=coef[:, 0:8], func=AF.Ln, bias=1.0, scale=1.0),
                     writes=[coefB])
                S.op("dve", lambda e: e.tensor_scalar(out=coef[:, 8:16], in0=coef[:, 0:8], scalar1=-16.0, scalar2=None,
                                                      op0=ALU.mult), writes=[coefB])
                S.op("dve", lambda e: e.tensor_scalar(out=coef[:, 0:8], in0=coef[:, 0:8], scalar1=-8.0, scalar2=None,
                                                      op0=ALU.mult), writes=[coefB])
                Uc = sb(st, "Uc", [128, T], F32); UcB = Buf()
                u = sb(st, "u", [128, T], F32); uB = Buf()
                ub = sb(st, "ub", [128, T], BF16); ubB = Buf()
                rr = sb(st, "rr", [128, T], F32); rrB = Buf()
                ii = sb(st, "ii", [128, T], F32); iiB = Buf()
                e2 = sb(st, "e2", [128, T], F32); e2B = Buf()
                hs = [sb(st, f"hs{d}", [128, T], F32) for d in range(2)]; hsB = [Buf(), Buf()]
                recb = sb(st, "recb", [128, T], BF16); recB = Buf()
                cw = VC[f"convw{l}"]
                for c in range(4):
                    S.dma("sp", Uc[:], Us[c * 128:(c + 1) * 128, :], writes=[UcB])
                    S.op("dve", lambda e, c=c: e.tensor_scalar(
                        out=u[:], in0=Uc[:], scalar1=vecs[:, cw + 2 * 4 + c:cw + 2 * 4 + c + 1],
                        scalar2=vcol(f"convb{l}", c), op0=ALU.mult, op1=ALU.add), reads=[UcB, vecsB], writes=[uB])
                    for (s0, s1) in ((0, LC), (LC, T)):
                        for tap, off in ((0, -2), (1, -1), (3, 1)):
                            lo = s0 + max(0, -off)
                            hi = s1 - max(0, off)
                            S.op("dve", lambda e, c=c, tap=tap, lo=lo, hi=hi, off=off: e.scalar_tensor_tensor(
                                out=u[:, lo:hi], in0=Uc[:, lo + off:hi + off],
                                scalar=vecs[:, cw + tap * 4 + c:cw + tap * 4 + c + 1], in1=u[:, lo:hi],
                                op0=ALU.mult, op1=ALU.add), reads=[UcB, vecsB], writes=[uB])
                    S.op("act", lambda e: e.activation(out=ub[:], in_=u[:], func=AF.Copy), reads=[uB], writes=[ubB])
                    for d in range(2):
                        for g, (dst, dB, bname) in enumerate(((rr, rrB, f"ba{l}"), (ii, iiB, f"bx{l}"))):
                            idx = (d * 2 + g) * 4 + c
                            for (t0, N) in TILES:
                                pt, pb = P.next()
                                S.group(pb, pt[:, :N], [(wblk[:, idx, :], ub[:, t0:t0 + N])], reads=[wbB, ubB])
                                S.op("act", lambda e, pt=pt, dst=dst, t0=t0, N=N, bname=bname, d=d, c=c: e.activation(
                                    out=dst[:, t0:t0 + N], in_=pt[:, :N], func=AF.Sigmoid,
                                    bias=vcol(bname, d * 4 + c), scale=1.0), reads=[pb, vecsB], writes=[dB])
                        S.op("act", lambda e, d=d, c=c: e.activation(out=e2[:], in_=rr[:], func=AF.Exp,
                                                                     scale=coef[:, 8 + d * 4 + c:8 + d * 4 + c + 1]),
                             reads=[rrB, coefB], writes=[e2B])
                        S.op("act", lambda e, d=d, c=c: e.activation(out=rr[:], in_=rr[:], func=AF.Exp,
                                                                     scale=coef[:, d * 4 + c:d * 4 + c + 1]),
                             reads=[coefB], writes=[rrB])
                        S.op("act", lambda e: e.activation(out=e2[:], in_=e2[:], func=AF.Sqrt, bias=1.0, scale=-1.0),
                             writes=[e2B])
                        S.op("dve", lambda e: e.tensor_tensor(out=ii[:], in0=ii[:], in1=u[:], op=ALU.mult),
                             reads=[uB], writes=[iiB])
                        S.op("dve", lambda e: e.tensor_tensor(out=ii[:], in0=ii[:], in1=e2[:], op=ALU.mult),
                             reads=[e2B], writes=[iiB])
                        h = hs[d]
                        if d == 0:
                            S.op("dve", lambda e, h=h: e.tensor_tensor_scan(out=h[:], data0=rr[:], data1=ii[:], initial=0.0,
                                                                           op0=ALU.mult, op1=ALU.add),
                                 reads=[rrB, iiB], writes=[hsB[d]])
                        else:
                            S.op("dve", lambda e, h=h: e.tensor_tensor_scan(
                                out=h[:, 0:LC][:, ::-1], data0=rr[:, 0:LC][:, ::-1],
                                data1=ii[:, 0:LC][:, ::-1], initial=0.0, op0=ALU.mult, op1=ALU.add),
                                 reads=[rrB, iiB], writes=[hsB[d]])
                            S.op("dve", lambda e, h=h: e.tensor_tensor_scan(
                                out=h[:, LC:T][:, ::-1], data0=rr[:, LC:T][:, ::-1], data1=ii[:, LC:T][:, ::-1],
                                initial=h[:, 0:1], op0=ALU.mult, op1=ALU.add),
                                 reads=[rrB, iiB], writes=[hsB[d]])
                    S.op("dve", lambda e: e.tensor_tensor(out=recb[:], in0=hs[0][:], in1=hs[1][:], op=ALU.add),
                         reads=[hsB[0], hsB[1]], writes=[recB])
                    S.dma("pool", RECs[c * 128:(c + 1) * 128, :], recb[:], reads=[recB])
                S.barrier()
            if stop_after == (l, 2):
                break

            with ExitStack() as st:
                rq = sb(st, "rq", [64, T], BF16)
                rkf = sb(st, "rkf", [64, T], BF16)
                ktm = sb(st, "ktm", [128, NCH, 64], BF16)
                vtm = sb(st, "vtm", [128, NCH, 128], BF16)
                ldB = Buf()
                qd = sb(st, "qd", [64, 2, 128], F32)
                mk = sb(st, "mk", [128, 512], F32)
                rdc = sb(st, "rdc", [128, 32], F32)
                cB = Buf()
                S.dma("sp", rdc[:], rdec, writes=[cB])
                qfr = [sb(st, f"qfr{d}", [64, T], BF16) for d in range(2)]; qfrB = [Buf(), Buf()]
                kfr = [sb(st, f"kfr{d}", [128, NCH, 64], BF16) for d in range(2)]; kfrB = [Buf(), Buf()]
                Rst = [sb(st, f"Rst{d}", [64, NCH, 128], BF16) for d in range(2)]; RstB = [Buf(), Buf()]
                hist = [sb(st, f"hist{d}", [64, NCH + 1, 128], F32) for d in range(2)]; histB = [Buf(), Buf()]
                step_of = [{c: i for i, c in enumerate(orders_)} for orders_ in ([list(range(NCH)), [1, 0] + list(range(NCH - 1, 1, -1))])]
                sm = [sb(st, f"sm{i}", [128, 512], BF16) for i in range(2)]; smB = [Buf(), Buf()]
                Oh = sb(st, "Oh", [128, T], F32); OhB = Buf()
                osq = sb(st, "osq", [128, 512], F32); osqB = Buf()
                msq = sb(st, "msq", [128, 512], F32); msqB = Buf()
                var = sb(st, "var", [128, 512], F32); varB = Buf()
                cen = sb(st, "cen", [128, 512], F32); cenB = Buf()
                rn = [sb(st, f"rn{i}", [128, 512], BF16) for i in range(2)]; rnB = [Buf(), Buf()]
                orders = [list(range(NCH)), [1, 0] + list(range(NCH - 1, 1, -1))]
                smc = 0
                rnc = 0
                for h in range(4):
                    S.dma("sp", rq[:], RQs[h * 64:(h + 1) * 64, :], writes=[ldB])
                    S.dma("sp", rkf[:], RKf[h * 64:(h + 1) * 64, :], writes=[ldB])
                    S.dma("sp", ktm[:], RKt[:, h * 64:(h + 1) * 64].rearrange("(c p) d -> p c d", p=128), writes=[ldB])
                    S.dma("sp", vtm[:], RVs[:, h * 128:(h + 1) * 128].rearrange("(c p) d -> p c d", p=128), writes=[ldB])
                    S.dma("sp", qd[:], qdec[h].rearrange("d p j -> p d j"), writes=[cB])
                    S.dma("sp", mk[:], maskT[h], writes=[cB])
                    for d in range(2):
                        S.op("dve", lambda e, d=d: e.tensor_tensor(
                            out=qfr[d][:].rearrange("p (c j) -> p c j", j=128),
                            in0=rq[:].rearrange("p (c j) -> p c j", j=128),
                            in1=qd[:, d, :].unsqueeze(1).to_broadcast([64, NCH, 128]), op=ALU.mult),
                             reads=[ldB, cB], writes=[qfrB[d]])
                        S.op("dve", lambda e, d=d, h=h: e.tensor_scalar(
                            out=kfr[d][:], in0=ktm[:], scalar1=rdc[:, h * 8 + d:h * 8 + d + 1], scalar2=None,
                            op0=ALU.mult), reads=[ldB, cB], writes=[kfrB[d]])
                    for d in range(2):
                        S.op("dve", lambda e, d=d: e.memset(hist[d][:, 0, :], 0.0), writes=[histB[d]])
                    for s_ in range(NCH):
                        for d in range(2):
                            c = orders[d][s_]
                            pt, pb = P.next()
                            S.group(pb, pt[0:64, 0:128], [(kfr[d][:, c, :], vtm[:, c, :])], reads=[kfrB[d], ldB])
                            S.op("dve", lambda e, d=d, pt=pt, h=h, s_=s_: e.scalar_tensor_tensor(
                                out=hist[d][:, s_ + 1, :], in0=hist[d][:, s_, :], scalar=rdc[0:64, h * 8 + 2 + d:h * 8 + 3 + d],
                                in1=pt[0:64, 0:128], op0=ALU.mult, op1=ALU.add), reads=[pb, cB], writes=[histB[d]])
                    for d in range(2):
                        S.op("act", lambda e, d=d: e.activation(out=Rst[d][:], in_=hist[d][:, 0:NCH, :], func=AF.Copy),
                             reads=[histB[d]], writes=[RstB[d]])
                    for g in range((NCH + 3) // 4):
                        cs = list(range(g * 4, min(NCH, g * 4 + 4)))
                        W = len(cs) * 128
                        pS, pSb = P.next()
                        for ci, c in enumerate(cs):
                            S.group(pSb, pS[:, ci * 128:(ci + 1) * 128],
                                    [(rkf[:, c * 128:(c + 1) * 128], rq[:, c * 128:(c + 1) * 128])], reads=[ldB])
                        sm_, smB_ = sm[smc % 2], smB[smc % 2]
                        smc += 1
                        S.op("dve", lambda e, pS=pS, sm_=sm_, W=W: e.tensor_tensor(out=sm_[:, :W], in0=pS[:, :W],
                                                                                  in1=mk[:, :W], op=ALU.mult),
                             reads=[pSb, cB], writes=[smB_])
                        po, pob = P.next()
                        for ci, c in enumerate(cs):
                            oap = po[:, ci * 128:(ci + 1) * 128]
                            S.mm(oap, vtm[:, c, :], sm_[:, ci * 128:(ci + 1) * 128], True, False,
                                 reads=[ldB, smB_], out_buf=pob)
                            S.mm(oap, Rst[0][:, step_of[0][c], :], qfr[0][:, c * 128:(c + 1) * 128], False, False,
                                 reads=[RstB[0], qfrB[0]], out_buf=pob)
                            S.mm(oap, Rst[1][:, step_of[1][c], :], qfr[1][:, c * 128:(c + 1) * 128], False, True,
                                 reads=[RstB[1], qfrB[1]], out_buf=pob)
                        S.op("act", lambda e, po=po, g=g, W=W: e.activation(out=Oh[:, g * 512:g * 512 + W], in_=po[:, :W],
                                                                            func=AF.Copy), reads=[pob], writes=[OhB])
                    for (t0, N) in TILES:
                        pm, pmb = P.next()
                        S.group(pmb, pm[:, :N], [(onesf, Oh[:, t0:t0 + N])], reads=[OhB, cmB])
                        S.op("act", lambda e, t0=t0, N=N: e.activation(out=osq[:, :N], in_=Oh[:, t0:t0 + N], func=AF.Square),
                             reads=[OhB], writes=[osqB])
                        pe2, pe2b = P.next()
                        S.group(pe2b, pe2[:, :N], [(onesf, osq[:, :N])], reads=[osqB, cmB])
                        S.op("act", lambda e, pm=pm, N=N: e.activation(out=msq[:, :N], in_=pm[:, :N], func=AF.Square,
                                                                       scale=1.0 / 128), reads=[pmb], writes=[msqB])
                        S.op("dve", lambda e, pe2=pe2, N=N: e.scalar_tensor_tensor(
                            out=var[:, :N], in0=pe2[:, :N], scalar=1.0 / 128, in1=msq[:, :N], op0=ALU.mult,
                            op1=ALU.subtract), reads=[pe2b, msqB], writes=[varB])
                        S.op("act", lambda e, N=N: e.activation(out=var[:, :N], in_=var[:, :N], func=AF.Sqrt, bias=epsc[:],
                                                                scale=1.0), reads=[constB], writes=[varB])
                        S.op("dve", lambda e, N=N: e.reciprocal(out=var[:, :N], in_=var[:, :N]), writes=[varB])
                        S.op("dve", lambda e, pm=pm, t0=t0, N=N: e.scalar_tensor_tensor(
                            out=cen[:, :N], in0=pm[:, :N], scalar=-1.0 / 128, in1=Oh[:, t0:t0 + N], op0=ALU.mult,
                            op1=ALU.add), reads=[pmb, OhB], writes=[cenB])
                        rn_, rnB_ = rn[rnc % 2], rnB[rnc % 2]
                        rnc += 1
                        S.op("dve", lambda e, rn_=rn_, N=N: e.tensor_tensor(out=rn_[:, :N], in0=cen[:, :N], in1=var[:, :N],
                                                                           op=ALU.mult), reads=[cenB, varB], writes=[rnB_])
                        S.dma("pool", RETs[h * 128:(h + 1) * 128, t0:t0 + N], rn_[:, :N], reads=[rnB_])
                S.barrier()
            if stop_after == (l, 4):
                break

            w5st = ExitStack()
            wG = sb(w5st, "wG", [128, 8, 4096], BF16)
            wo = [sb(w5st, f"wo{i}", [128, 4, 1024], BF16) for i in range(3)]
            wout = sb(w5st, "wout", [128, 8, 1024], BF16)
            w5B = Buf()
            for k in range(8):
                S.dma("pool", wG[:, k, :], w_in[l][k * 128:(k + 1) * 128, 2208:NIN], writes=[w5B])
                S.dma("pool", wout[:, k, :], w_out[l][k * 128:(k + 1) * 128, :], writes=[w5B])
            for i, wsrc in enumerate((w_oa, w_ob, w_oc)):
                S.dma("pool", wo[i][:], wsrc[l].rearrange("(k p) c -> p k c", p=128), writes=[w5B])
            with ExitStack() as st:
                Kt = [sb(st, f"Kt{i}", [96, T], BF16) for i in range(2)]
                Qt = [sb(st, f"Qt{i}", [96, T], BF16) for i in range(2)]
                Vt = [sb(st, f"Vt{i}", [128, NCH, 128], BF16) for i in range(2)]
                hdB = [Buf(), Buf()]
                NPT = 6
                Pt = [sb(st, f"Pt{i}", [128, 512], BF16) for i in range(NPT)]; PtB = [Buf() for _ in range(NPT)]
                rd = [sb(st, f"rd{i}", [128, 512], F32) for i in range(2)]; rdB = [Buf(), Buf()]
                osb = [sb(st, f"osb{i}", [64, 512], F32) for i in range(2)]; osbB = [Buf(), Buf()]
                ybs = [sb(st, f"ybs{i}", [64, 512], BF16) for i in range(2)]; ybsB = [Buf(), Buf()]
                for i in range(2):
                    S.op("dve", lambda e, i=i: e.memset(Vt[i][:, :, 64:128], 1.0), writes=[hdB[i]])
                scale = 96.0 ** -0.5
                NSB = 4
                Sbk = [(P.t[i], P.b[i]) for i in range(NSB)]
                Obk = [(P.t[4], P.b[4]), (P.t[5], P.b[5])]
                Rbk = (P.t[6], P.b[6])
                LA = 3
                qtiles = TILES[1:] + ([TILES[0]] if not last else [])
                items = []
                qi = 0
                for h in range(8):
                    for (q0, N) in qtiles:
                        keys = list(range(NCH)) if q0 >= LC else list(range(LC // 128))
                        for ji, j in enumerate(keys):
                            items.append((h, q0, N, j, ji == 0, ji == len(keys) - 1, qi))
                        qi += 1
                loaded = set()

                def load_head(h):
                    b = h % 2
                    S.dma("sp", Kt[b][0:64, :], Kns[h * 64:(h + 1) * 64, :], writes=[hdB[b]])
                    S.dma("sp", Kt[b][64:96, :], Krs[:, :], writes=[hdB[b]])
                    S.dma("sp", Qt[b][0:64, :], Qns[h * 64:(h + 1) * 64, :], writes=[hdB[b]])
                    S.dma("sp", Qt[b][64:96, :], Qrs[h * 32:(h + 1) * 32, :], writes=[hdB[b]])
                    S.dma("sp", Vt[b][:, :, 0:64], Vs[:, h * 64:(h + 1) * 64].rearrange("(j p) d -> p j d", p=128),
                          writes=[hdB[b]])

                def issue_qk(idx):
                    h, q0, N, j, first, lastk, qi = items[idx]
                    b = h % 2
                    if h not in loaded:
                        loaded.add(h)
                        load_head(h)
                    pS, pSb = Sbk[idx % NSB]
                    S.group(pSb, pS[:, :N], [(Kt[b][:, j * 128:(j + 1) * 128], Qt[b][:, q0:q0 + N])], reads=[hdB[b]])

                pending = []

                def epilogue_pe(h, q0, N, qi):
                    po, pob = Obk[qi % 2]
                    pr, prb = Rbk
                    r_, rB_ = rd[qi % 2], rdB[qi % 2]
                    o_, oB_ = osb[qi % 2], osbB[qi % 2]
                    S.group(prb, pr[0:64, :N], [(cm[64:128, 192:256], r_[64:128, :N])], reads=[rB_, cmB])
                    yb_, ybB_ = ybs[qi % 2], ybsB[qi % 2]
                    S.op("dve", lambda e: e.tensor_tensor(out=yb_[:, :N], in0=o_[:, :N], in1=pr[0:64, :N], op=ALU.mult),
                         reads=[oB_, prb], writes=[ybB_])
                    S.dma("pool", YBs[h * 64:(h + 1) * 64, q0:q0 + N], yb_[:, :N], reads=[ybB_])

                for idx in range(min(LA, len(items))):
                    issue_qk(idx)
                for idx in range(len(items)):
                    h, q0, N, j, first, lastk, qi = items[idx]
                    b = h % 2
                    if idx + LA < len(items):
                        issue_qk(idx + LA)
                    pS, pSb = Sbk[idx % NSB]
                    pt_, ptB_ = Pt[idx % NPT], PtB[idx % NPT]
                    S.op("act", lambda e, pS=pS, pt_=pt_, N=N: e.activation(out=pt_[:, :N], in_=pS[:, :N], func=AF.Exp,
                                                                            scale=scale), reads=[pSb], writes=[ptB_])
                    po, pob = Obk[qi % 2]
                    S.mm(po[:, :N], Vt[b][:, j, :], pt_[:, :N], first, lastk, reads=[ptB_, hdB[b]], out_buf=pob)
                    pending = [(c - 1, a) for (c, a) in pending]
                    for (c, a) in pending:
                        if c <= 0:
                            epilogue_pe(*a)
                    pending = [(c, a) for (c, a) in pending if c > 0]
                    if lastk:
                        r_, rB_ = rd[qi % 2], rdB[qi % 2]
                        o_, oB_ = osb[qi % 2], osbB[qi % 2]
                        S.op("dve", lambda e, r_=r_, po=po, N=N: e.reciprocal(out=r_[64:128, :N], in_=po[64:128, :N]),
                             reads=[pob], writes=[rB_])
                        S.op("act", lambda e, o_=o_, po=po, N=N: e.activation(out=o_[:, :N], in_=po[0:64, :N], func=AF.Copy),
                             reads=[pob], writes=[oB_])
                        pending.append((2, (h, q0, N, qi)))
                for (c, a) in pending:
                    epilogue_pe(*a)
                S.barrier()
            if stop_after == (l, 3):
                break

            with ExitStack() as st:
                ht = sb(st, "aht", [128, 8, 512], BF16)
                xt = sb(st, "axt", [128, 8, 512], F32)
                yin = [sb(st, f"yin{i}", [128, 4, 512], BF16) for i in range(3)]
                inB = Buf()
                gt = sb(st, "agt", [128, 512], F32); gtB = Buf()
                sg = [sb(st, f"sg{i}", [128, 512], F32) for i in range(3)]; sgB = [Buf() for _ in range(3)]
                macc = sb(st, "macc", [128, 512], F32); maccB = Buf()
                mt2 = sb(st, "mt2", [128, 512], F32); mt2B = Buf()
                mb = sb(st, "amb", [128, 8, 512], BF16); mbB = Buf()
                for ti, (t0, N) in enumerate(TILES):
                    if last and ti == 0:
                        continue
                    jj = 1 if ti == 0 else 0
                    S.dma("sp", ht[:, :, :N], Hs[:, t0:t0 + N].rearrange("(k p) t -> p k t", p=128), writes=[inB])
                    S.dma("sp", xt[:, :, :N], xsrc[:, t0:t0 + N].rearrange("(k p) t -> p k t", p=128), writes=[inB])
                    for i, src in enumerate((RECs, YBs, RETs)):
                        S.dma("sp", yin[i][:, :, :N], src[:, t0:t0 + N].rearrange("(k p) t -> p k t", p=128), writes=[inB])
                    hr = [ht[:, k, :N] for k in range(8)]
                    for gi, (func, yi) in enumerate(((AF.Gelu, 0), (AF.Silu, 2))):
                        for m in range(4):
                            c0 = gi * 512 + m * 128
                            pt, pb = P.next()
                            S.group(pb, pt[:, :N], [(wG[:, k, c0:c0 + 128], hr[k]) for k in range(8)], reads=[inB, w5B])
                            S.op("act", lambda e, pt=pt, func=func: e.activation(out=gt[:, :N], in_=pt[:, :N], func=func),
                                 reads=[pb], writes=[gtB])
                            S.op("dve", lambda e, yi=yi, m=m: e.tensor_tensor(out=yin[yi][:, m, :N], in0=gt[:, :N],
                                                                             in1=yin[yi][:, m, :N], op=ALU.mult),
                                 reads=[gtB], writes=[inB])
                    for mo in range(8):
                        for br in range(3):
                            c0 = 1024 + br * 1024 + mo * 128
                            pt, pb = P.next()
                            S.group(pb, pt[:, :N], [(wG[:, k, c0:c0 + 128], hr[k]) for k in range(8)], reads=[inB, w5B])
                            S.op("act", lambda e, pt=pt, br=br: e.activation(out=sg[br][:, :N], in_=pt[:, :N],
                                                                             func=AF.Sigmoid), reads=[pb], writes=[sgB[br]])
                        for br in range(3):
                            pt, pb = P.next()
                            S.group(pb, pt[:, :N], [(wo[br][:, k, mo * 128:(mo + 1) * 128], yin[br][:, k, :N])
                                                    for k in range(4)], reads=[inB, w5B])
                            if br == 0:
                                S.op("dve", lambda e, pt=pt: e.tensor_tensor(out=macc[:, :N], in0=pt[:, :N], in1=sg[0][:, :N],
                                                                             op=ALU.mult), reads=[pb, sgB[0]], writes=[maccB])
                            else:
                                S.op("dve", lambda e, pt=pt, br=br: e.tensor_tensor(out=mt2[:, :N], in0=pt[:, :N],
                                                                                    in1=sg[br][:, :N], op=ALU.mult),
                                     reads=[pb, sgB[br]], writes=[mt2B])
                                if br == 1:
                                    S.op("dve", lambda e: e.tensor_tensor(out=macc[:, :N], in0=macc[:, :N], in1=mt2[:, :N],
                                                                          op=ALU.add), reads=[mt2B], writes=[maccB])
                                else:
                                    S.op("dve", lambda e, mo=mo: e.tensor_tensor(out=mb[:, mo, :N], in0=macc[:, :N],
                                                                                 in1=mt2[:, :N], op=ALU.add),
                                         reads=[mt2B, maccB], writes=[mbB])
                    for mo in range(8):
                        pt, pb = P.next()
                        S.group(pb, pt[:, :N], [(wout[:, k, mo * 128:(mo + 1) * 128], mb[:, k, :N]) for k in range(8)],
                                reads=[mbB, w5B])
                        S.op("dve", lambda e, pt=pt, mo=mo: e.scalar_tensor_tensor(
                            out=xt[:, mo, :N], in0=pt[:, :N], scalar=modv[:, 32 + 2 * mo + jj:32 + 2 * mo + jj + 1],
                            in1=xt[:, mo, :N], op0=ALU.mult, op1=ALU.add), reads=[pb, modvB], writes=[inB])
                    S.dma("pool", X1s[:, t0:t0 + N].rearrange("(k p) t -> p k t", p=128), xt[:, :, :N], reads=[inB])
                S.barrier()
            w5st.close()
            if stop_after == (l, 5):
                break

            with ExitStack() as st:
                w1 = sb(st, "w1", [128, 8, DFF], BF16)
                w3 = sb(st, "w3", [128, 8, DFF], BF16)
                w2 = sb(st, "w2", [128, 22, D], BF16)
                w1B, w3B, w2B = Buf(), Buf(), Buf()
                S.dma("pool", w1[:], w_ff1[l].rearrange("(k p) c -> p k c", p=128), writes=[w1B])
                S.dma("pool", w3[:], w_ff3[l].rearrange("(k p) c -> p k c", p=128), writes=[w3B])
                for k0 in (0, 11):
                    S.dma("pool", w2[:, k0:k0 + 11, :], w_ff2[l][k0 * 128:(k0 + 11) * 128, :].rearrange("(k p) c -> p k c", p=128),
                          writes=[w2B])
                xt = sb(st, "bxt", [128, 8, 512], F32); xB = Buf()
                hf = sb(st, "bhf", [128, 8, 512], BF16); hfB = Buf()
                wk = mk_norm_wk(st, "b")
                s1 = sb(st, "bs1", [128, 512], F32); s1B = Buf()
                gg = sb(st, "bgg", [128, 22, 512], BF16); ggB = Buf()
                for ti, (t0, N) in enumerate(TILES):
                    if last and ti == 0:
                        continue
                    jj = 1 if ti == 0 else 0
                    S.dma("sp", xt[:, :, :N], X1s[:, t0:t0 + N].rearrange("(k p) t -> p k t", p=128), writes=[xB])
                    norm_mod(st, xt, xB, N, jj, affn, 48, hf, hfB, wk)
                    hr = [hf[:, k, :N] for k in range(8)]
                    for m in range(22):
                        pt, pb = P.next()
                        S.group(pb, pt[:, :N], [(w1[:, k, m * 128:(m + 1) * 128], hr[k]) for k in range(8)], reads=[hfB, w1B])
                        S.op("act", lambda e, pt=pt: e.activation(out=s1[:, :N], in_=pt[:, :N], func=AF.Silu),
                             reads=[pb], writes=[s1B])
                        pt3, pb3 = P.next()
                        S.group(pb3, pt3[:, :N], [(w3[:, k, m * 128:(m + 1) * 128], hr[k]) for k in range(8)], reads=[hfB, w3B])
                        S.op("dve", lambda e, pt3=pt3, m=m: e.tensor_tensor(out=gg[:, m, :N], in0=pt3[:, :N], in1=s1[:, :N],
                                                                           op=ALU.mult), reads=[pb3, s1B], writes=[ggB])
                    for mo in range(8):
                        pt, pb = P.next()
                        S.group(pb, pt[:, :N], [(w2[:, k, mo * 128:(mo + 1) * 128], gg[:, k, :N]) for k in range(22)],
                                reads=[ggB, w2B])
                        S.op("dve", lambda e, pt=pt, mo=mo: e.scalar_tensor_tensor(
                            out=xt[:, mo, :N], in0=pt[:, :N], scalar=modv[:, 80 + 2 * mo + jj:80 + 2 * mo + jj + 1],
                            in1=xt[:, mo, :N], op0=ALU.mult, op1=ALU.add), reads=[pb, modvB], writes=[xB])
                    if not last:
                        S.dma("pool", X2s[:, t0:t0 + N].rearrange("(k p) t -> p k t", p=128), xt[:, :, :N], reads=[xB])
                    else:
                        S.op("act", lambda e: e.activation(out=wk["xsq"][:, :, :N], in_=xt[:, :, :N], func=AF.Square),
                             reads=[xB], writes=[wk["xsqB"]])
                        rs = rms_rstd(st, [wk["xsq"][:, k, :N] for k in range(8)], 1.0 / D, N,
                                      {"reads": [wk["xsqB"]], "sq": wk["rstd"], "sqB": wk["rstdB"]})
                        for k in range(8):
                            S.op("dve", lambda e, k=k: e.scalar_tensor_tensor(
                                out=xt[:, k, :N], in0=xt[:, k, :N], scalar=vcol("gfin", k), in1=rs[:, :N],
                                op0=ALU.mult, op1=ALU.mult), reads=[wk["rstdB"], vecsB], writes=[xB])
                        S.dma("pool", outT[:, t0 - LC:t0 - LC + N].rearrange("(k p) t -> p k t", p=128), xt[:, :, :N],
                              reads=[xB])
                S.barrier()
            if stop_after == (l, 6):
                break
        S.barrier()
    return nc


def _fm(v):
    return np.ascontiguousarray(np.asarray(v, np.float32).reshape(-1, 128).T)


def _const_tables():
    f32 = np.float32
    rows = LL // 64
    row = np.repeat(np.arange(rows, dtype=f32), 64)
    col = np.tile(np.arange(64, dtype=f32), rows)
    inv = np.power(f32(10000.0), -np.arange(8, dtype=f32) / f32(8)).astype(f32)
    ang = np.concatenate([row[:, None] * inv, col[:, None] * inv], axis=-1).astype(f32)
    cos = np.cos(ang).astype(f32); sin = np.sin(ang).astype(f32)
    C = np.ones((32, T), f32); Sg = np.zeros((32, T), f32)
    C[0:16, LC:] = cos.T; C[16:32, LC:] = cos.T
    Sg[0:16, LC:] = -sin.T; Sg[16:32, LC:] = sin.T
    ropeA = np.stack([np.tile(C, (4, 1)), np.tile(Sg, (4, 1))]).astype(f32)
    theta = (1.0 / np.power(f32(10000.0), np.linspace(0.0, 1.0, 32, dtype=f32))).astype(f32)
    pos = np.arange(T, dtype=f32)
    ang = (pos[:, None] * theta).astype(f32)
    cos = np.cos(ang).astype(f32); sin = np.sin(ang).astype(f32)
    C = np.concatenate([cos.T, cos.T], 0); Sg = np.concatenate([-sin.T, sin.T], 0)
    retA = np.stack([np.tile(C, (2, 1)), np.tile(Sg, (2, 1))]).astype(f32)
    retAk = (retA * f32(0.125)).astype(f32)
    hh = np.arange(4, dtype=f32)
    lg = [np.log1p(-np.exp2(-5.0 - hh)).astype(f32), np.log1p(-np.exp2(-5.5 - hh)).astype(f32)]
    p = np.arange(128, dtype=f32)
    rdec = np.zeros((128, 32), f32)
    qdec = np.zeros((4, 2, 64, 128), f32)
    maskT = np.zeros((4, 128, 512), f32)
    for h in range(4):
        gf, gr = lg[0][h], lg[1][h]
        rdec[:, h * 8 + 0] = np.exp(gf * (127.0 - p))
        rdec[:, h * 8 + 1] = np.exp(gr * p)
        rdec[:, h * 8 + 2] = np.exp(gf * 128.0)
        rdec[:, h * 8 + 3] = np.exp(gr * 128.0)
        qdec[h, 0] = np.exp(gf * (p + 1.0))[None, :]
        qdec[h, 1] = np.exp(gr * (128.0 - p))[None, :]
        jj, ii = np.meshgrid(p, p, indexing="ij")
        m = np.where(ii > jj, np.exp(gf * np.maximum(ii - jj, 0)), np.where(ii < jj, np.exp(gr * np.maximum(jj - ii, 0)), 2.0))
        maskT[h] = np.tile(m.astype(f32), (1, 4))
    cmat = np.concatenate([np.ones((128, 128), f32), np.eye(128, dtype=f32)], 1)
    return dict(ropeA=ropeA, retA=retA, retAk=retAk, rdec=rdec.astype(f32), qdec=qdec.astype(f32),
                maskT=maskT.astype(f32), cmat=cmat)


def _pack_vecs(inp, b):
    v = np.zeros((128, NV), np.float32)
    cc = np.stack([_fm(inp["c"][b]), _fm(inp["c_ctx"])], -1).reshape(128, 16)
    v[:, VC["c"]:VC["c"] + 16] = cc
    for l in range(DEPTH):
        v[:, VC[f"gmix{l}"]:VC[f"gmix{l}"] + 8] = _fm(inp["g_mix"][l])
        v[:, VC[f"gffn{l}"]:VC[f"gffn{l}"] + 8] = _fm(inp["g_ffn"][l])
        v[:, VC[f"bmod{l}"]:VC[f"bmod{l}"] + 48] = _fm(inp["b_mod"][l])
        for tap in range(4):
            v[:, VC[f"convw{l}"] + tap * 4:VC[f"convw{l}"] + tap * 4 + 4] = _fm(inp["conv_w"][l, tap])
        v[:, VC[f"convb{l}"]:VC[f"convb{l}"] + 4] = _fm(inp["conv_b"][l])
        for d in range(2):
            v[:, VC[f"ba{l}"] + d * 4:VC[f"ba{l}"] + d * 4 + 4] = _fm(inp["lru_ba"][l, d])
            v[:, VC[f"bx{l}"] + d * 4:VC[f"bx{l}"] + d * 4 + 4] = _fm(inp["lru_bx"][l, d])
            v[:, VC[f"lam{l}"] + d * 4:VC[f"lam{l}"] + d * 4 + 4] = _fm(inp["lru_lam"][l, d])
        v[:, VC[f"gq{l}"]:VC[f"gq{l}"] + 3] = _fm(inp["g_q"][l])
        v[:, VC[f"gkv{l}"]:VC[f"gkv{l}"] + 2] = _fm(inp["g_kv"][l])
    v[:, VC["gfin"]:VC["gfin"] + 8] = _fm(inp["g_final"])
    return v


WKEYS = ["w_mod", "w_in", "lru_wa", "lru_wx", "w_uq", "w_ukv", "w_oa", "w_ob", "w_oc", "w_out", "w_ff1", "w_ff3", "w_ff2"]


def make_in_maps(inp, cores=range(8)):
    inp = {k: np.asarray(v) for k, v in inp.items()}
    consts = _const_tables()
    shared = {k: np.ascontiguousarray(inp[k], dtype=np.float32) for k in WKEYS}
    shared.update(consts)
    maps = []
    for b in cores:
        m = dict(shared)
        m["xin"] = np.ascontiguousarray(np.concatenate([inp["ctx"][b].T, inp["x"][b].T], axis=1), dtype=np.float32)
        m["vecs"] = _pack_vecs(inp, b)
        maps.append(m)
    return maps


_NC_CACHE = {}


def kernel(**inputs):
    if "nc" not in _NC_CACHE:
        _NC_CACHE["nc"] = build_program()
    nc = _NC_CACHE["nc"]
    in_maps = make_in_maps(inputs)
    res = run_bass_kernel_spmd(nc, in_maps, core_ids=list(range(8)))
    out = np.stack([np.ascontiguousarray(r["outT"].T) for r in res.results], axis=0)
    return out.astype(np.float32)
```

```python
import numpy as np
from contextlib import ExitStack
import concourse.bass as bass
import concourse.mybir as mybir
from concourse.bass_utils import run_bass_kernel_spmd

F32 = mybir.dt.float32
BF16 = mybir.dt.bfloat16
AF = mybir.ActivationFunctionType
ALU = mybir.AluOpType

D = 1024
LC = 256
LL = 4096
T = LC + LL
DEPTH = 2
NIN = 6304
DFF = 2816
EPS = 1e-6
TILES = [(0, 256)] + [(256 + 512 * i, 512) for i in range(8)]
NCH = T // 128

VC = {}
_off = 0
def _reg(name, n):
    global _off
    VC[name] = _off
    _off += n
_reg("c", 16)
for _l in range(DEPTH):
    _reg(f"gmix{_l}", 8); _reg(f"gffn{_l}", 8); _reg(f"bmod{_l}", 48)
    _reg(f"convw{_l}", 16)
    _reg(f"convb{_l}", 4)
    _reg(f"ba{_l}", 8); _reg(f"bx{_l}", 8); _reg(f"lam{_l}", 8)
    _reg(f"gq{_l}", 3); _reg(f"gkv{_l}", 2)
_reg("gfin", 8)
NV = _off


class Buf:
    __slots__ = ("w", "rs", "excl")

    def __init__(self, excl=False):
        self.w = None
        self.rs = {}
        self.excl = excl


class Sched:
    def __init__(self, nc, es):
        self.nc = nc
        self.engs = {"pe": nc.tensor, "act": nc.scalar, "dve": nc.vector, "pool": nc.gpsimd, "sp": nc.sync}
        self.semobj = {}
        self.cnt = {}
        for e in ("pe", "act", "dve", "pool"):
            self.semobj[e] = es.enter_context(nc.semaphore("s_" + e))
            self.cnt[e] = 0
        self.R = 8
        self.rpos = {"sp": 0, "pool": 0}
        self.rcnt = {}
        for q in ("sp", "pool"):
            for j in range(self.R):
                self.semobj[(q, j)] = es.enter_context(nc.semaphore(f"d_{q}{j}"))
                self.rcnt[(q, j)] = 0
        self.known = {e: {} for e in self.engs}

    def _wait(self, eng, tok):
        if tok is None:
            return
        k, v = tok
        if self.known[eng].get(k, 0) >= v:
            return
        self.engs[eng].wait_ge(self.semobj[k], v)
        self.known[eng][k] = v

    def _deps(self, eng, reads, writes):
        for b in reads:
            if b.excl:
                self._wait(eng, b.w)
                for k, v in b.rs.items():
                    self._wait(eng, (k, v))
            else:
                self._wait(eng, b.w)
        for b in writes:
            self._wait(eng, b.w)
            for k, v in b.rs.items():
                self._wait(eng, (k, v))

    def _register(self, tok, reads, writes):
        k, v = tok
        for b in reads:
            if b.excl:
                b.w = tok
                b.rs = {}
            else:
                if b.rs.get(k, 0) < v:
                    b.rs[k] = v
        for b in writes:
            b.w = tok
            b.rs = {}

    def op(self, eng, fn, reads=(), writes=()):
        self._deps(eng, reads, writes)
        ins = fn(self.engs[eng])
        self.cnt[eng] += 1
        ins.then_inc(self.semobj[eng], 1)
        tok = (eng, self.cnt[eng])
        self._register(tok, reads, writes)
        return tok

    def mm(self, out_ap, lhsT, rhs, start, stop, reads=(), out_buf=None, inc=None):
        if inc is None:
            inc = stop
        self._deps("pe", reads, [out_buf] if start else [])
        ins = self.nc.tensor.matmul(out_ap, lhsT, rhs, start=start, stop=stop)
        tok = ("pe", self.cnt["pe"] + 1)
        if inc:
            self.cnt["pe"] += 1
            ins.then_inc(self.semobj["pe"], 1)
        self._register(tok, reads, [out_buf] if stop else [])
        return tok

    def group(self, out_buf, out_ap, pairs, reads=()):
        n = len(pairs)
        for i, (l, r) in enumerate(pairs):
            self.mm(out_ap, l, r, i == 0, i == n - 1, reads=reads, out_buf=out_buf)

    def transpose(self, out_buf, out_ap, in_ap, ident, reads=()):
        self._deps("pe", reads, [out_buf])
        ins = self.nc.tensor.transpose(out_ap, in_ap, ident)
        self.cnt["pe"] += 1
        ins.then_inc(self.semobj["pe"], 1)
        tok = ("pe", self.cnt["pe"])
        self._register(tok, reads, [out_buf])

    def dma(self, q, out, in_, reads=(), writes=()):
        j = self.rpos[q]
        self.rpos[q] = (j + 1) % self.R
        key = (q, j)
        prev = self.rcnt[key]
        if prev > 0:
            self._wait(q, (key, prev))
        self._deps(q, reads, writes)
        ins = self.engs[q].dma_start(out=out, in_=in_)
        ins.then_inc(self.semobj[key], 16)
        self.rcnt[key] = prev + 16
        tok = (key, prev + 16)
        self._register(tok, reads, writes)
        return tok

    def barrier(self):
        toks = [(e, self.cnt[e]) for e in ("pe", "act", "dve", "pool") if self.cnt[e] > 0]
        toks += [(k, v) for k, v in self.rcnt.items() if v > 0]
        for e in self.engs:
            for t in toks:
                self._wait(e, t)


class PS:
    def __init__(self, nc, es):
        self.t = [es.enter_context(nc.psum_tensor(f"ps{i}", [128, 512], F32)) for i in range(7)]
        self.b = [Buf(excl=True) for _ in range(7)]
        self.tb = es.enter_context(nc.psum_tensor("psb", [128, 1024], BF16))
        self.bb = Buf(excl=True)
        self.pos = 0

    def next(self):
        i = self.pos
        self.pos = (i + 1) % 7
        return self.t[i], self.b[i]


def build_program(debug=False, nlayers=DEPTH, stop_after=None):
    nc = bass.Bass("TRN2", target_bir_lowering=False)
    dt_in = lambda name, shape, dt=F32: nc.dram_tensor(name, shape, dt, kind="ExternalInput").ap()
    skind = "ExternalOutput" if debug else "Internal"
    dt_sc = lambda name, shape, dt: nc.dram_tensor(name, shape, dt, kind=skind).ap()

    xin = dt_in("xin", [D, T])
    vecs_d = dt_in("vecs", [128, NV])
    w_mod = dt_in("w_mod", [DEPTH, D, 6 * D])
    w_in = dt_in("w_in", [DEPTH, D, NIN])
    lru_wa = dt_in("lru_wa", [DEPTH, 2, 8, 64, 64])
    lru_wx = dt_in("lru_wx", [DEPTH, 2, 8, 64, 64])
    w_uq = dt_in("w_uq", [DEPTH, 384, 768])
    w_ukv = dt_in("w_ukv", [DEPTH, 256, 1024])
    w_oa = dt_in("w_oa", [DEPTH, 512, D])
    w_ob = dt_in("w_ob", [DEPTH, 512, D])
    w_oc = dt_in("w_oc", [DEPTH, 512, D])
    w_out = dt_in("w_out", [DEPTH, D, D])
    w_ff1 = dt_in("w_ff1", [DEPTH, D, DFF])
    w_ff3 = dt_in("w_ff3", [DEPTH, D, DFF])
    w_ff2 = dt_in("w_ff2", [DEPTH, DFF, D])
    ropeA = dt_in("ropeA", [2, 128, T])
    retA = dt_in("retA", [2, 128, T])
    retAk = dt_in("retAk", [2, 128, T])
    rdec = dt_in("rdec", [128, 4 * 8])
    qdec = dt_in("qdec", [4, 2, 64, 128])
    maskT = dt_in("maskT", [4, 128, 512])
    cmat = dt_in("cmat", [128, 256])
    outT = nc.dram_tensor("outT", [D, LL], F32, kind="ExternalOutput").ap()

    Hs = dt_sc("Hs", [D, T], BF16)
    Us = dt_sc("Us", [512, T], F32)
    Kns = dt_sc("Kns", [512, T], BF16)
    Krs = dt_sc("Krs", [32, T], BF16)
    Vs = dt_sc("Vs", [T, 512], BF16)
    RKf = dt_sc("RKf", [256, T], BF16)
    RKt = dt_sc("RKt", [T, 256], BF16)
    RVs = dt_sc("RVs", [T, 512], BF16)
    Qns = dt_sc("Qns", [512, T], BF16)
    Qrs = dt_sc("Qrs", [256, T], BF16)
    RQs = dt_sc("RQs", [256, T], BF16)
    RECs = dt_sc("RECs", [512, T], BF16)
    YBs = dt_sc("YBs", [512, T], BF16)
    RETs = dt_sc("RETs", [512, T], BF16)
    X1s = dt_sc("X1s", [D, T], F32)
    X2s = dt_sc("X2s", [D, T], F32)

    with ExitStack() as es:
        S = Sched(nc, es)
        P = PS(nc, es)
        _uid = [0]

        def sb(st, name, shape, dt):
            _uid[0] += 1
            return st.enter_context(nc.sbuf_tensor(f"sb{_uid[0]}_{name}", shape, dt))

        vecs = sb(es, "vecs", [128, NV], F32)
        vecsB = Buf()
        cm = sb(es, "cm", [128, 256], F32)
        cmB = Buf()
        onesb = sb(es, "onesb", [128, 128], BF16)
        identb = sb(es, "identb", [128, 128], BF16)
        constB = Buf()
        modv = sb(es, "modv", [128, 96], F32)
        modvB = Buf()
        amix = sb(es, "amix", [128, 16], F32)
        affn = sb(es, "affn", [128, 16], F32)
        cact = sb(es, "cact", [128, 16], F32)
        epsc = sb(es, "epsc", [128, 1], F32)
        S.dma("sp", vecs[:], vecs_d, writes=[vecsB])
        S.dma("sp", cm[:], cmat, writes=[cmB])
        S.op("dve", lambda e: e.tensor_copy(out=onesb[:], in_=cm[:, 0:128]), reads=[cmB], writes=[constB])
        S.op("dve", lambda e: e.tensor_copy(out=identb[:], in_=cm[:, 128:256]), reads=[cmB], writes=[constB])
        S.op("dve", lambda e: e.memset(epsc[:], EPS), writes=[constB])
        S.op("act", lambda e: e.activation(out=cact[:], in_=vecs[:, VC["c"]:VC["c"] + 16], func=AF.Silu),
             reads=[vecsB], writes=[constB])
        onesf = cm[:, 0:128]
        identf = cm[:, 128:256]
        S.barrier()

        def vcol(name, i, n=1):
            return vecs[:, VC[name] + i:VC[name] + i + n]

        def rms_rstd(st, src_sq_list, nparts_scale, N, tag):
            pt, pb = P.next()
            S.group(pb, pt[:, :N], [(onesb[:], a) for a in src_sq_list], reads=[constB] + tag["reads"])
            sq = tag["sq"]
            S.op("act", lambda e: e.activation(out=sq[:, :N], in_=pt[:, :N], func=AF.Sqrt,
                                               bias=epsc[:], scale=nparts_scale), reads=[pb], writes=[tag["sqB"]])
            S.op("dve", lambda e: e.reciprocal(out=sq[:, :N], in_=sq[:, :N]), reads=[], writes=[tag["sqB"]])
            return sq

        for l in range(nlayers):
            last = l == DEPTH - 1
            xsrc = xin if l == 0 else X2s

            w1st = ExitStack()
            wA = sb(w1st, "wA", [128, 8, 2208], BF16)
            wS = sb(w1st, "wS", [128, 8, 544], BF16)
            wkvN = sb(w1st, "wkvN", [128, 2, 512], BF16)
            wkvV = sb(w1st, "wkvV", [128, 2, 512], BF16)
            wqN = sb(w1st, "wqN", [128, 3, 512], BF16)
            wqR = sb(w1st, "wqR", [128, 3, 256], BF16)
            wqRs = sb(w1st, "wqRs", [128, 3, 256], BF16)
            wB = Buf()
            wl = w_in[l]
            S.dma("pool", wA[:], wl[:, 0:2208].rearrange("(k p) c -> p k c", p=128), writes=[wB])
            for k in range(8):
                rows = wl[k * 128:(k + 1) * 128, :]
                S.dma("pool", wS[:, k, 0:16], rows[:, 784:800], writes=[wB])
                S.dma("pool", wS[:, k, 16:32], rows[:, 768:784], writes=[wB])
                for (dst0, src0) in ((32, 800), (288, 1952)):
                    src = rows[:, src0:src0 + 256].rearrange("p (h two d) -> p h two d", two=2, d=32)
                    dst = wS[:, k, dst0:dst0 + 256].rearrange("p (h two d) -> p h two d", two=2, d=32)
                    S.dma("pool", dst[:, :, 0, :], src[:, :, 1, :], writes=[wB])
                    S.dma("pool", dst[:, :, 1, :], src[:, :, 0, :], writes=[wB])
            for k in range(2):
                src = w_ukv[l][k * 128:(k + 1) * 128, :].rearrange("p (h x) -> p h x", x=128)
                S.dma("pool", wkvN[:, k, :].rearrange("p (h x) -> p h x", x=64), src[:, :, 0:64], writes=[wB])
                S.dma("pool", wkvV[:, k, :].rearrange("p (h x) -> p h x", x=64), src[:, :, 64:128], writes=[wB])
            for k in range(3):
                src = w_uq[l][k * 128:(k + 1) * 128, :].rearrange("p (h x) -> p h x", x=96)
                S.dma("pool", wqN[:, k, :].rearrange("p (h x) -> p h x", x=64), src[:, :, 0:64], writes=[wB])
                S.dma("pool", wqR[:, k, :].rearrange("p (h x) -> p h x", x=32), src[:, :, 64:96], writes=[wB])
                S.dma("pool", wqRs[:, k, :].rearrange("p (h x) -> p h x", x=32)[:, :, 0:16], src[:, :, 80:96],
                      writes=[wB])
                S.dma("pool", wqRs[:, k, :].rearrange("p (h x) -> p h x", x=32)[:, :, 16:32], src[:, :, 64:80],
                      writes=[wB])

            with ExitStack() as st:
                wst = [sb(st, f"wm{i}", [128, 8, 768], F32) for i in range(2)]
                wstB = [Buf(), Buf()]
                pt, pb = P.next()
                for blk in range(8):
                    buf, bB = wst[blk % 2], wstB[blk % 2]
                    S.dma("sp", buf[:], w_mod[l][:, blk * 768:(blk + 1) * 768].rearrange("(k p) c -> p k c", p=128),
                          writes=[bB])
                    for fc in range(6):
                        f = blk * 6 + fc
                        S.group(pb, pt[:, 2 * f:2 * f + 2],
                                [(buf[:, k, fc * 128:(fc + 1) * 128], cact[:, 2 * k:2 * k + 2]) for k in range(8)],
                                reads=[bB, constB])
                bm = VC[f"bmod{l}"]
                for j in range(2):
                    S.op("dve", lambda e, j=j: e.tensor_tensor(out=modv[:, j:96:2], in0=pt[:, j:96:2],
                                                               in1=vecs[:, bm:bm + 48], op=ALU.add),
                         reads=[pb, vecsB], writes=[modvB])
                for j in range(2):
                    S.op("dve", lambda e, j=j: e.scalar_tensor_tensor(
                        out=amix[:, j:16:2], in0=modv[:, 16 + j:32:2], scalar=1.0, in1=vcol(f"gmix{l}", 0, 8),
                        op0=ALU.add, op1=ALU.mult), reads=[modvB], writes=[modvB])
                    S.op("dve", lambda e, j=j: e.scalar_tensor_tensor(
                        out=affn[:, j:16:2], in0=modv[:, 64 + j:80:2], scalar=1.0, in1=vcol(f"gffn{l}", 0, 8),
                        op0=ALU.add, op1=ALU.mult), reads=[modvB], writes=[modvB])
                S.barrier()
            if stop_after == (l, 0):
                break

            def norm_mod(st, xt, xB, N, jj, A, shift_base, ht, hB, wk):
                S.op("act", lambda e: e.activation(out=wk["xsq"][:, :, :N], in_=xt[:, :, :N], func=AF.Square),
                     reads=[xB], writes=[wk["xsqB"]])
                rs = rms_rstd(st, [wk["xsq"][:, k, :N] for k in range(8)], 1.0 / D, N,
                              {"reads": [wk["xsqB"]], "sq": wk["rstd"], "sqB": wk["rstdB"]})
                for k in range(8):
                    tmp, tB = wk["tmp"][k % 2], wk["tmpB"][k % 2]
                    S.op("dve", lambda e, k=k, tmp=tmp: e.scalar_tensor_tensor(
                        out=tmp[:, :N], in0=xt[:, k, :N], scalar=A[:, 2 * k + jj:2 * k + jj + 1], in1=rs[:, :N],
                        op0=ALU.mult, op1=ALU.mult), reads=[xB, wk["rstdB"], modvB], writes=[tB])
                    S.op("act", lambda e, k=k, tmp=tmp: e.activation(
                        out=ht[:, k, :N], in_=tmp[:, :N], func=AF.Identity,
                        bias=modv[:, shift_base + 2 * k + jj:shift_base + 2 * k + jj + 1], scale=1.0),
                         reads=[tB, modvB], writes=[hB])

            def mk_norm_wk(st, pfx):
                return {"xsq": sb(st, pfx + "xsq", [128, 8, 512], BF16), "xsqB": Buf(),
                        "rstd": sb(st, pfx + "rstd", [128, 512], F32), "rstdB": Buf(),
                        "tmp": [sb(st, pfx + f"tmp{i}", [128, 512], F32) for i in range(2)], "tmpB": [Buf(), Buf()]}

            with ExitStack() as st:
                xt = sb(st, "p1xt", [128, 8, 512], F32); xB = Buf()
                ht = sb(st, "p1ht", [128, 8, 512], BF16); hB = Buf()
                wk = mk_norm_wk(st, "p1")
                tabA = sb(st, "tabA", [128, 2, 512], F32)
                tabR = sb(st, "tabR", [128, 2, 512], F32)
                tabRk = sb(st, "tabRk", [128, 2, 512], F32)
                tabB = Buf()
                stf = [sb(st, f"stf{i}", [128, 4, 512], F32) for i in range(2)]; stfB = [Buf(), Buf()]
                stb = [sb(st, f"stb{i}", [128, 4, 512], BF16) for i in range(3)]; stbB = [Buf() for _ in range(3)]
                lat = sb(st, "p1lat", [128, 3, 512], F32); latB = Buf()
                latsq = sb(st, "p1latsq", [128, 3, 512], BF16); latsqB = Buf()
                latn = sb(st, "p1latn", [128, 3, 512], BF16); latnB = Buf()
                lrs = sb(st, "p1lrs", [128, 512], F32); lrsB = Buf()
                r1 = sb(st, "p1r1", [128, 512], F32); r1B = Buf()
                r2 = sb(st, "p1r2", [128, 512], F32); r2B = Buf()
                cnt = {"f": 0, "b": 0}

                def nstf():
                    i = cnt["f"] % 2; cnt["f"] += 1
                    return stf[i], stfB[i]

                def nstb():
                    i = cnt["b"] % 3; cnt["b"] += 1
                    return stb[i], stbB[i]

                def rope_fm(pairs_lhs, pairs_lhs_sw, rhs_list, M, N, tab, dst_ap, dstB, reads):
                    p1, b1 = P.next()
                    S.group(b1, p1[:M, :N], list(zip(pairs_lhs, rhs_list)), reads=reads)
                    p2, b2 = P.next()
                    S.group(b2, p2[:M, :N], list(zip(pairs_lhs_sw, rhs_list)), reads=reads)
                    S.op("dve", lambda e: e.tensor_tensor(out=r1[:M, :N], in0=p1[:M, :N], in1=tab[:M, 0, :N], op=ALU.mult),
                         reads=[b1, tabB], writes=[r1B])
                    S.op("dve", lambda e: e.tensor_tensor(out=r2[:M, :N], in0=p2[:M, :N], in1=tab[:M, 1, :N], op=ALU.mult),
                         reads=[b2, tabB], writes=[r2B])
                    S.op("dve", lambda e: e.tensor_tensor(out=dst_ap, in0=r1[:M, :N], in1=r2[:M, :N], op=ALU.add),
                         reads=[r1B, r2B], writes=[dstB])

                for ti, (t0, N) in enumerate(TILES):
                    jj = 1 if ti == 0 else 0
                    ns = N // 128
                    S.dma("sp", xt[:, :, :N], xsrc[:, t0:t0 + N].rearrange("(k p) t -> p k t", p=128), writes=[xB])
                    S.dma("sp", tabA[:, :, :N], ropeA[:, :, t0:t0 + N].rearrange("c p t -> p c t"), writes=[tabB])
                    S.dma("sp", tabR[:, :, :N], retA[:, :, t0:t0 + N].rearrange("c p t -> p c t"), writes=[tabB])
                    S.dma("sp", tabRk[:, :, :N], retAk[:, :, t0:t0 + N].rearrange("c p t -> p c t"), writes=[tabB])
                    norm_mod(st, xt, xB, N, jj, amix, 0, ht, hB, wk)
                    S.dma("pool", Hs[:, t0:t0 + N].rearrange("(k p) t -> p k t", p=128), ht[:, :, :N], reads=[hB])
                    hr = [ht[:, k, :N] for k in range(8)]
                    R = [hB, wB]
                    sf, sfB = nstf()
                    for m in range(4):
                        pt, pb = P.next()
                        S.group(pb, pt[:, :N], [(wA[:, k, m * 128:(m + 1) * 128], hr[k]) for k in range(8)], reads=R)
                        S.op("act", lambda e, m=m, pt=pt: e.activation(out=sf[:, m, :N], in_=pt[:, :N], func=AF.Copy),
                             reads=[pb], writes=[sfB])
                    S.dma("pool", Us[:, t0:t0 + N].rearrange("(k p) t -> p k t", p=128), sf[:, :, :N], reads=[sfB])

                    def latent_norm(c0, nchunk, gname):
                        for m in range(nchunk):
                            pt, pb = P.next()
                            S.group(pb, pt[:, :N], [(wA[:, k, c0 + m * 128:c0 + (m + 1) * 128], hr[k]) for k in range(8)],
                                    reads=R)
                            S.op("act", lambda e, m=m, pt=pt: e.activation(out=lat[:, m, :N], in_=pt[:, :N], func=AF.Copy),
                                 reads=[pb], writes=[latB])
                        S.op("act", lambda e: e.activation(out=latsq[:, :nchunk, :N], in_=lat[:, :nchunk, :N],
                                                           func=AF.Square), reads=[latB], writes=[latsqB])
                        rs = rms_rstd(st, [latsq[:, m, :N] for m in range(nchunk)], 1.0 / (128 * nchunk), N,
                                      {"reads": [latsqB], "sq": lrs, "sqB": lrsB})
                        for m in range(nchunk):
                            S.op("dve", lambda e, m=m: e.scalar_tensor_tensor(
                                out=latn[:, m, :N], in0=lat[:, m, :N], scalar=vcol(gname, m), in1=rs[:, :N],
                                op0=ALU.mult, op1=ALU.mult), reads=[latB, lrsB, vecsB], writes=[latnB])

                    latent_norm(512, 2, f"gkv{l}")
                    sbf, sbB = nstb()
                    for m in range(4):
                        pt, pb = P.next()
                        S.group(pb, pt[:, :N], [(wkvN[:, k, m * 128:(m + 1) * 128], latn[:, k, :N]) for k in range(2)],
                                reads=[latnB, wB])
                        S.op("act", lambda e, m=m, pt=pt: e.activation(out=sbf[:, m, :N], in_=pt[:, :N], func=AF.Copy),
                             reads=[pb], writes=[sbB])
                    S.dma("pool", Kns[:, t0:t0 + N].rearrange("(k p) t -> p k t", p=128), sbf[:, :, :N], reads=[sbB])
                    sbf, sbB = nstb()
                    for s in range(ns):
                        pt, pb = P.next()
                        S.group(pb, pt[:, :], [(latn[:, k, s * 128:(s + 1) * 128], wkvV[:, k, :]) for k in range(2)],
                                reads=[latnB, wB])
                        S.op("act", lambda e, s=s, pt=pt: e.activation(out=sbf[:, s, :], in_=pt[:, :], func=AF.Copy),
                             reads=[pb], writes=[sbB])
                    S.dma("pool", Vs[t0:t0 + N, :].rearrange("(s p) c -> p s c", p=128), sbf[:, :ns, :], reads=[sbB])

                    sbf, sbB = nstb()
                    rope_fm([wA[:, k, 768:800] for k in range(8)], [wS[:, k, 0:32] for k in range(8)], hr, 32, N,
                            tabA, sbf[:32, 0, :N], sbB, R)
                    S.dma("pool", Krs[:, t0:t0 + N], sbf[:32, 0, :N], reads=[sbB])

                    sbf, sbB = nstb()
                    for m in range(2):
                        rope_fm([wA[:, k, 800 + m * 128:800 + (m + 1) * 128] for k in range(8)],
                                [wS[:, k, 32 + m * 128:32 + (m + 1) * 128] for k in range(8)], hr, 128, N,
                                tabRk, sbf[:, m, :N], sbB, R)
                    S.dma("pool", RKf[:, t0:t0 + N].rearrange("(k p) t -> p k t", p=128), sbf[:, 0:2, :N], reads=[sbB])
                    sb2, sb2B = nstb()
                    for s in range(ns):
                        for m in range(2):
                            c0 = (s * 2 + m) * 128
                            S.transpose(P.bb, P.tb[:, c0:c0 + 128], sbf[:, m, s * 128:(s + 1) * 128], identb[:],
                                        reads=[sbB, constB])
                    S.op("act", lambda e: e.activation(out=sb2[:, 0:2, :].rearrange("p a b -> p (a b)")[:, :ns * 256],
                                                       in_=P.tb[:, :ns * 256], func=AF.Copy),
                         reads=[P.bb], writes=[sb2B])
                    S.dma("pool", RKt[t0:t0 + N, :].rearrange("(s p) c -> p s c", p=128),
                          sb2[:, 0:2, :].rearrange("p a b -> p (a b)")[:, :ns * 256].rearrange("p (s c) -> p s c", c=256),
                          reads=[sb2B])

                    sbf, sbB = nstb()
                    for s in range(ns):
                        pt, pb = P.next()
                        S.group(pb, pt[:, :], [(ht[:, k, s * 128:(s + 1) * 128], wA[:, k, 1056:1568]) for k in range(8)],
                                reads=R)
                        S.op("act", lambda e, s=s, pt=pt: e.activation(out=sbf[:, s, :], in_=pt[:, :], func=AF.Copy),
                             reads=[pb], writes=[sbB])
                    S.dma("pool", RVs[t0:t0 + N, :].rearrange("(s p) c -> p s c", p=128), sbf[:, :ns, :], reads=[sbB])

                    latent_norm(1568, 3, f"gq{l}")
                    qr = [latn[:, k, :N] for k in range(3)]
                    sbf, sbB = nstb()
                    for m in range(4):
                        pt, pb = P.next()
                        S.group(pb, pt[:, :N], [(wqN[:, k, m * 128:(m + 1) * 128], qr[k]) for k in range(3)],
                                reads=[latnB, wB])
                        S.op("act", lambda e, m=m, pt=pt: e.activation(out=sbf[:, m, :N], in_=pt[:, :N], func=AF.Copy),
                             reads=[pb], writes=[sbB])
                    S.dma("pool", Qns[:, t0:t0 + N].rearrange("(k p) t -> p k t", p=128), sbf[:, :, :N], reads=[sbB])
                    sbf, sbB = nstb()
                    for m in range(2):
                        rope_fm([wqR[:, k, m * 128:(m + 1) * 128] for k in range(3)],
                                [wqRs[:, k, m * 128:(m + 1) * 128] for k in range(3)], qr, 128, N,
                                tabA, sbf[:, m, :N], sbB, [latnB, wB])
                    S.dma("pool", Qrs[:, t0:t0 + N].rearrange("(k p) t -> p k t", p=128), sbf[:, 0:2, :N], reads=[sbB])

                    sbf, sbB = nstb()
                    for m in range(2):
                        rope_fm([wA[:, k, 1952 + m * 128:1952 + (m + 1) * 128] for k in range(8)],
                                [wS[:, k, 288 + m * 128:288 + (m + 1) * 128] for k in range(8)], hr, 128, N,
                                tabR, sbf[:, m, :N], sbB, R)
                    S.dma("pool", RQs[:, t0:t0 + N].rearrange("(k p) t -> p k t", p=128), sbf[:, 0:2, :N], reads=[sbB])
                S.barrier()
            w1st.close()
            if stop_after == (l, 1):
                break

            with ExitStack() as st:
                wblk = sb(st, "wblk", [128, 16, 128], BF16)
                wbB = Buf()
                S.op("dve", lambda e: e.memset(wblk[:], 0.0), writes=[wbB])
                for d in range(2):
                    for g, wsrc in enumerate((lru_wa, lru_wx)):
                        for c in range(4):
                            idx = (d * 2 + g) * 4 + c
                            S.dma("pool", wblk[0:64, idx, 0:64], wsrc[l, d, 2 * c], writes=[wbB])
                            S.dma("pool", wblk[64:128, idx, 64:128], wsrc[l, d, 2 * c + 1], writes=[wbB])
                coef = sb(st, "coef", [128, 16], F32)
                coefB = Buf()
                lamc = VC[f"lam{l}"]
                S.op("act", lambda e: e.activation(out=coef[:, 0:8], in_=vecs[:, lamc:lamc + 8], func=AF.Exp, scale=-1.0),
                     reads=[vecsB], writes=[coefB])
                S.op("act", lambda e: e.activation(out=coef[:, 0:8], in_=coef[:, 0:8], func=AF.Ln, bias=1.0, scale=1.0),
                     writes=[coefB])
                S.op("dve", lambda e: e.tensor_scalar(out=coef[:, 8:16], in0=coef[:, 0:8], scalar1=-16.0, scalar2=None,
                                                      op0=ALU.mult), writes=[coefB])
                S.op("dve", lambda e: e.tensor_scalar(out=coef[:, 0:8], in0=coef[:, 0:8], scalar1=-8.0, scalar2=None,
                                                      op0=ALU.mult), writes=[coefB])
                Uc = sb(st, "Uc", [128, T], F32); UcB = Buf()
                u = sb(st, "u", [128, T], F32); uB = Buf()
                ub = sb(st, "ub", [128, T], BF16); ubB = Buf()
                rr = sb(st, "rr", [128, T], F32); rrB = Buf()
                ii = sb(st, "ii", [128, T], F32); iiB = Buf()
                e2 = sb(st, "e2", [128, T], F32); e2B = Buf()
                hs = [sb(st, f"hs{d}", [128, T], F32) for d in range(2)]; hsB = [Buf(), Buf()]
                recb = sb(st, "recb", [128, T], BF16); recB = Buf()
                cw = VC[f"convw{l}"]
                for c in range(4):
                    S.dma("sp", Uc[:], Us[c * 128:(c + 1) * 128, :], writes=[UcB])
                    S.op("dve", lambda e, c=c: e.tensor_scalar(
                        out=u[:], in0=Uc[:], scalar1=vecs[:, cw + 2 * 4 + c:cw + 2 * 4 + c + 1],
                        scalar2=vcol(f"convb{l}", c), op0=ALU.mult, op1=ALU.add), reads=[UcB, vecsB], writes=[uB])
                    for (s0, s1) in ((0, LC), (LC, T)):
                        for tap, off in ((0, -2), (1, -1), (3, 1)):
                            lo = s0 + max(0, -off)
                            hi = s1 - max(0, off)
                            S.op("dve", lambda e, c=c, tap=tap, lo=lo, hi=hi, off=off: e.scalar_tensor_tensor(
                                out=u[:, lo:hi], in0=Uc[:, lo + off:hi + off],
                                scalar=vecs[:, cw + tap * 4 + c:cw + tap * 4 + c + 1], in1=u[:, lo:hi],
                                op0=ALU.mult, op1=ALU.add), reads=[UcB, vecsB], writes=[uB])
                    S.op("act", lambda e: e.activation(out=ub[:], in_=u[:], func=AF.Copy), reads=[uB], writes=[ubB])
                    for d in range(2):
                        for g, (dst, dB, bname) in enumerate(((rr, rrB, f"ba{l}"), (ii, iiB, f"bx{l}"))):
                            idx = (d * 2 + g) * 4 + c
                            for (t0, N) in TILES:
                                pt, pb = P.next()
                                S.group(pb, pt[:, :N], [(wblk[:, idx, :], ub[:, t0:t0 + N])], reads=[wbB, ubB])
                                S.op("act", lambda e, pt=pt, dst=dst, t0=t0, N=N, bname=bname, d=d, c=c: e.activation(
                                    out=dst[:, t0:t0 + N], in_=pt[:, :N], func=AF.Sigmoid,
                                    bias=vcol(bname, d * 4 + c), scale=1.0), reads=[pb, vecsB], writes=[dB])
                        S.op("act", lambda e, d=d, c=c: e.activation(out=e2[:], in_=rr[:], func=AF.Exp,
                                                                     scale=coef[:, 8 + d * 4 + c:8 + d * 4 + c + 1]),
                             reads=[rrB, coefB], writes=[e2B])
                        S.op("act", lambda e, d=d, c=c: e.activation(out=rr[:], in_=rr[:], func=AF.Exp,
                                                                     scale=coef[:, d * 4 + c:d * 4 + c + 1]),
                             reads=[coefB], writes=[rrB])
                        S.op("act", lambda e: e.activation(out=e2[:], in_=e2[:], func=AF.Sqrt, bias=1.0, scale=-1.0),
                             writes=[e2B])
                        S.op("dve", lambda e: e.tensor_tensor(out=ii[:], in0=ii[:], in1=u[:], op=ALU.mult),
                             reads=[uB], writes=[iiB])
                        S.op("dve", lambda e: e.tensor_tensor(out=ii[:], in0=ii[:], in1=e2[:], op=ALU.mult),
                             reads=[e2B], writes=[iiB])
                        h = hs[d]
                        if d == 0:
                            S.op("dve", lambda e, h=h: e.tensor_tensor_scan(out=h[:], data0=rr[:], data1=ii[:], initial=0.0,
                                                                           op0=ALU.mult, op1=ALU.add),
                                 reads=[rrB, iiB], writes=[hsB[d]])
                        else:
                            S.op("dve", lambda e, h=h: e.tensor_tensor_scan(
                                out=h[:, 0:LC][:, ::-1], data0=rr[:, 0:LC][:, ::-1],
                                data1=ii[:, 0:LC][:, ::-1], initial=0.0, op0=ALU.mult, op1=ALU.add),
                                 reads=[rrB, iiB], writes=[hsB[d]])
                            S.op("dve", lambda e, h=h: e.tensor_tensor_scan(
                                out=h[:, LC:T][:, ::-1], data0=rr[:, LC:T][:, ::-1], data1=ii[:, LC:T][:, ::-1],
                                initial=h[:, 0:1], op0=ALU.mult, op1=ALU.add),
                                 reads=[rrB, iiB], writes=[hsB[d]])
                    S.op("dve", lambda e: e.tensor_tensor(out=recb[:], in0=hs[0][:], in1=hs[1][:], op=ALU.add),
                         reads=[hsB[0], hsB[1]], writes=[recB])
                    S.dma("pool", RECs[c * 128:(c + 1) * 128, :], recb[:], reads=[recB])
                S.barrier()
            if stop_after == (l, 2):
                break

            with ExitStack() as st:
                rq = sb(st, "rq", [64, T], BF16)
                rkf = sb(st, "rkf", [64, T], BF16)
                ktm = sb(st, "ktm", [128, NCH, 64], BF16)
                vtm = sb(st, "vtm", [128, NCH, 128], BF16)
                ldB = Buf()
                qd = sb(st, "qd", [64, 2, 128], F32)
                mk = sb(st, "mk", [128, 512], F32)
                rdc = sb(st, "rdc", [128, 32], F32)
                cB = Buf()
                S.dma("sp", rdc[:], rdec, writes=[cB])
                qfr = [sb(st, f"qfr{d}", [64, T], BF16) for d in range(2)]; qfrB = [Buf(), Buf()]
                kfr = [sb(st, f"kfr{d}", [128, NCH, 64], BF16) for d in range(2)]; kfrB = [Buf(), Buf()]
                Rst = [sb(st, f"Rst{d}", [64, NCH, 128], BF16) for d in range(2)]; RstB = [Buf(), Buf()]
                hist = [sb(st, f"hist{d}", [64, NCH + 1, 128], F32) for d in range(2)]; histB = [Buf(), Buf()]
                step_of = [{c: i for i, c in enumerate(orders_)} for orders_ in ([list(range(NCH)), [1, 0] + list(range(NCH - 1, 1, -1))])]
                sm = [sb(st, f"sm{i}", [128, 512], BF16) for i in range(2)]; smB = [Buf(), Buf()]
                Oh = sb(st, "Oh", [128, T], F32); OhB = Buf()
                osq = sb(st, "osq", [128, 512], F32); osqB = Buf()
                msq = sb(st, "msq", [128, 512], F32); msqB = Buf()
                var = sb(st, "var", [128, 512], F32); varB = Buf()
                cen = sb(st, "cen", [128, 512], F32); cenB = Buf()
                rn = [sb(st, f"rn{i}", [128, 512], BF16) for i in range(2)]; rnB = [Buf(), Buf()]
                orders = [list(range(NCH)), [1, 0] + list(range(NCH - 1, 1, -1))]
                smc = 0
                rnc = 0
                for h in range(4):
                    S.dma("sp", rq[:], RQs[h * 64:(h + 1) * 64, :], writes=[ldB])
                    S.dma("sp", rkf[:], RKf[h * 64:(h + 1) * 64, :], writes=[ldB])
                    S.dma("sp", ktm[:], RKt[:, h * 64:(h + 1) * 64].rearrange("(c p) d -> p c d", p=128), writes=[ldB])
                    S.dma("sp", vtm[:], RVs[:, h * 128:(h + 1) * 128].rearrange("(c p) d -> p c d", p=128), writes=[ldB])
                    S.dma("sp", qd[:], qdec[h].rearrange("d p j -> p d j"), writes=[cB])
                    S.dma("sp", mk[:], maskT[h], writes=[cB])
                    for d in range(2):
                        S.op("dve", lambda e, d=d: e.tensor_tensor(
                            out=qfr[d][:].rearrange("p (c j) -> p c j", j=128),
                            in0=rq[:].rearrange("p (c j) -> p c j", j=128),
                            in1=qd[:, d, :].unsqueeze(1).to_broadcast([64, NCH, 128]), op=ALU.mult),
                             reads=[ldB, cB], writes=[qfrB[d]])
                        S.op("dve", lambda e, d=d, h=h: e.tensor_scalar(
                            out=kfr[d][:], in0=ktm[:], scalar1=rdc[:, h * 8 + d:h * 8 + d + 1], scalar2=None,
                            op0=ALU.mult), reads=[ldB, cB], writes=[kfrB[d]])
                    for d in range(2):
                        S.op("dve", lambda e, d=d: e.memset(hist[d][:, 0, :], 0.0), writes=[histB[d]])
                    for s_ in range(NCH):
                        for d in range(2):
                            c = orders[d][s_]
                            pt, pb = P.next()
                            S.group(pb, pt[0:64, 0:128], [(kfr[d][:, c, :], vtm[:, c, :])], reads=[kfrB[d], ldB])
                            S.op("dve", lambda e, d=d, pt=pt, h=h, s_=s_: e.scalar_tensor_tensor(
                                out=hist[d][:, s_ + 1, :], in0=hist[d][:, s_, :], scalar=rdc[0:64, h * 8 + 2 + d:h * 8 + 3 + d],
                                in1=pt[0:64, 0:128], op0=ALU.mult, op1=ALU.add), reads=[pb, cB], writes=[histB[d]])
                    for d in range(2):
                        S.op("act", lambda e, d=d: e.activation(out=Rst[d][:], in_=hist[d][:, 0:NCH, :], func=AF.Copy),
                             reads=[histB[d]], writes=[RstB[d]])
                    for g in range((NCH + 3) // 4):
                        cs = list(range(g * 4, min(NCH, g * 4 + 4)))
                        W = len(cs) * 128
                        pS, pSb = P.next()
                        for ci, c in enumerate(cs):
                            S.group(pSb, pS[:, ci * 128:(ci + 1) * 128],
                                    [(rkf[:, c * 128:(c + 1) * 128], rq[:, c * 128:(c + 1) * 128])], reads=[ldB])
                        sm_, smB_ = sm[smc % 2], smB[smc % 2]
                        smc += 1
                        S.op("dve", lambda e, pS=pS, sm_=sm_, W=W: e.tensor_tensor(out=sm_[:, :W], in0=pS[:, :W],
                                                                                  in1=mk[:, :W], op=ALU.mult),
                             reads=[pSb, cB], writes=[smB_])
                        po, pob = P.next()
                        for ci, c in enumerate(cs):
                            oap = po[:, ci * 128:(ci + 1) * 128]
                            S.mm(oap, vtm[:, c, :], sm_[:, ci * 128:(ci + 1) * 128], True, False,
                                 reads=[ldB, smB_], out_buf=pob)
                            S.mm(oap, Rst[0][:, step_of[0][c], :], qfr[0][:, c * 128:(c + 1) * 128], False, False,
                                 reads=[RstB[0], qfrB[0]], out_buf=pob)
                            S.mm(oap, Rst[1][:, step_of[1][c], :], qfr[1][:, c * 128:(c + 1) * 128], False, True,
                                 reads=[RstB[1], qfrB[1]], out_buf=pob)
                        S.op("act", lambda e, po=po, g=g, W=W: e.activation(out=Oh[:, g * 512:g * 512 + W], in_=po[:, :W],
                                                                            func=AF.Copy), reads=[pob], writes=[OhB])
                    for (t0, N) in TILES:
                        pm, pmb = P.next()
                        S.group(pmb, pm[:, :N], [(onesf, Oh[:, t0:t0 + N])], reads=[OhB, cmB])
                        S.op("act", lambda e, t0=t0, N=N: e.activation(out=osq[:, :N], in_=Oh[:, t0:t0 + N], func=AF.Square),
                             reads=[OhB], writes=[osqB])
                        pe2, pe2b = P.next()
                        S.group(pe2b, pe2[:, :N], [(onesf, osq[:, :N])], reads=[osqB, cmB])
                        S.op("act", lambda e, pm=pm, N=N: e.activation(out=msq[:, :N], in_=pm[:, :N], func=AF.Square,
                                                                       scale=1.0 / 128), reads=[pmb], writes=[msqB])
                        S.op("dve", lambda e, pe2=pe2, N=N: e.scalar_tensor_tensor(
                            out=var[:, :N], in0=pe2[:, :N], scalar=1.0 / 128, in1=msq[:, :N], op0=ALU.mult,
                            op1=ALU.subtract), reads=[pe2b, msqB], writes=[varB])
                        S.op("act", lambda e, N=N: e.activation(out=var[:, :N], in_=var[:, :N], func=AF.Sqrt, bias=epsc[:],
                                                                scale=1.0), reads=[constB], writes=[varB])
                        S.op("dve", lambda e, N=N: e.reciprocal(out=var[:, :N], in_=var[:, :N]), writes=[varB])
                        S.op("dve", lambda e, pm=pm, t0=t0, N=N: e.scalar_tensor_tensor(
                            out=cen[:, :N], in0=pm[:, :N], scalar=-1.0 / 128, in1=Oh[:, t0:t0 + N], op0=ALU.mult,
                            op1=ALU.add), reads=[pmb, OhB], writes=[cenB])
                        rn_, rnB_ = rn[rnc % 2], rnB[rnc % 2]
                        rnc += 1
                        S.op("dve", lambda e, rn_=rn_, N=N: e.tensor_tensor(out=rn_[:, :N], in0=cen[:, :N], in1=var[:, :N],
                                                                           op=ALU.mult), reads=[cenB, varB], writes=[rnB_])
                        S.dma("pool", RETs[h * 128:(h + 1) * 128, t0:t0 + N], rn_[:, :N], reads=[rnB_])
                S.barrier()
            if stop_after == (l, 4):
                break

            w5st = ExitStack()
            wG = sb(w5st, "wG", [128, 8, 4096], BF16)
            wo = [sb(w5st, f"wo{i}", [128, 4, 1024], BF16) for i in range(3)]
            wout = sb(w5st, "wout", [128, 8, 1024], BF16)
            w5B = Buf()
            for k in range(8):
                S.dma("pool", wG[:, k, :], w_in[l][k * 128:(k + 1) * 128, 2208:NIN], writes=[w5B])
                S.dma("pool", wout[:, k, :], w_out[l][k * 128:(k + 1) * 128, :], writes=[w5B])
            for i, wsrc in enumerate((w_oa, w_ob, w_oc)):
                S.dma("pool", wo[i][:], wsrc[l].rearrange("(k p) c -> p k c", p=128), writes=[w5B])
            with ExitStack() as st:
                Kt = [sb(st, f"Kt{i}", [96, T], BF16) for i in range(2)]
                Qt = [sb(st, f"Qt{i}", [96, T], BF16) for i in range(2)]
                Vt = [sb(st, f"Vt{i}", [128, NCH, 128], BF16) for i in range(2)]
                hdB = [Buf(), Buf()]
                NPT = 6
                Pt = [sb(st, f"Pt{i}", [128, 512], BF16) for i in range(NPT)]; PtB = [Buf() for _ in range(NPT)]
                rd = [sb(st, f"rd{i}", [128, 512], F32) for i in range(2)]; rdB = [Buf(), Buf()]
                osb = [sb(st, f"osb{i}", [64, 512], F32) for i in range(2)]; osbB = [Buf(), Buf()]
                ybs = [sb(st, f"ybs{i}", [64, 512], BF16) for i in range(2)]; ybsB = [Buf(), Buf()]
                for i in range(2):
                    S.op("dve", lambda e, i=i: e.memset(Vt[i][:, :, 64:128], 1.0), writes=[hdB[i]])
                scale = 96.0 ** -0.5
                NSB = 4
                Sbk = [(P.t[i], P.b[i]) for i in range(NSB)]
                Obk = [(P.t[4], P.b[4]), (P.t[5], P.b[5])]
                Rbk = (P.t[6], P.b[6])
                LA = 3
                qtiles = TILES[1:] + ([TILES[0]] if not last else [])
                items = []
                qi = 0
                for h in range(8):
                    for (q0, N) in qtiles:
                        keys = list(range(NCH)) if q0 >= LC else list(range(LC // 128))
                        for ji, j in enumerate(keys):
                            items.append((h, q0, N, j, ji == 0, ji == len(keys) - 1, qi))
                        qi += 1
                loaded = set()

                def load_head(h):
                    b = h % 2
                    S.dma("sp", Kt[b][0:64, :], Kns[h * 64:(h + 1) * 64, :], writes=[hdB[b]])
                    S.dma("sp", Kt[b][64:96, :], Krs[:, :], writes=[hdB[b]])
                    S.dma("sp", Qt[b][0:64, :], Qns[h * 64:(h + 1) * 64, :], writes=[hdB[b]])
                    S.dma("sp", Qt[b][64:96, :], Qrs[h * 32:(h + 1) * 32, :], writes=[hdB[b]])
                    S.dma("sp", Vt[b][:, :, 0:64], Vs[:, h * 64:(h + 1) * 64].rearrange("(j p) d -> p j d", p=128),
                          writes=[hdB[b]])

                def issue_qk(idx):
                    h, q0, N, j, first, lastk, qi = items[idx]
                    b = h % 2
                    if h not in loaded:
                        loaded.add(h)
                        load_head(h)
                    pS, pSb = Sbk[idx % NSB]
                    S.group(pSb, pS[:, :N], [(Kt[b][:, j * 128:(j + 1) * 128], Qt[b][:, q0:q0 + N])], reads=[hdB[b]])

                pending = []

                def epilogue_pe(h, q0, N, qi):
                    po, pob = Obk[qi % 2]
                    pr, prb = Rbk
                    r_, rB_ = rd[qi % 2], rdB[qi % 2]
                    o_, oB_ = osb[qi % 2], osbB[qi % 2]
                    S.group(prb, pr[0:64, :N], [(cm[64:128, 192:256], r_[64:128, :N])], reads=[rB_, cmB])
                    yb_, ybB_ = ybs[qi % 2], ybsB[qi % 2]
                    S.op("dve", lambda e: e.tensor_tensor(out=yb_[:, :N], in0=o_[:, :N], in1=pr[0:64, :N], op=ALU.mult),
                         reads=[oB_, prb], writes=[ybB_])
                    S.dma("pool", YBs[h * 64:(h + 1) * 64, q0:q0 + N], yb_[:, :N], reads=[ybB_])

                for idx in range(min(LA, len(items))):
                    issue_qk(idx)
                for idx in range(len(items)):
                    h, q0, N, j, first, lastk, qi = items[idx]
                    b = h % 2
                    if idx + LA < len(items):
                        issue_qk(idx + LA)
                    pS, pSb = Sbk[idx % NSB]
                    pt_, ptB_ = Pt[idx % NPT], PtB[idx % NPT]
                    S.op("act", lambda e, pS=pS, pt_=pt_, N=N: e.activation(out=pt_[:, :N], in_=pS[:, :N], func=AF.Exp,
                                                                            scale=scale), reads=[pSb], writes=[ptB_])
                    po, pob = Obk[qi % 2]
                    S.mm(po[:, :N], Vt[b][:, j, :], pt_[:, :N], first, lastk, reads=[ptB_, hdB[b]], out_buf=pob)
                    pending = [(c - 1, a) for (c, a) in pending]
                    for (c, a) in pending:
                        if c <= 0:
                            epilogue_pe(*a)
                    pending = [(c, a) for (c, a) in pending if c > 0]
                    if lastk:
                        r_, rB_ = rd[qi % 2], rdB[qi % 2]
                        o_, oB_ = osb[qi % 2], osbB[qi % 2]
                        S.op("dve", lambda e, r_=r_, po=po, N=N: e.reciprocal(out=r_[64:128, :N], in_=po[64:128, :N]),
                             reads=[pob], writes=[rB_])
                        S.op("dve", lambda e, o_=o_, po=po, N=N: e.tensor_copy(out=o_[:, :N], in_=po[0:64, :N]),
                             reads=[pob], writes=[oB_])
                        pending.append((12, (h, q0, N, qi)))
                for (c, a) in pending:
                    epilogue_pe(*a)
                S.barrier()
            if stop_after == (l, 3):
                break

            with ExitStack() as st:
                ht2 = [sb(st, f"aht{i}", [128, 8, 512], BF16) for i in range(2)]; htB2 = [Buf(), Buf()]
                xt2 = [sb(st, f"axt{i}", [128, 8, 512], F32) for i in range(2)]; xtB2 = [Buf(), Buf()]
                yin2 = [[sb(st, f"yin{p}{i}", [128, 4, 512], BF16) for i in range(3)] for p in range(2)]
                yB2 = [[Buf() for i in range(3)] for p in range(2)]
                gts = [sb(st, f"agt{i}", [128, 512], F32) for i in range(2)]; gtsB = [Buf(), Buf()]
                sg = [sb(st, f"sg{i}", [128, 512], F32) for i in range(3)]; sgB = [Buf() for _ in range(3)]
                macc = sb(st, "macc", [128, 512], F32); maccB = Buf()
                mt2 = sb(st, "mt2", [128, 512], F32); mt2B = Buf()
                mb = sb(st, "amb", [128, 8, 512], BF16); mbB = Buf()
                tl5 = [(ti, t0, N) for ti, (t0, N) in enumerate(TILES) if not (last and ti == 0)]

                def load5a(idx):
                    ti, t0, N = tl5[idx]
                    p = idx % 2
                    S.dma("sp", ht2[p][:, :, :N], Hs[:, t0:t0 + N].rearrange("(k p) t -> p k t", p=128), writes=[htB2[p]])
                    for i, src in enumerate((RECs, YBs, RETs)):
                        S.dma("sp", yin2[p][i][:, :, :N], src[:, t0:t0 + N].rearrange("(k p) t -> p k t", p=128),
                              writes=[yB2[p][i]])
                    S.dma("sp", xt2[p][:, :, :N], xsrc[:, t0:t0 + N].rearrange("(k p) t -> p k t", p=128), writes=[xtB2[p]])

                load5a(0)
                gc = 0
                for idx, (ti, t0, N) in enumerate(tl5):
                    p = idx % 2
                    jj = 1 if ti == 0 else 0
                    if idx + 1 < len(tl5):
                        load5a(idx + 1)
                    ht, htB, xt, xtB, yin, yB = ht2[p], htB2[p], xt2[p], xtB2[p], yin2[p], yB2[p]
                    hr = [ht[:, k, :N] for k in range(8)]
                    for gi, (func, yi) in enumerate(((AF.Gelu, 0), (AF.Silu, 2))):
                        for m in range(4):
                            c0 = gi * 512 + m * 128
                            pt, pb = P.next()
                            S.group(pb, pt[:, :N], [(wG[:, k, c0:c0 + 128], hr[k]) for k in range(8)], reads=[htB, w5B])
                            gt, gtB = gts[gc % 2], gtsB[gc % 2]
                            gc += 1
                            S.op("act", lambda e, pt=pt, func=func, gt=gt: e.activation(out=gt[:, :N], in_=pt[:, :N], func=func),
                                 reads=[pb], writes=[gtB])
                            S.op("dve", lambda e, yi=yi, m=m, gt=gt: e.tensor_tensor(out=yin[yi][:, m, :N], in0=gt[:, :N],
                                                                                    in1=yin[yi][:, m, :N], op=ALU.mult),
                                 reads=[gtB], writes=[yB[yi]])
                    for mo in range(8):
                        for br in range(3):
                            c0 = 1024 + br * 1024 + mo * 128
                            pt, pb = P.next()
                            S.group(pb, pt[:, :N], [(wG[:, k, c0:c0 + 128], hr[k]) for k in range(8)], reads=[htB, w5B])
                            S.op("act", lambda e, pt=pt, br=br: e.activation(out=sg[br][:, :N], in_=pt[:, :N],
                                                                             func=AF.Sigmoid), reads=[pb], writes=[sgB[br]])
                        for br in range(3):
                            pt, pb = P.next()
                            S.group(pb, pt[:, :N], [(wo[br][:, k, mo * 128:(mo + 1) * 128], yin[br][:, k, :N])
                                                    for k in range(4)], reads=[yB[br], w5B])
                            if br == 0:
                                S.op("dve", lambda e, pt=pt: e.tensor_tensor(out=macc[:, :N], in0=pt[:, :N], in1=sg[0][:, :N],
                                                                             op=ALU.mult), reads=[pb, sgB[0]], writes=[maccB])
                            else:
                                S.op("dve", lambda e, pt=pt, br=br: e.tensor_tensor(out=mt2[:, :N], in0=pt[:, :N],
                                                                                    in1=sg[br][:, :N], op=ALU.mult),
                                     reads=[pb, sgB[br]], writes=[mt2B])
                                if br == 1:
                                    S.op("dve", lambda e: e.tensor_tensor(out=macc[:, :N], in0=macc[:, :N], in1=mt2[:, :N],
                                                                          op=ALU.add), reads=[mt2B], writes=[maccB])
                                else:
                                    S.op("dve", lambda e, mo=mo: e.tensor_tensor(out=mb[:, mo, :N], in0=macc[:, :N],
                                                                                 in1=mt2[:, :N], op=ALU.add),
                                         reads=[mt2B, maccB], writes=[mbB])
                    for mo in range(8):
                        pt, pb = P.next()
                        S.group(pb, pt[:, :N], [(wout[:, k, mo * 128:(mo + 1) * 128], mb[:, k, :N]) for k in range(8)],
                                reads=[mbB, w5B])
                        S.op("dve", lambda e, pt=pt, mo=mo: e.scalar_tensor_tensor(
                            out=xt[:, mo, :N], in0=pt[:, :N], scalar=modv[:, 32 + 2 * mo + jj:32 + 2 * mo + jj + 1],
                            in1=xt[:, mo, :N], op0=ALU.mult, op1=ALU.add), reads=[pb, modvB], writes=[xtB])
                    S.dma("pool", X1s[:, t0:t0 + N].rearrange("(k p) t -> p k t", p=128), xt[:, :, :N], reads=[xtB])
                S.barrier()
            w5st.close()
            if stop_after == (l, 5):
                break

            with ExitStack() as st:
                w1 = sb(st, "w1", [128, 8, DFF], BF16)
                w3 = sb(st, "w3", [128, 8, DFF], BF16)
                w2 = sb(st, "w2", [128, 22, D], BF16)
                w1B, w3B, w2B = Buf(), Buf(), Buf()
                S.dma("pool", w1[:], w_ff1[l].rearrange("(k p) c -> p k c", p=128), writes=[w1B])
                S.dma("pool", w3[:], w_ff3[l].rearrange("(k p) c -> p k c", p=128), writes=[w3B])
                for k0 in (0, 11):
                    S.dma("pool", w2[:, k0:k0 + 11, :], w_ff2[l][k0 * 128:(k0 + 11) * 128, :].rearrange("(k p) c -> p k c", p=128),
                          writes=[w2B])
                NB = 256
                xt2 = [sb(st, f"bxt{i}", [128, 8, NB], F32) for i in range(2)]; xB2 = [Buf(), Buf()]
                hf2 = [sb(st, f"bhf{i}", [128, 8, NB], BF16) for i in range(2)]; hfB2 = [Buf(), Buf()]
                xsq = sb(st, "bxsq", [128, 8, NB], BF16); xsqB = Buf()
                rstd = sb(st, "brstd", [128, NB], F32); rstdB = Buf()
                tmps = [sb(st, f"btmp{i}", [128, NB], F32) for i in range(2)]; tmpsB = [Buf(), Buf()]
                s1s = [sb(st, f"bs1{i}", [128, NB], F32) for i in range(2)]; s1sB = [Buf(), Buf()]
                gg = sb(st, "bgg", [128, 22, NB], BF16); ggB = Buf()
                tl6 = ([] if last else [(1, 0, 256)]) + [(0, 256 + NB * i, NB) for i in range(LL // NB)]

                def load6(idx):
                    jj, t0, N = tl6[idx]
                    p = idx % 2
                    S.dma("sp", xt2[p][:, :, :N], X1s[:, t0:t0 + N].rearrange("(k p) t -> p k t", p=128), writes=[xB2[p]])

                def norm_stages(idx):
                    jj, t0, N = tl6[idx]
                    p = idx % 2
                    xt, xB, hf, hfB = xt2[p], xB2[p], hf2[p], hfB2[p]
                    hold = {}
                    stg = []
                    stg.append(lambda: S.op("act", lambda e: e.activation(out=xsq[:, :, :N], in_=xt[:, :, :N], func=AF.Square),
                                            reads=[xB], writes=[xsqB]))

                    def s_mm():
                        pt, pb = P.next()
                        hold["pt"], hold["pb"] = pt, pb
                        S.group(pb, pt[:, :N], [(onesb[:], xsq[:, k, :N]) for k in range(8)], reads=[constB, xsqB])
                    stg.append(s_mm)

                    def s_rs():
                        pt, pb = hold["pt"], hold["pb"]
                        S.op("act", lambda e: e.activation(out=rstd[:, :N], in_=pt[:, :N], func=AF.Sqrt, bias=epsc[:],
                                                           scale=1.0 / D), reads=[pb, constB], writes=[rstdB])
                        S.op("dve", lambda e: e.reciprocal(out=rstd[:, :N], in_=rstd[:, :N]), writes=[rstdB])
                    stg.append(s_rs)
                    for k in range(8):
                        def s_k(k=k):
                            tmp, tB = tmps[k % 2], tmpsB[k % 2]
                            S.op("dve", lambda e: e.scalar_tensor_tensor(
                                out=tmp[:, :N], in0=xt[:, k, :N], scalar=affn[:, 2 * k + jj:2 * k + jj + 1], in1=rstd[:, :N],
                                op0=ALU.mult, op1=ALU.mult), reads=[xB, rstdB, modvB], writes=[tB])
                            S.op("act", lambda e: e.activation(
                                out=hf[:, k, :N], in_=tmp[:, :N], func=AF.Identity,
                                bias=modv[:, 48 + 2 * k + jj:48 + 2 * k + jj + 1], scale=1.0),
                                 reads=[tB, modvB], writes=[hfB])
                        stg.append(s_k)
                    return stg

                load6(0)
                for f_ in norm_stages(0):
                    f_()
                sc = 0
                for idx, (jj, t0, N) in enumerate(tl6):
                    p = idx % 2
                    xt, xB, hf, hfB = xt2[p], xB2[p], hf2[p], hfB2[p]
                    stages = []
                    if idx + 1 < len(tl6):
                        load6(idx + 1)
                        stages = norm_stages(idx + 1)
                    hr = [hf[:, k, :N] for k in range(8)]
                    for m in range(22):
                        pt, pb = P.next()
                        S.group(pb, pt[:, :N], [(w1[:, k, m * 128:(m + 1) * 128], hr[k]) for k in range(8)], reads=[hfB, w1B])
                        s1, s1B = s1s[sc % 2], s1sB[sc % 2]
                        sc += 1
                        S.op("act", lambda e, pt=pt, s1=s1: e.activation(out=s1[:, :N], in_=pt[:, :N], func=AF.Silu),
                             reads=[pb], writes=[s1B])
                        pt3, pb3 = P.next()
                        S.group(pb3, pt3[:, :N], [(w3[:, k, m * 128:(m + 1) * 128], hr[k]) for k in range(8)], reads=[hfB, w3B])
                        S.op("dve", lambda e, pt3=pt3, m=m, s1=s1: e.tensor_tensor(out=gg[:, m, :N], in0=pt3[:, :N], in1=s1[:, :N],
                                                                                  op=ALU.mult), reads=[pb3, s1B], writes=[ggB])
                        if stages and m >= 11:
                            stages.pop(0)()
                    while stages:
                        stages.pop(0)()
                    for mo in range(8):
                        pt, pb = P.next()
                        S.group(pb, pt[:, :N], [(w2[:, k, mo * 128:(mo + 1) * 128], gg[:, k, :N]) for k in range(22)],
                                reads=[ggB, w2B])
                        S.op("dve", lambda e, pt=pt, mo=mo: e.scalar_tensor_tensor(
                            out=xt[:, mo, :N], in0=pt[:, :N], scalar=modv[:, 80 + 2 * mo + jj:80 + 2 * mo + jj + 1],
                            in1=xt[:, mo, :N], op0=ALU.mult, op1=ALU.add), reads=[pb, modvB], writes=[xB])
                    if not last:
                        S.dma("pool", X2s[:, t0:t0 + N].rearrange("(k p) t -> p k t", p=128), xt[:, :, :N], reads=[xB])
                    else:
                        S.op("act", lambda e: e.activation(out=xsq[:, :, :N], in_=xt[:, :, :N], func=AF.Square),
                             reads=[xB], writes=[xsqB])
                        ptn, pbn = P.next()
                        S.group(pbn, ptn[:, :N], [(onesb[:], xsq[:, k, :N]) for k in range(8)], reads=[constB, xsqB])
                        S.op("act", lambda e: e.activation(out=rstd[:, :N], in_=ptn[:, :N], func=AF.Sqrt, bias=epsc[:],
                                                           scale=1.0 / D), reads=[pbn, constB], writes=[rstdB])
                        S.op("dve", lambda e: e.reciprocal(out=rstd[:, :N], in_=rstd[:, :N]), writes=[rstdB])
                        for k in range(8):
                            S.op("dve", lambda e, k=k: e.scalar_tensor_tensor(
                                out=xt[:, k, :N], in0=xt[:, k, :N], scalar=vcol("gfin", k), in1=rstd[:, :N],
                                op0=ALU.mult, op1=ALU.mult), reads=[rstdB, vecsB], writes=[xB])
                        S.dma("pool", outT[:, t0 - LC:t0 - LC + N].rearrange("(k p) t -> p k t", p=128), xt[:, :, :N],
                              reads=[xB])
                S.barrier()
            if stop_after == (l, 6):
                break
        S.barrier()
    return nc


def _fm(v):
    return np.ascontiguousarray(np.asarray(v, np.float32).reshape(-1, 128).T)


def _const_tables():
    f32 = np.float32
    rows = LL // 64
    row = np.repeat(np.arange(rows, dtype=f32), 64)
    col = np.tile(np.arange(64, dtype=f32), rows)
    inv = np.power(f32(10000.0), -np.arange(8, dtype=f32) / f32(8)).astype(f32)
    ang = np.concatenate([row[:, None] * inv, col[:, None] * inv], axis=-1).astype(f32)
    cos = np.cos(ang).astype(f32); sin = np.sin(ang).astype(f32)
    C = np.ones((32, T), f32); Sg = np.zeros((32, T), f32)
    C[0:16, LC:] = cos.T; C[16:32, LC:] = cos.T
    Sg[0:16, LC:] = -sin.T; Sg[16:32, LC:] = sin.T
    ropeA = np.stack([np.tile(C, (4, 1)), np.tile(Sg, (4, 1))]).astype(f32)
    theta = (1.0 / np.power(f32(10000.0), np.linspace(0.0, 1.0, 32, dtype=f32))).astype(f32)
    pos = np.arange(T, dtype=f32)
    ang = (pos[:, None] * theta).astype(f32)
    cos = np.cos(ang).astype(f32); sin = np.sin(ang).astype(f32)
    C = np.concatenate([cos.T, cos.T], 0); Sg = np.concatenate([-sin.T, sin.T], 0)
    retA = np.stack([np.tile(C, (2, 1)), np.tile(Sg, (2, 1))]).astype(f32)
    retAk = (retA * f32(0.125)).astype(f32)
    hh = np.arange(4, dtype=f32)
    lg = [np.log1p(-np.exp2(-5.0 - hh)).astype(f32), np.log1p(-np.exp2(-5.5 - hh)).astype(f32)]
    p = np.arange(128, dtype=f32)
    rdec = np.zeros((128, 32), f32)
    qdec = np.zeros((4, 2, 64, 128), f32)
    maskT = np.zeros((4, 128, 512), f32)
    for h in range(4):
        gf, gr = lg[0][h], lg[1][h]
        rdec[:, h * 8 + 0] = np.exp(gf * (127.0 - p))
        rdec[:, h * 8 + 1] = np.exp(gr * p)
        rdec[:, h * 8 + 2] = np.exp(gf * 128.0)
        rdec[:, h * 8 + 3] = np.exp(gr * 128.0)
        qdec[h, 0] = np.exp(gf * (p + 1.0))[None, :]
        qdec[h, 1] = np.exp(gr * (128.0 - p))[None, :]
        jj, ii = np.meshgrid(p, p, indexing="ij")
        m = np.where(ii > jj, np.exp(gf * np.maximum(ii - jj, 0)), np.where(ii < jj, np.exp(gr * np.maximum(jj - ii, 0)), 2.0))
        maskT[h] = np.tile(m.astype(f32), (1, 4))
    cmat = np.concatenate([np.ones((128, 128), f32), np.eye(128, dtype=f32)], 1)
    return dict(ropeA=ropeA, retA=retA, retAk=retAk, rdec=rdec.astype(f32), qdec=qdec.astype(f32),
                maskT=maskT.astype(f32), cmat=cmat)


def _pack_vecs(inp, b):
    v = np.zeros((128, NV), np.float32)
    cc = np.stack([_fm(inp["c"][b]), _fm(inp["c_ctx"])], -1).reshape(128, 16)
    v[:, VC["c"]:VC["c"] + 16] = cc
    for l in range(DEPTH):
        v[:, VC[f"gmix{l}"]:VC[f"gmix{l}"] + 8] = _fm(inp["g_mix"][l])
        v[:, VC[f"gffn{l}"]:VC[f"gffn{l}"] + 8] = _fm(inp["g_ffn"][l])
        v[:, VC[f"bmod{l}"]:VC[f"bmod{l}"] + 48] = _fm(inp["b_mod"][l])
        for tap in range(4):
            v[:, VC[f"convw{l}"] + tap * 4:VC[f"convw{l}"] + tap * 4 + 4] = _fm(inp["conv_w"][l, tap])
        v[:, VC[f"convb{l}"]:VC[f"convb{l}"] + 4] = _fm(inp["conv_b"][l])
        for d in range(2):
            v[:, VC[f"ba{l}"] + d * 4:VC[f"ba{l}"] + d * 4 + 4] = _fm(inp["lru_ba"][l, d])
            v[:, VC[f"bx{l}"] + d * 4:VC[f"bx{l}"] + d * 4 + 4] = _fm(inp["lru_bx"][l, d])
            v[:, VC[f"lam{l}"] + d * 4:VC[f"lam{l}"] + d * 4 + 4] = _fm(inp["lru_lam"][l, d])
        v[:, VC[f"gq{l}"]:VC[f"gq{l}"] + 3] = _fm(inp["g_q"][l])
        v[:, VC[f"gkv{l}"]:VC[f"gkv{l}"] + 2] = _fm(inp["g_kv"][l])
    v[:, VC["gfin"]:VC["gfin"] + 8] = _fm(inp["g_final"])
    return v


WKEYS = ["w_mod", "w_in", "lru_wa", "lru_wx", "w_uq", "w_ukv", "w_oa", "w_ob", "w_oc", "w_out", "w_ff1", "w_ff3", "w_ff2"]


def make_in_maps(inp, cores=range(8)):
    inp = {k: np.asarray(v) for k, v in inp.items()}
    consts = _const_tables()
    shared = {k: np.ascontiguousarray(inp[k], dtype=np.float32) for k in WKEYS}
    shared.update(consts)
    maps = []
    for b in cores:
        m = dict(shared)
        m["xin"] = np.ascontiguousarray(np.concatenate([inp["ctx"][b].T, inp["x"][b].T], axis=1), dtype=np.float32)
        m["vecs"] = _pack_vecs(inp, b)
        maps.append(m)
    return maps


_NC_CACHE = {}


def kernel(**inputs):
    if "nc" not in _NC_CACHE:
        _NC_CACHE["nc"] = build_program()
    nc = _NC_CACHE["nc"]
    in_maps = make_in_maps(inputs)
    res = run_bass_kernel_spmd(nc, in_maps, core_ids=list(range(8)))
    out = np.stack([np.ascontiguousarray(r["outT"].T) for r in res.results], axis=0)
    return out.astype(np.float32)
```
